# Optimizing a Trainium2 kernel written in Bass

```python
import jax, jax.numpy as jnp
from jax import lax
import numpy as np

D_MODEL = 1024
BATCH = 16
SEQ = 2048
DEPTH = 1

MIX_WIDTH = D_MODEL
ATTN_WIDTH = MIX_WIDTH // 2
CONV_WIDTH = MIX_WIDTH - ATTN_WIDTH
HEAD_DIM = 64
N_ATTN_HEADS = ATTN_WIDTH // HEAD_DIM
N_CONV_GROUPS = CONV_WIDTH // HEAD_DIM
CONV_KERNEL = 31
D_FF = 4 * D_MODEL
Q_BLOCK = 128
EPS = 1e-6
IN_COLS = 3 * ATTN_WIDTH + 2 * CONV_WIDTH + N_ATTN_HEADS
SPLITS = (ATTN_WIDTH, 2 * ATTN_WIDTH, 3 * ATTN_WIDTH,
          3 * ATTN_WIDTH + CONV_WIDTH, 3 * ATTN_WIDTH + 2 * CONV_WIDTH)

kernel_name = "hymba_fox_conformer_conv_hybrid"


def rmsnorm(x, g):
    xf = x.astype(jnp.float32)
    y = xf * lax.rsqrt(jnp.mean(xf * xf, axis=-1, keepdims=True) + EPS)
    return (y * g.astype(jnp.float32)).astype(x.dtype)


def layernorm(x, g, b):
    xf = x.astype(jnp.float32)
    mu = jnp.mean(xf, axis=-1, keepdims=True)
    xc = xf - mu
    y = xc * lax.rsqrt(jnp.mean(xc * xc, axis=-1, keepdims=True) + EPS)
    return (y * g.astype(jnp.float32) + b.astype(jnp.float32)).astype(x.dtype)


def headwise_rmsnorm(y, g, n_groups):
    B, S, _ = y.shape
    yh = y.reshape(B, S, n_groups, HEAD_DIM)
    out = rmsnorm(yh, g.reshape(n_groups, HEAD_DIM))
    return out.reshape(B, S, n_groups * HEAD_DIM)


def fox_attention(q, k, v, log_f):
    B, S, H, Dh = q.shape
    scale = Dh ** -0.5
    qh = jnp.transpose(q, (0, 2, 1, 3))
    kh = jnp.transpose(k, (0, 2, 1, 3))
    vh = jnp.transpose(v, (0, 2, 1, 3))
    c = jnp.transpose(jnp.cumsum(log_f.astype(jnp.float32), axis=1), (0, 2, 1))
    outs = []
    for blk in range(S // Q_BLOCK):
        q0 = blk * Q_BLOCK
        q1 = q0 + Q_BLOCK
        qb = qh[:, :, q0:q1]
        kb = kh[:, :, :q1]
        vb = vh[:, :, :q1]
        s = jnp.einsum('bhqd,bhkd->bhqk', qb, kb, preferred_element_type=jnp.float32) * scale
        s = s + c[:, :, q0:q1, None] - c[:, :, None, :q1]
        qpos = q0 + jnp.arange(Q_BLOCK)
        kpos = jnp.arange(q1)
        s = jnp.where(kpos[None, :] <= qpos[:, None], s, -jnp.inf)
        p = jax.nn.softmax(s, axis=-1)
        outs.append(jnp.einsum('bhqk,bhkd->bhqd', p.astype(vb.dtype), vb))
    o = jnp.concatenate(outs, axis=2)
    return jnp.transpose(o, (0, 2, 1, 3))


def causal_depthwise_conv(u, w, b):
    K, C = w.shape
    y = lax.conv_general_dilated(
        u, w[:, None, :].astype(u.dtype), window_strides=(1,), padding=[(K - 1, 0)],
        dimension_numbers=('NWC', 'WIO', 'NWC'), feature_group_count=C)
    return y + b.astype(u.dtype)


def setup_inputs(seed: int = 0) -> dict:
    key = jax.random.key(seed)
    ks = jax.random.split(key, 20)
    f32 = jnp.float32
    L = DEPTH
    nrm = lambda k, shape, s: jax.random.normal(k, shape, f32) * s
    x = jax.random.normal(ks[0], (BATCH, SEQ, D_MODEL), f32)
    norm_mix_g = 1.0 + nrm(ks[1], (L, D_MODEL), 0.02)
    w_in = nrm(ks[2], (L, D_MODEL, IN_COLS), D_MODEL ** -0.5)
    b_forget = jnp.linspace(1.0, 4.0, N_ATTN_HEADS, dtype=f32)[None, :] + nrm(ks[3], (L, N_ATTN_HEADS), 0.1)
    conv_dw_w = nrm(ks[4], (L, CONV_KERNEL, CONV_WIDTH), CONV_KERNEL ** -0.5)
    conv_dw_b = nrm(ks[5], (L, CONV_WIDTH), 0.02)
    conv_ln_g = 1.0 + nrm(ks[6], (L, CONV_WIDTH), 0.02)
    conv_ln_b = nrm(ks[7], (L, CONV_WIDTH), 0.02)
    w_conv_pw = nrm(ks[8], (L, CONV_WIDTH, CONV_WIDTH), CONV_WIDTH ** -0.5)
    b_conv_pw = nrm(ks[9], (L, CONV_WIDTH), 0.02)
    attn_out_g = 1.0 + nrm(ks[10], (L, ATTN_WIDTH), 0.02)
    conv_out_g = 1.0 + nrm(ks[11], (L, CONV_WIDTH), 0.02)
    w_out = nrm(ks[12], (L, MIX_WIDTH, D_MODEL), MIX_WIDTH ** -0.5)
    norm_ffn_g = 1.0 + nrm(ks[13], (L, D_MODEL), 0.02)
    w_ffn_up = nrm(ks[14], (L, D_MODEL, D_FF), D_MODEL ** -0.5)
    w_ffn_down = nrm(ks[15], (L, D_FF, D_MODEL), D_FF ** -0.5)
    norm_final_g = 1.0 + nrm(ks[16], (D_MODEL,), 0.02)
    return {"x": x, "norm_mix_g": norm_mix_g, "w_in": w_in, "b_forget": b_forget,
            "conv_dw_w": conv_dw_w, "conv_dw_b": conv_dw_b, "conv_ln_g": conv_ln_g,
            "conv_ln_b": conv_ln_b, "w_conv_pw": w_conv_pw, "b_conv_pw": b_conv_pw,
            "attn_out_g": attn_out_g, "conv_out_g": conv_out_g, "w_out": w_out,
            "norm_ffn_g": norm_ffn_g, "w_ffn_up": w_ffn_up, "w_ffn_down": w_ffn_down,
            "norm_final_g": norm_final_g}


def reference(x, norm_mix_g, w_in, b_forget, conv_dw_w, conv_dw_b, conv_ln_g, conv_ln_b,
              w_conv_pw, b_conv_pw, attn_out_g, conv_out_g, w_out, norm_ffn_g,
              w_ffn_up, w_ffn_down, norm_final_g):
    B, S, _ = x.shape
    for l in range(DEPTH):
        h = rmsnorm(x, norm_mix_g[l])
        proj = h @ w_in[l]
        q, k, v, ga, gb, f_logit = jnp.split(proj, SPLITS, axis=-1)
        log_f = jax.nn.log_sigmoid(f_logit.astype(jnp.float32) + b_forget[l].astype(jnp.float32))
        hs = (B, S, N_ATTN_HEADS, HEAD_DIM)
        attn = fox_attention(q.reshape(hs), k.reshape(hs), v.reshape(hs), log_f)
        attn = headwise_rmsnorm(attn.reshape(B, S, ATTN_WIDTH), attn_out_g[l], N_ATTN_HEADS)
        u = ga * jax.nn.sigmoid(gb)
        u = causal_depthwise_conv(u, conv_dw_w[l], conv_dw_b[l])
        u = jax.nn.silu(layernorm(u, conv_ln_g[l], conv_ln_b[l]))
        u = u @ w_conv_pw[l] + b_conv_pw[l]
        conv = headwise_rmsnorm(u, conv_out_g[l], N_CONV_GROUPS)
        x = x + jnp.concatenate([attn, conv], axis=-1) @ w_out[l]
        h = rmsnorm(x, norm_ffn_g[l])
        x = x + jnp.square(jax.nn.relu(h @ w_ffn_up[l])) @ w_ffn_down[l]
    return rmsnorm(x, norm_final_g)
```

```python
from contextlib import ExitStack
import numpy as np
import concourse.bass as bass
import concourse.mybir as mybir
from concourse.bass_utils import run_bass_kernel_spmd

F32 = mybir.dt.float32
BF16 = mybir.dt.bfloat16
U8 = mybir.dt.uint8
AF = mybir.ActivationFunctionType
ALU = mybir.AluOpType

EPS = 1e-6
NEG = -30000.0
NCORES = 8
SEQ = 2048
DM = 1024
ARENA = 206848
ENGS = ("pe", "act", "dve", "pool", "sp")


class Buf:
    __slots__ = ("name", "w", "r")

    def __init__(self, name):
        self.name = name
        self.w = {}
        self.r = {}


class Op:
    __slots__ = ("eng", "meth", "kw", "deps", "sig", "idx", "chan", "src", "order")


class Sched:
    def __init__(self):
        self.ops = []
        self.last = {}
        self.pending = {e: [] for e in ENGS}
        self.chan_count = {}

    def _add(self, eng, meth, kw, reads, writes, chan=None):
        op = Op()
        op.eng, op.meth, op.kw, op.chan = eng, meth, kw, chan
        op.src = ("c:" + chan) if chan else eng
        op.order = len(self.ops)
        op.sig = False
        op.idx = 0
        raw, oth = {}, {}

        def add(d, o):
            k = o.src
            if k not in d or d[k].order < o.order:
                d[k] = o

        for b in reads:
            for o in b.w.values():
                add(raw, o)
        for b in writes:
            for o in b.w.values():
                add(oth, o)
            for o in b.r.values():
                add(oth, o)
        for o in self.pending[eng]:
            add(oth, o)
        self.pending[eng] = []
        deps = {}
        for k, o in raw.items():
            if k == eng and eng == "pe":
                continue
            deps[k] = o
        for k, o in oth.items():
            if k == eng:
                continue
            if k not in deps or deps[k].order < o.order:
                deps[k] = o
        op.deps = list(deps.values())
        for o in op.deps:
            o.sig = True
        for b in reads:
            b.r[op.src] = op
        for b in writes:
            b.w[op.src] = op
        self.last[op.src] = op
        if chan:
            self.chan_count[chan] = self.chan_count.get(chan, 0) + 1
            op.idx = 16 * self.chan_count[chan]
        self.ops.append(op)
        return op

    def op(self, eng, meth, kw, reads=(), writes=()):
        return self._add(eng, meth, kw, reads, writes)

    def dma(self, queue, out, in_, reads=(), writes=(), chan=None):
        return self._add(queue, "dma_start", dict(out=out, in_=in_), reads, writes, chan=chan)

    def barrier(self):
        srcs = list(self.last.values())
        for e in ENGS:
            self.pending[e] = list(srcs)

    def finish(self):
        self.barrier()
        self._add("sp", None, None, (), ())

    def emit(self, nc):
        cnt = {e: 0 for e in ENGS}
        for o in self.ops:
            if o.chan is None and o.sig:
                cnt[o.eng] += 1
                o.idx = cnt[o.eng]
        streams = {e: [o for o in self.ops if o.eng == e] for e in ENGS}
        names = list(ENGS) + ["c:" + c for c in self.chan_count]
        with ExitStack() as st:
            sems = {n: st.enter_context(nc.semaphore("s_" + n.replace(":", "_"))) for n in names}
            block = st.enter_context(nc.Block())

            def run(e, h):
                waited = {}
                for o in streams[e]:
                    for d in o.deps:
                        if waited.get(d.src, 0) < d.idx:
                            h.wait_ge(sems[d.src], d.idx)
                            waited[d.src] = d.idx
                    if o.meth is None:
                        continue
                    ins = getattr(h, o.meth)(**o.kw)
                    if o.chan is not None:
                        ins.then_inc(sems[o.src], 16)
                    elif o.sig:
                        ins.then_inc(sems[o.src], 1)

            @block.sync
            def _(h):
                run("sp", h)

            @block.gpsimd
            def _(h):
                run("pool", h)

            @block.scalar
            def _(h):
                run("act", h)

            @block.vector
            def _(h):
                run("dve", h)

            @block.tensor
            def _(h):
                run("pe", h)
        return {e: len(streams[e]) for e in ENGS}


class Rot:
    def __init__(self, items):
        self.items = list(items)
        self.i = 0

    def next(self):
        it = self.items[self.i % len(self.items)]
        self.i += 1
        return it


def build_program():
    nc = bass.Bass("TRN2", target_bir_lowering=False)
    S = Sched()

    def din(name, shape):
        return nc.dram_tensor(name, list(shape), F32, kind="ExternalInput").ap()

    x = din("x", [2, SEQ, DM])
    w_in = din("w_in", [DM, 2568])
    w_pw = din("w_pw", [512, 512])
    w_out = din("w_out", [DM, DM])
    w_up = din("w_up", [DM, 4096])
    w_dn = din("w_dn", [4096, DM])
    gmix_d = din("gmix", [128, DM])
    gffn_d = din("gffn", [128, DM])
    gfin_d = din("gfin", [128, DM])
    pv_d = din("pv", [128, 24])
    cw_d = din("cw", [128, 124])
    bfg_d = din("bfg", [8, 1])
    out = nc.dram_tensor("out", [2, SEQ, DM], F32, kind="ExternalOutput").ap()
    x1s = nc.dram_tensor("x1s", [2 * SEQ, DM], F32).ap()

    arena = nc.alloc_sbuf_tensor("arena", [128, ARENA], U8)
    pb = [nc.alloc_psum_tensor("pb%d" % i, [128, 512], F32)[:] for i in range(8)]
    Bpb = [Buf("pb%d" % i) for i in range(8)]
    pbT = pb[7].bitcast(BF16)

    class Carver:
        def __init__(self, base):
            self.off = base

        def take(self, dtype, *shape):
            n = 1
            for s_ in shape:
                n *= s_
            nb = n * (4 if dtype == F32 else 2)
            ap = arena[:, self.off:self.off + nb].bitcast(dtype)
            self.off += (nb + 31) // 32 * 32
            assert self.off <= ARENA, self.off
            if len(shape) == 2:
                ap = ap.rearrange("p (a b) -> p a b", a=shape[0])
            elif len(shape) == 3:
                ap = ap.rearrange("p (a b c) -> p a b c", a=shape[0], b=shape[1])
            return ap

    C = Carver(0)
    ident = C.take(BF16, 128)
    maskT = C.take(BF16, 128)
    zt = C.take(BF16, 128)
    L1 = C.take(BF16, 128)
    L2 = C.take(BF16, 128)
    Lg = C.take(BF16, 128)
    o512 = C.take(BF16, 128)
    ones3 = C.take(BF16, 512)
    mh = C.take(F32, 512)
    gmix = C.take(F32, DM)
    pv = C.take(F32, 24)
    hgb = C.take(F32, 8)
    bfg = C.take(F32, 1)
    negb = C.take(F32, 1)
    ssq = C.take(F32, 8)
    rs = C.take(F32, 8)
    attnT = C.take(BF16, 4, SEQ)
    PBASE = C.off

    S.dma("sp", gmix, gmix_d, chan="cst")
    S.dma("sp", pv, pv_d, chan="cst")
    S.dma("sp", bfg[0:8, :], bfg_d, chan="cst")
    S.op("pool", "memset", dict(ap=ones3, constant=1.0))
    S.op("pool", "memset", dict(ap=zt, constant=0.0))
    S.op("pool", "memset", dict(ap=mh, constant=-0.5))
    S.op("pool", "affine_select", dict(out=ident, in_=ones3[:, 0:128], pattern=[[-1, 128]],
                                       compare_op=ALU.is_equal, fill=0.0, base=0, channel_multiplier=1))
    S.op("pool", "affine_select", dict(out=maskT, in_=zt, pattern=[[1, 128]],
                                       compare_op=ALU.is_ge, fill=NEG, base=0, channel_multiplier=-1))
    for t_, blocks in ((L1, ((0, 64, 0, 64, 1.0 / 64), (64, 128, 0, 64, EPS / 64), (0, 128, 64, 128, 0.0))),
                       (L2, ((0, 64, 64, 128, EPS / 64), (64, 128, 64, 128, 1.0 / 64), (0, 128, 0, 64, 0.0))),
                       (Lg, ((0, 64, 0, 64, 1.0 / 64), (64, 128, 64, 128, 1.0 / 64), (0, 64, 64, 128, 0.0),
                             (64, 128, 0, 64, 0.0))),
                       (o512, ((0, 128, 0, 128, 1.0 / 512),))):
        for (p0, p1, f0, f1, v) in blocks:
            S.op("dve", "memset", dict(ap=t_[p0:p1, f0:f1], constant=v))
    S.barrier()
    S.op("dve", "tensor_scalar", dict(out=hgb, in0=pv[:, 4:12], scalar1=0.5, scalar2=None, op0=ALU.mult))
    S.op("dve", "tensor_scalar", dict(out=negb[0:8, :], in0=bfg[0:8, :], scalar1=-1.0, scalar2=None, op0=ALU.mult))
    S.barrier()

    pool7 = Rot([(pb[i], Bpb[i]) for i in range(7)])

    def norm_transpose(src_rows, xt_ap, Bxt, chan, hb, Bhb, hT, BhT, tcol, gvec, si):
        S.dma("sp", xt_ap, src_rows, writes=[Bxt], chan=chan)
        Bs = Bssq[si]
        S.op("act", "activation", dict(out=junk, in_=xt_ap, func=AF.Square, scale=1.0 / 32.0,
                                       accum_out=ssq[:, si:si + 1]), reads=[Bxt], writes=[Bs])
        S.op("pool", "tensor_scalar", dict(out=rs[:, si:si + 1], in0=ssq[:, si:si + 1], scalar1=EPS, scalar2=None,
                                           op0=ALU.add), reads=[Bs], writes=[Brs[si]])
        S.op("pool", "tensor_tensor", dict(out=rs[:, si:si + 1], in0=rs[:, si:si + 1], in1=mh[:, 0:1], op=ALU.pow),
             reads=[Brs[si]], writes=[Brs[si]])
        S.op("dve", "scalar_tensor_tensor", dict(out=hb, in0=xt_ap, scalar=rs[:, si:si + 1], in1=gvec,
                                                 op0=ALU.mult, op1=ALU.mult), reads=[Bxt, Brs[si]], writes=[Bhb])
        for kc in range(8):
            S.op("pe", "transpose", dict(out=pbT[:, kc * 128:(kc + 1) * 128], in_=hb[:, kc * 128:(kc + 1) * 128],
                                         identity=ident), reads=[Bhb], writes=[Bpb[7]])
        S.op("act", "copy", dict(out=hT[:, :, tcol:tcol + 128], in_=pbT.rearrange("p (k t) -> p k t", k=8)),
             reads=[Bpb[7]], writes=[BhT])

    Bssq = [Buf("ssq%d" % i) for i in range(8)]
    Brs = [Buf("rs%d" % i) for i in range(8)]
    BattnT = Buf("attnT")

    def mm_group(ps, Bps, pairs, reads):
        n = len(pairs)
        for i, (l, r) in enumerate(pairs):
            S.op("pe", "matmul", dict(out=ps, lhsT=l, rhs=r, start=(i == 0), stop=(i == n - 1)),
                 reads=reads, writes=[Bps])

    for s in range(2):
        P = Carver(PBASE)
        win_a = P.take(BF16, 8, 1544)
        KA = P.take(BF16, 2, 4, SEQ)
        Vt = P.take(BF16, 16, 768)
        QA = [P.take(BF16, 2, 4, 512) for _ in range(2)]
        hT = P.take(BF16, 8, 512)
        xt = [P.take(F32, DM) for _ in range(2)]
        hb = P.take(BF16, DM)
        junk = P.take(BF16, DM)
        ef = P.take(F32, 512)
        spf = P.take(F32, 512)
        cn = [P.take(F32, 512) for _ in range(2)]
        r1 = P.take(F32, 512)
        r2 = P.take(F32, 512)
        kp = P.take(BF16, 3, 512)
        qp = P.take(BF16, 3, 512)
        pts = [P.take(BF16, 512) for _ in range(4)]
        sqA = P.take(BF16, 512)
        sqB = P.take(BF16, 512)
        stsb = P.take(F32, 512)
        rsat = P.take(F32, 512)
        Bwa, BKA, BKaug, BVt, BhT, Bhb = Buf("wa"), Buf("KA"), Buf("Kaug"), Buf("Vt"), Buf("hT"), Buf("hb")
        BQA = [Buf("QA0"), Buf("QA1")]
        BQaug = [Buf("Qaug0"), Buf("Qaug1")]
        Bxt = [Buf("xt0"), Buf("xt1")]
        Bef, Bspf, Br1, Br2, Bkp, Bqp = Buf("ef"), Buf("spf"), Buf("r1"), Buf("r2"), Buf("kp"), Buf("qp")
        Bcn = [Buf("cn0"), Buf("cn1")]
        Bpts = [Buf("pt%d" % i) for i in range(4)]
        BsqA, BsqB, Bstsb, Brsat = Buf("sqA"), Buf("sqB"), Buf("stsb"), Buf("rsat")
        Binit = Buf("init")

        for kc in range(8):
            S.dma("pool", win_a[:, kc, 0:1536], w_in[kc * 128:(kc + 1) * 128, 0:1536], writes=[Bwa], chan="wa")
            S.dma("pool", win_a[:, kc, 1536:1544], w_in[kc * 128:(kc + 1) * 128, 2560:2568], writes=[Bwa], chan="wa")
        S.op("pool", "memset", dict(ap=KA[64:128, 0], constant=0.0), writes=[Binit])
        S.op("pool", "memset", dict(ap=KA[0:64, 1], constant=0.0), writes=[Binit])
        for q_ in QA:
            S.op("dve", "memset", dict(ap=q_[64:128, 0], constant=0.0), writes=[Binit])
            S.op("dve", "memset", dict(ap=q_[0:64, 1], constant=0.0), writes=[Binit])
        for c in range(4):
            S.op("dve", "memset", dict(ap=Vt[:, :, c * 192 + 64:c * 192 + 128], constant=1.0), writes=[Binit])
        for par, a0 in ((0, 64), (1, 0)):
            for c in range(4):
                for blk in range(4):
                    S.dma("sp", KA[a0:a0 + 3, par, c, blk * 512:(blk + 1) * 512], ones3[0:3, :],
                          reads=[Binit], writes=[Binit], chan="ini")
                for q_ in QA:
                    S.dma("sp", q_[a0 + 3:a0 + 6, par, c, :], ones3[0:3, :], reads=[Binit], writes=[Binit], chan="ini")
        S.barrier()

        pool3 = Rot([(pb[i], Bpb[i]) for i in range(3)])
        accsets = Rot([((pb[3], Bpb[3]), (pb[4], Bpb[4])), ((pb[5], Bpb[5]), (pb[6], Bpb[6]))])
        ptrot = Rot(list(zip(pts, Bpts)))

        for b in range(4):
            qs = b % 2
            cols = slice(b * 512, (b + 1) * 512)
            for t in range(4):
                tt = b * 4 + t
                sl = tt % 2
                norm_transpose(x[s, tt * 128:(tt + 1) * 128, :], xt[sl], Bxt[sl], "xt%d" % sl, hb, Bhb, hT, BhT,
                               t * 128, gmix, sl)
            for c in range(4):
                ps, Bps = pool7.next()
                mm_group(ps, Bps, [(win_a[:, kc, c * 128:(c + 1) * 128], hT[:, kc, :]) for kc in range(8)], [Bwa, BhT])
                S.op("act", "activation", dict(out=QA[qs][0:64, 0, c, :], in_=ps[0:64, :], func=AF.Identity,
                                               scale=0.125), reads=[Bps], writes=[BQA[qs]])
                S.op("act", "activation", dict(out=QA[qs][64:128, 1, c, :], in_=ps[64:128, :], func=AF.Identity,
                                               scale=0.125), reads=[Bps], writes=[BQA[qs]])
            for c in range(4):
                ps, Bps = pool7.next()
                mm_group(ps, Bps, [(win_a[:, kc, 512 + c * 128:512 + (c + 1) * 128], hT[:, kc, :]) for kc in range(8)],
                         [Bwa, BhT])
                S.op("dve", "tensor_copy", dict(out=KA[0:64, 0, c, cols], in_=ps[0:64, :]), reads=[Bps], writes=[BKA])
                S.op("dve", "tensor_copy", dict(out=KA[64:128, 1, c, cols], in_=ps[64:128, :]), reads=[Bps],
                     writes=[BKA])
            for t in range(4):
                tt = b * 4 + t
                ps, Bps = pool7.next()
                mm_group(ps, Bps, [(hT[:, kc, t * 128:(t + 1) * 128], win_a[:, kc, 1024:1536]) for kc in range(8)],
                         [Bwa, BhT])
                vdst = Vt[:, tt, :].rearrange("p (c x) -> p c x", x=192)
                vsrc = ps[:].rearrange("p (c e d) -> p c e d", e=2, d=64)
                S.op("dve", "tensor_copy", dict(out=vdst[:, :, 0:64], in_=vsrc[:, :, 0, :]), reads=[Bps], writes=[BVt])
                S.op("act", "copy", dict(out=vdst[:, :, 128:192], in_=vsrc[:, :, 1, :]), reads=[Bps], writes=[BVt])
            ps, Bps = pool7.next()
            mm_group(ps[0:8, :], Bps, [(win_a[:, kc, 1536:1544], hT[:, kc, :]) for kc in range(8)], [Bwa, BhT])
            S.op("act", "activation", dict(out=ef[0:8, :], in_=ps[0:8, :], func=AF.Exp, scale=-1.0, bias=negb[0:8, :]),
                 reads=[Bps], writes=[Bef])
            S.op("act", "activation", dict(out=spf[0:8, :], in_=ef[0:8, :], func=AF.Ln, bias=1.0), reads=[Bef],
                 writes=[Bspf])
            cur, prv = b % 2, (b + 1) % 2
            init = 0.0 if b == 0 else cn[prv][0:8, 511:512]
            S.op("dve", "tensor_tensor_scan", dict(out=cn[cur][0:8, :], data0=spf[0:8, :], data1=spf[0:8, :],
                                                   initial=init, op0=ALU.add, op1=ALU.bypass),
                 reads=[Bspf, Bcn[prv]], writes=[Bcn[cur]])
            S.op("dve", "tensor_copy", dict(out=kp[0:8, 0, :], in_=cn[cur][0:8, :]), reads=[Bcn[cur]], writes=[Bkp])
            S.op("dve", "tensor_tensor", dict(out=r1[0:8, :], in0=cn[cur][0:8, :], in1=kp[0:8, 0, :], op=ALU.subtract),
                 reads=[Bcn[cur], Bkp], writes=[Br1])
            S.op("dve", "tensor_copy", dict(out=kp[0:8, 1, :], in_=r1[0:8, :]), reads=[Br1], writes=[Bkp])
            S.op("dve", "tensor_tensor", dict(out=r2[0:8, :], in0=r1[0:8, :], in1=kp[0:8, 1, :], op=ALU.subtract),
                 reads=[Br1, Bkp], writes=[Br2])
            S.op("dve", "tensor_copy", dict(out=kp[0:8, 2, :], in_=r2[0:8, :]), reads=[Br2], writes=[Bkp])
            S.op("dve", "tensor_scalar", dict(out=qp[0:8], in0=kp[0:8], scalar1=-1.0, scalar2=None, op0=ALU.mult),
                 reads=[Bkp], writes=[Bqp])
            for h in range(8):
                par, c = h % 2, h // 2
                a0 = 64 if par == 0 else 0
                S.dma("sp", KA[a0 + 3:a0 + 6, par, c, cols], kp[h:h + 1], reads=[Bkp], writes=[BKaug], chan="cq")
                S.dma("sp", QA[qs][a0:a0 + 3, par, c, :], qp[h:h + 1], reads=[Bqp], writes=[BQaug[qs]], chan="cq")
            nch = 4 * b + 4
            for c in range(4):
                (accA, BaccA), (accB, BaccB) = accsets.next()
                units = [(par, j) for j in range(nch) for par in (0, 1)]
                info = {}

                def emit_qk(u):
                    par, j = u
                    dg = j >= 4 * b
                    c0 = (j - 4 * b) * 128 if dg else 0
                    ps, Bps = pool3.next()
                    S.op("pe", "matmul", dict(out=ps[:, c0:512], lhsT=KA[:, par, c, j * 128:(j + 1) * 128],
                                              rhs=QA[qs][:, par, c, c0:512], start=True, stop=not dg),
                         reads=[BKA, BKaug, BQA[qs], BQaug[qs]], writes=[Bps])
                    if dg:
                        S.op("pe", "matmul", dict(out=ps[:, c0:c0 + 128], lhsT=ident, rhs=maskT, start=False,
                                                  stop=True), writes=[Bps])
                    pt, Bpt = ptrot.next()
                    S.op("act", "activation", dict(out=pt[:, c0:512], in_=ps[:, c0:512], func=AF.Exp), reads=[Bps],
                         writes=[Bpt])
                    info[u] = (pt, Bpt, c0)

                def emit_pv(u):
                    par, j = u
                    pt, Bpt, c0 = info[u]
                    acc, Bacc = (accA, BaccA) if par == 0 else (accB, BaccB)
                    v0 = c * 192 + (0 if par == 0 else 64)
                    S.op("pe", "matmul", dict(out=acc[:, c0:512], lhsT=Vt[:, j, v0:v0 + 128], rhs=pt[:, c0:512],
                                              start=(j == 0), stop=(j == nch - 1)), reads=[BVt, Bpt], writes=[Bacc])

                D = 2
                for i in range(len(units) + D):
                    if i < len(units):
                        emit_qk(units[i])
                    if i >= D:
                        emit_pv(units[i - D])
                S.op("act", "activation", dict(out=sqA, in_=accA[:], func=AF.Square), reads=[BaccA], writes=[BsqA])
                S.op("act", "activation", dict(out=sqB, in_=accB[:], func=AF.Square), reads=[BaccB], writes=[BsqB])
                ps, Bps = pool3.next()
                S.op("pe", "matmul", dict(out=ps[:], lhsT=L1, rhs=sqA, start=True, stop=False), reads=[BsqA],
                     writes=[Bps])
                S.op("pe", "matmul", dict(out=ps[:], lhsT=L2, rhs=sqB, start=False, stop=True), reads=[BsqB],
                     writes=[Bps])
                S.op("dve", "tensor_copy", dict(out=stsb, in_=ps[:]), reads=[Bps], writes=[Bstsb])
                S.op("pool", "tensor_tensor", dict(out=rsat, in0=stsb, in1=mh, op=ALU.pow), reads=[Bstsb],
                     writes=[Brsat])
                S.op("dve", "scalar_tensor_tensor", dict(out=attnT[0:64, c, cols], in0=accA[0:64, :],
                                                         scalar=pv[0:64, 16 + c:17 + c], in1=rsat[0:64, :],
                                                         op0=ALU.mult, op1=ALU.mult),
                     reads=[BaccA, Brsat], writes=[BattnT])
                S.op("dve", "scalar_tensor_tensor", dict(out=attnT[64:128, c, cols], in0=accB[64:128, :],
                                                         scalar=pv[64:128, 16 + c:17 + c], in1=rsat[64:128, :],
                                                         op0=ALU.mult, op1=ALU.mult),
                     reads=[BaccB, Brsat], writes=[BattnT])
        S.barrier()

        P = Carver(PBASE)
        win_b = P.take(BF16, 8, 1024)
        wpw = P.take(BF16, 4, 512)
        diag = P.take(BF16, 4, 31, 128)
        wout = P.take(BF16, 8, DM)
        uT = P.take(BF16, 4, SEQ + 32)
        hT = P.take(BF16, 8, 512)
        xt4 = [P.take(F32, DM) for _ in range(4)]
        hb = P.take(BF16, DM)
        junk = P.take(BF16, DM)
        tg = [P.take(F32, 512) for _ in range(2)]
        ycv = P.take(F32, 4, 512)
        ybf = P.take(BF16, 4, 512)
        ysq = P.take(BF16, 4, 512)
        musb = P.take(F32, 512)
        nm2 = P.take(F32, 512)
        vare = P.take(F32, 512)
        rsln = P.take(F32, 512)
        t2 = [P.take(F32, 512) for _ in range(2)]
        s2 = P.take(BF16, 4, 512)
        vpw = P.take(F32, 512)
        sqc = P.take(BF16, 512)
        stc = P.take(F32, 512)
        rsc = P.take(F32, 512)
        convT = P.take(BF16, 4, 512)
        x1t = [P.take(F32, DM) for _ in range(2)]
        cw = P.take(F32, 124)
        Bwb, Bwpw, Bdiag, Bwout, BuT, BhT, Bhb = (Buf("wb"), Buf("wpw"), Buf("diag"), Buf("wout"), Buf("uT"),
                                                  Buf("hT"), Buf("hb"))
        Bxt4 = [Buf("xt4_%d" % i) for i in range(4)]
        Btg = [Buf("tg0"), Buf("tg1")]
        Bycv = [Buf("ycv%d" % i) for i in range(4)]
        Bybf, Bysq, Bmusb, Bnm2, Bvare, Brsln = Buf("ybf"), Buf("ysq"), Buf("musb"), Buf("nm2"), Buf("vare"), Buf("rsln")
        Bt2 = [Buf("t2_0"), Buf("t2_1")]
        Bs2, Bvpw, Bsqc, Bstc, Brsc, BconvT, Bcw = (Buf("s2"), Buf("vpw"), Buf("sqc"), Buf("stc"), Buf("rsc"),
                                                    Buf("convT"), Buf("cw"))
        Bx1t = [Buf("x1t0"), Buf("x1t1")]

        S.dma("sp", cw, cw_d, writes=[Bcw], chan="cst")
        for kc in range(8):
            S.dma("pool", win_b[:, kc, :], w_in[kc * 128:(kc + 1) * 128, 1536:2560], writes=[Bwb], chan="wb")
        for kc in range(4):
            S.dma("pool", wpw[:, kc, :], w_pw[kc * 128:(kc + 1) * 128, :], writes=[Bwpw], chan="wb")
        for kc in range(8):
            S.dma("pool", wout[:, kc, :], w_out[kc * 128:(kc + 1) * 128, :], writes=[Bwout], chan="wb")
        for cc in range(4):
            S.op("dve", "memset", dict(ap=uT[:, cc, 0:32], constant=0.0), writes=[BuT])
            for k in range(31):
                S.op("dve", "tensor_scalar", dict(out=diag[:, cc, k, :], in0=ident,
                                                  scalar1=cw[:, cc * 31 + k:cc * 31 + k + 1], scalar2=None,
                                                  op0=ALU.mult), reads=[Bcw], writes=[Bdiag])
        strot = Rot(list(zip(x1t, Bx1t, ("st0", "st1"))))
        tgrot = Rot(list(zip(tg, Btg)))
        t2rot = Rot(list(zip(t2, Bt2)))
        U0 = 2

        for b in range(4):
            cols = slice(b * 512, (b + 1) * 512)
            for t in range(4):
                tt = b * 4 + t
                norm_transpose(x[s, tt * 128:(tt + 1) * 128, :], xt4[t], Bxt4[t], "xq%d" % t, hb, Bhb, hT, BhT,
                               t * 128, gmix, 2 + t)
            for cc in range(4):
                psa, Bpsa = pool7.next()
                mm_group(psa, Bpsa, [(win_b[:, kc, cc * 128:(cc + 1) * 128], hT[:, kc, :]) for kc in range(8)],
                         [Bwb, BhT])
                psb, Bpsb = pool7.next()
                mm_group(psb, Bpsb, [(win_b[:, kc, 512 + cc * 128:512 + (cc + 1) * 128], hT[:, kc, :])
                                     for kc in range(8)], [Bwb, BhT])
                tg_, Btg_ = tgrot.next()
                S.op("act", "activation", dict(out=tg_, in_=psb[:], func=AF.Tanh, scale=0.5), reads=[Bpsb],
                     writes=[Btg_])
                S.op("dve", "scalar_tensor_tensor", dict(out=uT[:, cc, 32 + b * 512:32 + (b + 1) * 512], in0=tg_,
                                                         scalar=1.0, in1=psa[:], op0=ALU.add, op1=ALU.mult),
                     reads=[Btg_, Bpsa], writes=[BuT])
            for cc in range(4):
                ps, Bps = pool7.next()
                mm_group(ps, Bps, [(diag[:, cc, k, :], uT[:, cc, b * 512 + k + U0:b * 512 + k + U0 + 512])
                                   for k in range(31)], [Bdiag, BuT])
                S.op("act", "activation", dict(out=ycv[:, cc, :], in_=ps[:], func=AF.Identity, scale=0.5,
                                               bias=pv[:, 0 + cc:1 + cc]), reads=[Bps], writes=[Bycv[cc]])
                S.op("dve", "tensor_copy", dict(out=ybf[:, cc, :], in_=ycv[:, cc, :]), reads=[Bycv[cc]], writes=[Bybf])
                S.op("act", "activation", dict(out=ysq[:, cc, :], in_=ycv[:, cc, :], func=AF.Square),
                     reads=[Bycv[cc]], writes=[Bysq])
            psm, Bpsm = pool7.next()
            mm_group(psm, Bpsm, [(o512, ybf[:, cc, :]) for cc in range(4)], [Bybf])
            pse, Bpse = pool7.next()
            mm_group(pse, Bpse, [(o512, ysq[:, cc, :]) for cc in range(4)], [Bysq])
            S.op("dve", "tensor_copy", dict(out=musb, in_=psm[:]), reads=[Bpsm], writes=[Bmusb])
            S.op("dve", "scalar_tensor_tensor", dict(out=nm2, in0=musb, scalar=-1.0, in1=musb, op0=ALU.mult,
                                                     op1=ALU.mult), reads=[Bmusb], writes=[Bnm2])
            S.op("dve", "scalar_tensor_tensor", dict(out=vare, in0=pse[:], scalar=EPS, in1=nm2, op0=ALU.add,
                                                     op1=ALU.add), reads=[Bpse, Bnm2], writes=[Bvare])
            S.op("pool", "tensor_tensor", dict(out=rsln, in0=vare, in1=mh, op=ALU.pow), reads=[Bvare], writes=[Brsln])
            for cc in range(4):
                yc = ycv[:, cc, :]
                S.op("dve", "tensor_tensor", dict(out=yc, in0=yc, in1=musb, op=ALU.subtract),
                     reads=[Bycv[cc], Bmusb], writes=[Bycv[cc]])
                S.op("dve", "tensor_tensor", dict(out=yc, in0=yc, in1=rsln, op=ALU.mult),
                     reads=[Bycv[cc], Brsln], writes=[Bycv[cc]])
                t2_, Bt2_ = t2rot.next()
                S.op("act", "activation", dict(out=t2_, in_=yc, func=AF.Tanh, scale=hgb[:, cc:cc + 1],
                                               bias=hgb[:, 4 + cc:5 + cc]), reads=[Bycv[cc]], writes=[Bt2_])
                S.op("dve", "tensor_scalar", dict(out=yc, in0=yc, scalar1=pv[:, 4 + cc:5 + cc],
                                                  scalar2=pv[:, 8 + cc:9 + cc], op0=ALU.mult, op1=ALU.add),
                     reads=[Bycv[cc]], writes=[Bycv[cc]])
                S.op("dve", "scalar_tensor_tensor", dict(out=s2[:, cc, :], in0=t2_, scalar=1.0, in1=yc, op0=ALU.add,
                                                         op1=ALU.mult), reads=[Bt2_, Bycv[cc]], writes=[Bs2])
            for mo in range(4):
                ps, Bps = pool7.next()
                mm_group(ps, Bps, [(wpw[:, kc, mo * 128:(mo + 1) * 128], s2[:, kc, :]) for kc in range(4)],
                         [Bwpw, Bs2])
                S.op("act", "activation", dict(out=vpw, in_=ps[:], func=AF.Identity, scale=0.5,
                                               bias=pv[:, 12 + mo:13 + mo]), reads=[Bps], writes=[Bvpw])
                S.op("act", "activation", dict(out=sqc, in_=ps[:], func=AF.Square, scale=0.5,
                                               bias=pv[:, 12 + mo:13 + mo]), reads=[Bps], writes=[Bsqc])
                pss, Bpss = pool7.next()
                S.op("pe", "matmul", dict(out=pss[:], lhsT=Lg, rhs=sqc, start=True, stop=True), reads=[Bsqc],
                     writes=[Bpss])
                S.op("dve", "tensor_scalar", dict(out=stc, in0=pss[:], scalar1=EPS, scalar2=None, op0=ALU.add),
                     reads=[Bpss], writes=[Bstc])
                S.op("pool", "tensor_tensor", dict(out=rsc, in0=stc, in1=mh, op=ALU.pow), reads=[Bstc], writes=[Brsc])
                S.op("dve", "scalar_tensor_tensor", dict(out=convT[:, mo, :], in0=vpw, scalar=pv[:, 20 + mo:21 + mo],
                                                         in1=rsc, op0=ALU.mult, op1=ALU.mult),
                     reads=[Bvpw, Brsc], writes=[BconvT])
            for t in range(4):
                tt = b * 4 + t
                x1_, Bx1_, ch = strot.next()
                for nh in range(2):
                    ps, Bps = pool7.next()
                    pairs = []
                    for kc in range(8):
                        l = attnT[:, kc, b * 512 + t * 128:b * 512 + (t + 1) * 128] if kc < 4 else \
                            convT[:, kc - 4, t * 128:(t + 1) * 128]
                        pairs.append((l, wout[:, kc, nh * 512:(nh + 1) * 512]))
                    mm_group(ps, Bps, pairs, [Bwout, BattnT, BconvT])
                    S.op("dve", "tensor_tensor", dict(out=x1_[:, nh * 512:(nh + 1) * 512], in0=ps[:],
                                                      in1=xt4[t][:, nh * 512:(nh + 1) * 512], op=ALU.add),
                         reads=[Bps, Bxt4[t]], writes=[Bx1_])
                S.dma("sp", x1s[s * SEQ + tt * 128:s * SEQ + (tt + 1) * 128, :], x1_, reads=[Bx1_], chan=ch)
        S.barrier()

    P = Carver(PBASE - SEQ * 8)
    wup = P.take(BF16, 8, 4096)
    wdn = P.take(BF16, 32, DM)
    gffn = P.take(F32, DM)
    gfin = P.take(F32, DM)
    xb = [P.take(F32, DM) for _ in range(4)]
    hb = P.take(BF16, DM)
    junk = P.take(BF16, DM)
    h2T = P.take(BF16, 8, 256)
    rl = [P.take(F32, 256) for _ in range(2)]
    hid = P.take(BF16, 32, 256)
    ob = [P.take(F32, DM) for _ in range(2)]
    Bwup, Bwdn, Bg = Buf("wup"), Buf("wdn"), Buf("g")
    Bxb = [Buf("xb%d" % i) for i in range(4)]
    Bhb, Bh2T, Bhid = Buf("hb"), Buf("h2T"), Buf("hid")
    Brl = [Buf("rl0"), Buf("rl1")]
    Bob = [Buf("ob0"), Buf("ob1")]
    S.dma("sp", gffn, gffn_d, writes=[Bg], chan="cst")
    S.dma("sp", gfin, gfin_d, writes=[Bg], chan="cst")
    for kc in range(8):
        for hf in range(2):
            S.dma("pool", wup[:, kc, hf * 2048:(hf + 1) * 2048], w_up[kc * 128:(kc + 1) * 128, hf * 2048:(hf + 1) * 2048],
                  writes=[Bwup], chan="wu")
    for kc in range(32):
        S.dma("pool", wdn[:, kc, :], w_dn[kc * 128:(kc + 1) * 128, :], writes=[Bwdn], chan="wd")
    rlrot = Rot(list(zip(rl, Brl)))
    obrot = Rot(list(zip(ob, Bob, ("so0", "so1"))))
    outf = out.rearrange("s t d -> (s t) d")
    for j in range(16):
        for t in range(2):
            tt = j * 2 + t
            k4 = tt % 4
            norm_transpose(x1s[tt * 128:(tt + 1) * 128, :], xb[k4], Bxb[k4], "xb%d" % k4, hb, Bhb, h2T, Bh2T, t * 128,
                           gffn, k4)
        for m in range(32):
            ps, Bps = pool7.next()
            mm_group(ps[:, 0:256], Bps, [(wup[:, kc, m * 128:(m + 1) * 128], h2T[:, kc, :]) for kc in range(8)],
                     [Bwup, Bh2T, Bg])
            rl_, Brl_ = rlrot.next()
            S.op("act", "activation", dict(out=rl_, in_=ps[:, 0:256], func=AF.Relu), reads=[Bps], writes=[Brl_])
            S.op("dve", "tensor_tensor", dict(out=hid[:, m, :], in0=rl_, in1=rl_, op=ALU.mult), reads=[Brl_],
                 writes=[Bhid])
        for t in range(2):
            tt = j * 2 + t
            k4 = tt % 4
            for nh in range(2):
                ps, Bps = pool7.next()
                mm_group(ps, Bps, [(hid[:, kc, t * 128:(t + 1) * 128], wdn[:, kc, nh * 512:(nh + 1) * 512])
                                   for kc in range(32)], [Bwdn, Bhid])
                S.op("dve", "tensor_tensor", dict(out=xb[k4][:, nh * 512:(nh + 1) * 512], in0=ps[:],
                                                  in1=xb[k4][:, nh * 512:(nh + 1) * 512], op=ALU.add),
                     reads=[Bps, Bxb[k4]], writes=[Bxb[k4]])
            si = 4 + k4
            S.op("act", "activation", dict(out=junk, in_=xb[k4], func=AF.Square, scale=1.0 / 32.0,
                                           accum_out=ssq[:, si:si + 1]), reads=[Bxb[k4]], writes=[Bssq[si]])
            S.op("pool", "tensor_scalar", dict(out=rs[:, si:si + 1], in0=ssq[:, si:si + 1], scalar1=EPS, scalar2=None,
                                               op0=ALU.add), reads=[Bssq[si]], writes=[Brs[si]])
            S.op("pool", "tensor_tensor", dict(out=rs[:, si:si + 1], in0=rs[:, si:si + 1], in1=mh[:, 0:1], op=ALU.pow),
                 reads=[Brs[si]], writes=[Brs[si]])
            ob_, Bob_, ch = obrot.next()
            S.op("dve", "scalar_tensor_tensor", dict(out=ob_, in0=xb[k4], scalar=rs[:, si:si + 1], in1=gfin,
                                                     op0=ALU.mult, op1=ALU.mult),
                 reads=[Bxb[k4], Brs[si], Bg], writes=[Bob_])
            S.dma("sp", outf[tt * 128:(tt + 1) * 128, :], ob_, reads=[Bob_], chan=ch)
    S.finish()
    counts = S.emit(nc)
    return nc, counts


_CACHE = {}


def kernel(x, norm_mix_g, w_in, b_forget, conv_dw_w, conv_dw_b, conv_ln_g, conv_ln_b, w_conv_pw, b_conv_pw,
           attn_out_g, conv_out_g, w_out, norm_ffn_g, w_ffn_up, w_ffn_down, norm_final_g):
    f32 = np.float32
    x = np.asarray(x, f32)

    def bc(v):
        return np.ascontiguousarray(np.broadcast_to(np.asarray(v, f32).reshape(1, DM), (128, DM)))

    def pvec(v):
        return np.asarray(v, f32).reshape(4, 128).T

    pv = np.ascontiguousarray(np.concatenate(
        [pvec(conv_dw_b), pvec(conv_ln_g), pvec(conv_ln_b), pvec(b_conv_pw), pvec(attn_out_g), pvec(conv_out_g)],
        axis=1))
    cw = np.ascontiguousarray(np.asarray(conv_dw_w, f32).reshape(31, 4, 128).transpose(2, 1, 0).reshape(128, 124))
    shared = {
        "w_in": np.ascontiguousarray(np.asarray(w_in, f32).reshape(DM, 2568)),
        "w_pw": np.ascontiguousarray(np.asarray(w_conv_pw, f32).reshape(512, 512)),
        "w_out": np.ascontiguousarray(np.asarray(w_out, f32).reshape(DM, DM)),
        "w_up": np.ascontiguousarray(np.asarray(w_ffn_up, f32).reshape(DM, 4096)),
        "w_dn": np.ascontiguousarray(np.asarray(w_ffn_down, f32).reshape(4096, DM)),
        "gmix": bc(norm_mix_g), "gffn": bc(norm_ffn_g), "gfin": bc(norm_final_g),
        "pv": pv, "cw": cw,
        "bfg": np.ascontiguousarray(np.asarray(b_forget, f32).reshape(8, 1)),
    }
    if "nc" not in _CACHE:
        _CACHE["nc"] = build_program()[0]
    nc = _CACHE["nc"]
    in_maps = []
    for i in range(NCORES):
        m = dict(shared)
        m["x"] = np.ascontiguousarray(x[2 * i:2 * i + 2])
        in_maps.append(m)
    res = run_bass_kernel_spmd(nc, in_maps, core_ids=list(range(NCORES)))
    return np.concatenate([np.asarray(r["out"], f32) for r in res.results], axis=0)
```

```python
from contextlib import ExitStack
import numpy as np
import concourse.bass as bass
import concourse.mybir as mybir
from concourse.bass_utils import run_bass_kernel_spmd

F32 = mybir.dt.float32
BF16 = mybir.dt.bfloat16
U8 = mybir.dt.uint8
AF = mybir.ActivationFunctionType
ALU = mybir.AluOpType

EPS = 1e-6
NEG = -30000.0
NCORES = 8
SEQ = 2048
DM = 1024
ARENA = 206848
ENGS = ("pe", "act", "dve", "pool", "sp")


class Buf:
    __slots__ = ("name", "w", "r")

    def __init__(self, name):
        self.name = name
        self.w = {}
        self.r = {}


class Op:
    __slots__ = ("eng", "meth", "kw", "deps", "sig", "idx", "chan", "src", "order")


class Sched:
    def __init__(self):
        self.ops = []
        self.last = {}
        self.pending = {e: [] for e in ENGS}
        self.chan_count = {}

    def _add(self, eng, meth, kw, reads, writes, chan=None):
        op = Op()
        op.eng, op.meth, op.kw, op.chan = eng, meth, kw, chan
        op.src = ("c:" + chan) if chan else eng
        op.order = len(self.ops)
        op.sig = False
        op.idx = 0
        raw, oth = {}, {}

        def add(d, o):
            k = o.src
            if k not in d or d[k].order < o.order:
                d[k] = o

        for b in reads:
            for o in b.w.values():
                add(raw, o)
        for b in writes:
            for o in b.w.values():
                add(oth, o)
            for o in b.r.values():
                add(oth, o)
        for o in self.pending[eng]:
            add(oth, o)
        self.pending[eng] = []
        deps = {}
        for k, o in raw.items():
            if k == eng and eng == "pe":
                continue
            deps[k] = o
        for k, o in oth.items():
            if k == eng:
                continue
            if k not in deps or deps[k].order < o.order:
                deps[k] = o
        op.deps = list(deps.values())
        for o in op.deps:
            o.sig = True
        for b in reads:
            b.r[op.src] = op
        for b in writes:
            b.w[op.src] = op
        self.last[op.src] = op
        if chan:
            self.chan_count[chan] = self.chan_count.get(chan, 0) + 1
            op.idx = 16 * self.chan_count[chan]
        self.ops.append(op)
        return op

    def op(self, eng, meth, kw, reads=(), writes=()):
        return self._add(eng, meth, kw, reads, writes)

    def dma(self, queue, out, in_, reads=(), writes=(), chan=None):
        return self._add(queue, "dma_start", dict(out=out, in_=in_), reads, writes, chan=chan)

    def barrier(self):
        srcs = list(self.last.values())
        for e in ENGS:
            self.pending[e] = list(srcs)

    def finish(self):
        self.barrier()
        self._add("sp", None, None, (), ())

    def emit(self, nc):
        cnt = {e: 0 for e in ENGS}
        for o in self.ops:
            if o.chan is None and o.sig:
                cnt[o.eng] += 1
                o.idx = cnt[o.eng]
        streams = {e: [o for o in self.ops if o.eng == e] for e in ENGS}
        names = list(ENGS) + ["c:" + c for c in self.chan_count]
        with ExitStack() as st:
            sems = {n: st.enter_context(nc.semaphore("s_" + n.replace(":", "_"))) for n in names}
            block = st.enter_context(nc.Block())

            def run(e, h):
                waited = {}
                for o in streams[e]:
                    for d in o.deps:
                        if waited.get(d.src, 0) < d.idx:
                            h.wait_ge(sems[d.src], d.idx)
                            waited[d.src] = d.idx
                    if o.meth is None:
                        continue
                    ins = getattr(h, o.meth)(**o.kw)
                    if o.chan is not None:
                        ins.then_inc(sems[o.src], 16)
                    elif o.sig:
                        ins.then_inc(sems[o.src], 1)

            @block.sync
            def _(h):
                run("sp", h)

            @block.gpsimd
            def _(h):
                run("pool", h)

            @block.scalar
            def _(h):
                run("act", h)

            @block.vector
            def _(h):
                run("dve", h)

            @block.tensor
            def _(h):
                run("pe", h)
        return {e: len(streams[e]) for e in ENGS}


class Rot:
    def __init__(self, items):
        self.items = list(items)
        self.i = 0

    def next(self):
        it = self.items[self.i % len(self.items)]
        self.i += 1
        return it


def build_program():
    nc = bass.Bass("TRN2", target_bir_lowering=False)
    S = Sched()

    def din(name, shape):
        return nc.dram_tensor(name, list(shape), F32, kind="ExternalInput").ap()

    x = din("x", [2, SEQ, DM])
    w_in = din("w_in", [DM, 2568])
    w_pw = din("w_pw", [512, 512])
    w_out = din("w_out", [DM, DM])
    w_up = din("w_up", [DM, 4096])
    w_dn = din("w_dn", [4096, DM])
    gmix_d = din("gmix", [128, DM])
    gffn_d = din("gffn", [128, DM])
    gfin_d = din("gfin", [128, DM])
    pv_d = din("pv", [128, 24])
    cw_d = din("cw", [128, 124])
    bfg_d = din("bfg", [8, 1])
    out = nc.dram_tensor("out", [2, SEQ, DM], F32, kind="ExternalOutput").ap()
    x1s = nc.dram_tensor("x1s", [2 * SEQ, DM], F32).ap()

    arena = nc.alloc_sbuf_tensor("arena", [128, ARENA], U8)
    pb = [nc.alloc_psum_tensor("pb%d" % i, [128, 512], F32)[:] for i in range(8)]
    Bpb = [Buf("pb%d" % i) for i in range(8)]
    pbT = pb[7].bitcast(BF16)

    class Carver:
        def __init__(self, base):
            self.off = base

        def take(self, dtype, *shape):
            n = 1
            for s_ in shape:
                n *= s_
            nb = n * (4 if dtype == F32 else 2)
            ap = arena[:, self.off:self.off + nb].bitcast(dtype)
            self.off += (nb + 31) // 32 * 32
            assert self.off <= ARENA, self.off
            if len(shape) == 2:
                ap = ap.rearrange("p (a b) -> p a b", a=shape[0])
            elif len(shape) == 3:
                ap = ap.rearrange("p (a b c) -> p a b c", a=shape[0], b=shape[1])
            return ap

    C = Carver(0)
    ident = C.take(BF16, 128)
    maskT = C.take(BF16, 128)
    zt = C.take(BF16, 128)
    L1 = C.take(BF16, 128)
    L2 = C.take(BF16, 128)
    Lg = C.take(BF16, 128)
    o512 = C.take(BF16, 128)
    ones3 = C.take(BF16, 512)
    mh = C.take(F32, 512)
    gmix = C.take(F32, DM)
    pv = C.take(F32, 24)
    hgb = C.take(F32, 8)
    bfg = C.take(F32, 1)
    negb = C.take(F32, 1)
    ssq = C.take(F32, 8)
    epsb = C.take(F32, 1)
    rs = C.take(F32, 8)
    attnT = C.take(BF16, 4, SEQ)
    PBASE = C.off

    S.dma("sp", gmix, gmix_d, chan="cst")
    S.dma("sp", pv, pv_d, chan="cst")
    S.dma("sp", bfg[0:8, :], bfg_d, chan="cst")
    S.op("pool", "memset", dict(ap=ones3, constant=1.0))
    S.op("pool", "memset", dict(ap=zt, constant=0.0))
    S.op("pool", "memset", dict(ap=mh, constant=-0.5))
    S.op("pool", "memset", dict(ap=epsb, constant=EPS))
    S.op("pool", "affine_select", dict(out=ident, in_=ones3[:, 0:128], pattern=[[-1, 128]],
                                       compare_op=ALU.is_equal, fill=0.0, base=0, channel_multiplier=1))
    S.op("pool", "affine_select", dict(out=maskT, in_=zt, pattern=[[1, 128]],
                                       compare_op=ALU.is_ge, fill=NEG, base=0, channel_multiplier=-1))
    for t_, blocks in ((L1, ((0, 64, 0, 64, 1.0 / 64), (64, 128, 0, 64, EPS / 64), (0, 128, 64, 128, 0.0))),
                       (L2, ((0, 64, 64, 128, EPS / 64), (64, 128, 64, 128, 1.0 / 64), (0, 128, 0, 64, 0.0))),
                       (Lg, ((0, 64, 0, 64, 1.0 / 64), (64, 128, 64, 128, 1.0 / 64), (0, 64, 64, 128, 0.0),
                             (64, 128, 0, 64, 0.0))),
                       (o512, ((0, 128, 0, 128, 1.0 / 512),))):
        for (p0, p1, f0, f1, v) in blocks:
            S.op("dve", "memset", dict(ap=t_[p0:p1, f0:f1], constant=v))
    S.barrier()
    S.op("dve", "tensor_scalar", dict(out=hgb, in0=pv[:, 4:12], scalar1=-1.0, scalar2=None, op0=ALU.mult))
    S.op("dve", "tensor_scalar", dict(out=negb[0:8, :], in0=bfg[0:8, :], scalar1=-1.0, scalar2=None, op0=ALU.mult))
    S.barrier()

    pool7 = Rot([(pb[i], Bpb[i]) for i in range(7)])

    def norm_transpose(src_rows, xt_ap, Bxt, chan, hb, Bhb, hT, BhT, tcol, gvec, si):
        S.dma("sp", xt_ap, src_rows, writes=[Bxt], chan=chan)
        Bs = Bssq[si]
        S.op("act", "activation", dict(out=junk, in_=xt_ap, func=AF.Square, scale=1.0 / 32.0,
                                       accum_out=ssq[:, si:si + 1]), reads=[Bxt], writes=[Bs])
        S.op("pool", "tensor_scalar", dict(out=rs[:, si:si + 1], in0=ssq[:, si:si + 1], scalar1=EPS, scalar2=None,
                                           op0=ALU.add), reads=[Bs], writes=[Brs[si]])
        S.op("pool", "tensor_tensor", dict(out=rs[:, si:si + 1], in0=rs[:, si:si + 1], in1=mh[:, 0:1], op=ALU.pow),
             reads=[Brs[si]], writes=[Brs[si]])
        S.op("dve", "scalar_tensor_tensor", dict(out=hb, in0=xt_ap, scalar=rs[:, si:si + 1], in1=gvec,
                                                 op0=ALU.mult, op1=ALU.mult), reads=[Bxt, Brs[si]], writes=[Bhb])
        for kc in range(8):
            S.op("pe", "transpose", dict(out=pbT[:, kc * 128:(kc + 1) * 128], in_=hb[:, kc * 128:(kc + 1) * 128],
                                         identity=ident), reads=[Bhb], writes=[Bpb[7]])
        S.op("act", "copy", dict(out=hT[:, :, tcol:tcol + 128], in_=pbT.rearrange("p (k t) -> p k t", k=8)),
             reads=[Bpb[7]], writes=[BhT])

    Bssq = [Buf("ssq%d" % i) for i in range(8)]
    Brs = [Buf("rs%d" % i) for i in range(8)]
    BattnT = Buf("attnT")

    def mm_group(ps, Bps, pairs, reads):
        n = len(pairs)
        for i, (l, r) in enumerate(pairs):
            S.op("pe", "matmul", dict(out=ps, lhsT=l, rhs=r, start=(i == 0), stop=(i == n - 1)),
                 reads=reads, writes=[Bps])

    for s in range(2):
        P = Carver(PBASE)
        win_a = P.take(BF16, 8, 1544)
        KA = P.take(BF16, 2, 4, SEQ)
        Vt = P.take(BF16, 16, 768)
        QA = [P.take(BF16, 2, 4, 512) for _ in range(2)]
        hT = P.take(BF16, 8, 512)
        xt = [P.take(F32, DM) for _ in range(2)]
        hb = P.take(BF16, DM)
        junk = P.take(BF16, DM)
        ef = P.take(F32, 512)
        spf = P.take(F32, 512)
        cn = [P.take(F32, 512) for _ in range(2)]
        r1 = P.take(F32, 512)
        r2 = P.take(F32, 512)
        kp = P.take(BF16, 3, 512)
        qp = P.take(BF16, 3, 512)
        pts = [P.take(BF16, 512) for _ in range(4)]
        sqA = P.take(BF16, 512)
        sqB = P.take(BF16, 512)
        stsb = P.take(F32, 512)
        rsat = P.take(F32, 512)
        Bwa, BKA, BKaug, BVt, BhT, Bhb = Buf("wa"), Buf("KA"), Buf("Kaug"), Buf("Vt"), Buf("hT"), Buf("hb")
        BQA = [Buf("QA0"), Buf("QA1")]
        BQaug = [Buf("Qaug0"), Buf("Qaug1")]
        Bxt = [Buf("xt0"), Buf("xt1")]
        Bef, Bspf, Br1, Br2, Bkp, Bqp = Buf("ef"), Buf("spf"), Buf("r1"), Buf("r2"), Buf("kp"), Buf("qp")
        Bcn = [Buf("cn0"), Buf("cn1")]
        Bpts = [Buf("pt%d" % i) for i in range(4)]
        BsqA, BsqB, Bstsb, Brsat = Buf("sqA"), Buf("sqB"), Buf("stsb"), Buf("rsat")
        Binit = Buf("init")

        for kc in range(8):
            S.dma("pool", win_a[:, kc, 0:1536], w_in[kc * 128:(kc + 1) * 128, 0:1536], writes=[Bwa], chan="wa")
            S.dma("pool", win_a[:, kc, 1536:1544], w_in[kc * 128:(kc + 1) * 128, 2560:2568], writes=[Bwa], chan="wa")
        S.op("pool", "memset", dict(ap=KA[64:128, 0], constant=0.0), writes=[Binit])
        S.op("pool", "memset", dict(ap=KA[0:64, 1], constant=0.0), writes=[Binit])
        for q_ in QA:
            S.op("dve", "memset", dict(ap=q_[64:128, 0], constant=0.0), writes=[Binit])
            S.op("dve", "memset", dict(ap=q_[0:64, 1], constant=0.0), writes=[Binit])
        for c in range(4):
            S.op("dve", "memset", dict(ap=Vt[:, :, c * 192 + 64:c * 192 + 128], constant=1.0), writes=[Binit])
        for par, a0 in ((0, 64), (1, 0)):
            for c in range(4):
                for blk in range(4):
                    S.dma("sp", KA[a0:a0 + 3, par, c, blk * 512:(blk + 1) * 512], ones3[0:3, :],
                          reads=[Binit], writes=[Binit], chan="ini")
                for q_ in QA:
                    S.dma("sp", q_[a0 + 3:a0 + 6, par, c, :], ones3[0:3, :], reads=[Binit], writes=[Binit], chan="ini")
        S.barrier()

        pool3 = Rot([(pb[i], Bpb[i]) for i in range(3)])
        accsets = Rot([((pb[3], Bpb[3]), (pb[4], Bpb[4])), ((pb[5], Bpb[5]), (pb[6], Bpb[6]))])
        ptrot = Rot(list(zip(pts, Bpts)))

        for b in range(4):
            qs = b % 2
            cols = slice(b * 512, (b + 1) * 512)
            for t in range(4):
                tt = b * 4 + t
                sl = tt % 2
                norm_transpose(x[s, tt * 128:(tt + 1) * 128, :], xt[sl], Bxt[sl], "xt%d" % sl, hb, Bhb, hT, BhT,
                               t * 128, gmix, sl)
            for c in range(4):
                ps, Bps = pool7.next()
                mm_group(ps, Bps, [(win_a[:, kc, c * 128:(c + 1) * 128], hT[:, kc, :]) for kc in range(8)], [Bwa, BhT])
                S.op("act", "activation", dict(out=QA[qs][0:64, 0, c, :], in_=ps[0:64, :], func=AF.Identity,
                                               scale=0.125), reads=[Bps], writes=[BQA[qs]])
                S.op("act", "activation", dict(out=QA[qs][64:128, 1, c, :], in_=ps[64:128, :], func=AF.Identity,
                                               scale=0.125), reads=[Bps], writes=[BQA[qs]])
            for c in range(4):
                ps, Bps = pool7.next()
                mm_group(ps, Bps, [(win_a[:, kc, 512 + c * 128:512 + (c + 1) * 128], hT[:, kc, :]) for kc in range(8)],
                         [Bwa, BhT])
                S.op("dve", "tensor_copy", dict(out=KA[0:64, 0, c, cols], in_=ps[0:64, :]), reads=[Bps], writes=[BKA])
                S.op("dve", "tensor_copy", dict(out=KA[64:128, 1, c, cols], in_=ps[64:128, :]), reads=[Bps],
                     writes=[BKA])
            for t in range(4):
                tt = b * 4 + t
                ps, Bps = pool7.next()
                mm_group(ps, Bps, [(hT[:, kc, t * 128:(t + 1) * 128], win_a[:, kc, 1024:1536]) for kc in range(8)],
                         [Bwa, BhT])
                vdst = Vt[:, tt, :].rearrange("p (c x) -> p c x", x=192)
                vsrc = ps[:].rearrange("p (c e d) -> p c e d", e=2, d=64)
                S.op("dve", "tensor_copy", dict(out=vdst[:, :, 0:64], in_=vsrc[:, :, 0, :]), reads=[Bps], writes=[BVt])
                S.op("act", "copy", dict(out=vdst[:, :, 128:192], in_=vsrc[:, :, 1, :]), reads=[Bps], writes=[BVt])
            ps, Bps = pool7.next()
            mm_group(ps[0:8, :], Bps, [(win_a[:, kc, 1536:1544], hT[:, kc, :]) for kc in range(8)], [Bwa, BhT])
            S.op("act", "activation", dict(out=ef[0:8, :], in_=ps[0:8, :], func=AF.Exp, scale=-1.0, bias=negb[0:8, :]),
                 reads=[Bps], writes=[Bef])
            S.op("act", "activation", dict(out=spf[0:8, :], in_=ef[0:8, :], func=AF.Ln, bias=1.0), reads=[Bef],
                 writes=[Bspf])
            cur, prv = b % 2, (b + 1) % 2
            init = 0.0 if b == 0 else cn[prv][0:8, 511:512]
            S.op("dve", "tensor_tensor_scan", dict(out=cn[cur][0:8, :], data0=spf[0:8, :], data1=spf[0:8, :],
                                                   initial=init, op0=ALU.add, op1=ALU.bypass),
                 reads=[Bspf, Bcn[prv]], writes=[Bcn[cur]])
            S.op("dve", "tensor_copy", dict(out=kp[0:8, 0, :], in_=cn[cur][0:8, :]), reads=[Bcn[cur]], writes=[Bkp])
            S.op("dve", "tensor_tensor", dict(out=r1[0:8, :], in0=cn[cur][0:8, :], in1=kp[0:8, 0, :], op=ALU.subtract),
                 reads=[Bcn[cur], Bkp], writes=[Br1])
            S.op("dve", "tensor_copy", dict(out=kp[0:8, 1, :], in_=r1[0:8, :]), reads=[Br1], writes=[Bkp])
            S.op("dve", "tensor_tensor", dict(out=r2[0:8, :], in0=r1[0:8, :], in1=kp[0:8, 1, :], op=ALU.subtract),
                 reads=[Br1, Bkp], writes=[Br2])
            S.op("dve", "tensor_copy", dict(out=kp[0:8, 2, :], in_=r2[0:8, :]), reads=[Br2], writes=[Bkp])
            S.op("dve", "tensor_scalar", dict(out=qp[0:8], in0=kp[0:8], scalar1=-1.0, scalar2=None, op0=ALU.mult),
                 reads=[Bkp], writes=[Bqp])
            for h in range(8):
                par, c = h % 2, h // 2
                a0 = 64 if par == 0 else 0
                S.dma("sp", KA[a0 + 3:a0 + 6, par, c, cols], kp[h:h + 1], reads=[Bkp], writes=[BKaug], chan="cq")
                S.dma("sp", QA[qs][a0:a0 + 3, par, c, :], qp[h:h + 1], reads=[Bqp], writes=[BQaug[qs]], chan="cq")
            nch = 4 * b + 4
            for c in range(4):
                (accA, BaccA), (accB, BaccB) = accsets.next()
                units = [(par, j) for j in range(nch) for par in (0, 1)]
                info = {}

                def emit_qk(u):
                    par, j = u
                    dg = j >= 4 * b
                    c0 = (j - 4 * b) * 128 if dg else 0
                    ps, Bps = pool3.next()
                    S.op("pe", "matmul", dict(out=ps[:, c0:512], lhsT=KA[:, par, c, j * 128:(j + 1) * 128],
                                              rhs=QA[qs][:, par, c, c0:512], start=True, stop=not dg),
                         reads=[BKA, BKaug, BQA[qs], BQaug[qs]], writes=[Bps])
                    if dg:
                        S.op("pe", "matmul", dict(out=ps[:, c0:c0 + 128], lhsT=ident, rhs=maskT, start=False,
                                                  stop=True), writes=[Bps])
                    pt, Bpt = ptrot.next()
                    S.op("act", "activation", dict(out=pt[:, c0:512], in_=ps[:, c0:512], func=AF.Exp), reads=[Bps],
                         writes=[Bpt])
                    info[u] = (pt, Bpt, c0)

                def emit_pv(u):
                    par, j = u
                    pt, Bpt, c0 = info[u]
                    acc, Bacc = (accA, BaccA) if par == 0 else (accB, BaccB)
                    v0 = c * 192 + (0 if par == 0 else 64)
                    S.op("pe", "matmul", dict(out=acc[:, c0:512], lhsT=Vt[:, j, v0:v0 + 128], rhs=pt[:, c0:512],
                                              start=(j == 0), stop=(j == nch - 1)), reads=[BVt, Bpt], writes=[Bacc])

                D = 2
                for i in range(len(units) + D):
                    if i < len(units):
                        emit_qk(units[i])
                    if i >= D:
                        emit_pv(units[i - D])
                S.op("act", "activation", dict(out=sqA, in_=accA[:], func=AF.Square), reads=[BaccA], writes=[BsqA])
                S.op("act", "activation", dict(out=sqB, in_=accB[:], func=AF.Square), reads=[BaccB], writes=[BsqB])
                ps, Bps = pool3.next()
                S.op("pe", "matmul", dict(out=ps[:], lhsT=L1, rhs=sqA, start=True, stop=False), reads=[BsqA],
                     writes=[Bps])
                S.op("pe", "matmul", dict(out=ps[:], lhsT=L2, rhs=sqB, start=False, stop=True), reads=[BsqB],
                     writes=[Bps])
                S.op("act", "activation", dict(out=stsb, in_=ps[:], func=AF.Ln), reads=[Bps], writes=[Bstsb])
                S.op("act", "activation", dict(out=rsat, in_=stsb, func=AF.Exp, scale=-0.5), reads=[Bstsb],
                     writes=[Brsat])
                S.op("dve", "scalar_tensor_tensor", dict(out=attnT[0:64, c, cols], in0=accA[0:64, :],
                                                         scalar=pv[0:64, 16 + c:17 + c], in1=rsat[0:64, :],
                                                         op0=ALU.mult, op1=ALU.mult),
                     reads=[BaccA, Brsat], writes=[BattnT])
                S.op("dve", "scalar_tensor_tensor", dict(out=attnT[64:128, c, cols], in0=accB[64:128, :],
                                                         scalar=pv[64:128, 16 + c:17 + c], in1=rsat[64:128, :],
                                                         op0=ALU.mult, op1=ALU.mult),
                     reads=[BaccB, Brsat], writes=[BattnT])
        S.barrier()

        P = Carver(PBASE)
        win_b = P.take(BF16, 8, 1024)
        wpw = P.take(BF16, 4, 512)
        diag = P.take(BF16, 4, 31, 128)
        wout = P.take(BF16, 8, DM)
        uT = P.take(BF16, 4, SEQ + 32)
        hT = P.take(BF16, 8, 512)
        xt4 = [P.take(F32, DM) for _ in range(4)]
        hb = P.take(BF16, DM)
        junk = P.take(BF16, DM)
        tg = [P.take(F32, 512) for _ in range(2)]
        ycv = P.take(F32, 4, 512)
        ybf = P.take(BF16, 4, 512)
        ysq = P.take(BF16, 4, 512)
        musb = P.take(F32, 512)
        nm2 = P.take(F32, 512)
        vare = P.take(F32, 512)
        rsln = P.take(F32, 512)
        t2 = [P.take(F32, 512) for _ in range(2)]
        s2 = P.take(BF16, 4, 512)
        vpw = P.take(F32, 512)
        sqc = P.take(BF16, 512)
        stc = P.take(F32, 512)
        rsc = P.take(F32, 512)
        convT = P.take(BF16, 4, 512)
        x1t = [P.take(F32, DM) for _ in range(2)]
        cw = P.take(F32, 124)
        Bwb, Bwpw, Bdiag, Bwout, BuT, BhT, Bhb = (Buf("wb"), Buf("wpw"), Buf("diag"), Buf("wout"), Buf("uT"),
                                                  Buf("hT"), Buf("hb"))
        Bxt4 = [Buf("xt4_%d" % i) for i in range(4)]
        Btg = [Buf("tg0"), Buf("tg1")]
        Bycv = [Buf("ycv%d" % i) for i in range(4)]
        Bybf, Bysq, Bmusb, Bnm2, Bvare, Brsln = Buf("ybf"), Buf("ysq"), Buf("musb"), Buf("nm2"), Buf("vare"), Buf("rsln")
        Bt2 = [Buf("t2_0"), Buf("t2_1")]
        Bs2, Bvpw, Bsqc, Bstc, Brsc, BconvT, Bcw = (Buf("s2"), Buf("vpw"), Buf("sqc"), Buf("stc"), Buf("rsc"),
                                                    Buf("convT"), Buf("cw"))
        Bx1t = [Buf("x1t0"), Buf("x1t1")]

        S.dma("sp", cw, cw_d, writes=[Bcw], chan="cst")
        for kc in range(8):
            S.dma("pool", win_b[:, kc, :], w_in[kc * 128:(kc + 1) * 128, 1536:2560], writes=[Bwb], chan="wb")
        for kc in range(4):
            S.dma("pool", wpw[:, kc, :], w_pw[kc * 128:(kc + 1) * 128, :], writes=[Bwpw], chan="wb")
        for kc in range(8):
            S.dma("pool", wout[:, kc, :], w_out[kc * 128:(kc + 1) * 128, :], writes=[Bwout], chan="wb")
        for cc in range(4):
            S.op("dve", "memset", dict(ap=uT[:, cc, 0:32], constant=0.0), writes=[BuT])
            for k in range(31):
                S.op("dve", "tensor_scalar", dict(out=diag[:, cc, k, :], in0=ident,
                                                  scalar1=cw[:, cc * 31 + k:cc * 31 + k + 1], scalar2=None,
                                                  op0=ALU.mult), reads=[Bcw], writes=[Bdiag])
        strot = Rot(list(zip(x1t, Bx1t, ("st0", "st1"))))
        tgrot = Rot(list(zip(tg, Btg)))
        t2rot = Rot(list(zip(t2, Bt2)))
        U0 = 2

        for b in range(4):
            cols = slice(b * 512, (b + 1) * 512)
            for t in range(4):
                tt = b * 4 + t
                norm_transpose(x[s, tt * 128:(tt + 1) * 128, :], xt4[t], Bxt4[t], "xq%d" % t, hb, Bhb, hT, BhT,
                               t * 128, gmix, 2 + t)
            for cc in range(4):
                psa, Bpsa = pool7.next()
                mm_group(psa, Bpsa, [(win_b[:, kc, cc * 128:(cc + 1) * 128], hT[:, kc, :]) for kc in range(8)],
                         [Bwb, BhT])
                psb, Bpsb = pool7.next()
                mm_group(psb, Bpsb, [(win_b[:, kc, 512 + cc * 128:512 + (cc + 1) * 128], hT[:, kc, :])
                                     for kc in range(8)], [Bwb, BhT])
                tg_, Btg_ = tgrot.next()
                S.op("act", "activation", dict(out=tg_, in_=psb[:], func=AF.Exp, scale=-1.0), reads=[Bpsb],
                     writes=[Btg_])
                S.op("act", "activation", dict(out=tg_, in_=tg_, func=AF.Ln, bias=1.0), reads=[Btg_], writes=[Btg_])
                S.op("act", "activation", dict(out=tg_, in_=tg_, func=AF.Exp, scale=-1.0), reads=[Btg_], writes=[Btg_])
                S.op("dve", "tensor_tensor", dict(out=uT[:, cc, 32 + b * 512:32 + (b + 1) * 512], in0=psa[:],
                                                  in1=tg_, op=ALU.mult), reads=[Btg_, Bpsa], writes=[BuT])
            for cc in range(4):
                ps, Bps = pool7.next()
                mm_group(ps, Bps, [(diag[:, cc, k, :], uT[:, cc, b * 512 + k + U0:b * 512 + k + U0 + 512])
                                   for k in range(31)], [Bdiag, BuT])
                S.op("act", "activation", dict(out=ycv[:, cc, :], in_=ps[:], func=AF.Identity, scale=1.0,
                                               bias=pv[:, 0 + cc:1 + cc]), reads=[Bps], writes=[Bycv[cc]])
                S.op("dve", "tensor_copy", dict(out=ybf[:, cc, :], in_=ycv[:, cc, :]), reads=[Bycv[cc]], writes=[Bybf])
                S.op("act", "activation", dict(out=ysq[:, cc, :], in_=ycv[:, cc, :], func=AF.Square),
                     reads=[Bycv[cc]], writes=[Bysq])
            psm, Bpsm = pool7.next()
            mm_group(psm, Bpsm, [(o512, ybf[:, cc, :]) for cc in range(4)], [Bybf])
            pse, Bpse = pool7.next()
            mm_group(pse, Bpse, [(o512, ysq[:, cc, :]) for cc in range(4)], [Bysq])
            S.op("dve", "tensor_copy", dict(out=musb, in_=psm[:]), reads=[Bpsm], writes=[Bmusb])
            S.op("dve", "scalar_tensor_tensor", dict(out=nm2, in0=musb, scalar=-1.0, in1=musb, op0=ALU.mult,
                                                     op1=ALU.mult), reads=[Bmusb], writes=[Bnm2])
            S.op("dve", "scalar_tensor_tensor", dict(out=vare, in0=pse[:], scalar=EPS, in1=nm2, op0=ALU.add,
                                                     op1=ALU.add), reads=[Bpse, Bnm2], writes=[Bvare])
            S.op("act", "activation", dict(out=vare, in_=vare, func=AF.Ln), reads=[Bvare], writes=[Bvare])
            S.op("act", "activation", dict(out=rsln, in_=vare, func=AF.Exp, scale=-0.5), reads=[Bvare], writes=[Brsln])
            for cc in range(4):
                yc = ycv[:, cc, :]
                S.op("dve", "tensor_tensor", dict(out=yc, in0=yc, in1=musb, op=ALU.subtract),
                     reads=[Bycv[cc], Bmusb], writes=[Bycv[cc]])
                S.op("dve", "tensor_tensor", dict(out=yc, in0=yc, in1=rsln, op=ALU.mult),
                     reads=[Bycv[cc], Brsln], writes=[Bycv[cc]])
                t2_, Bt2_ = t2rot.next()
                S.op("act", "activation", dict(out=t2_, in_=yc, func=AF.Exp, scale=hgb[:, cc:cc + 1],
                                               bias=hgb[:, 4 + cc:5 + cc]), reads=[Bycv[cc]], writes=[Bt2_])
                S.op("act", "activation", dict(out=t2_, in_=t2_, func=AF.Ln, bias=1.0), reads=[Bt2_], writes=[Bt2_])
                S.op("act", "activation", dict(out=t2_, in_=t2_, func=AF.Exp, scale=-1.0), reads=[Bt2_], writes=[Bt2_])
                S.op("dve", "tensor_scalar", dict(out=yc, in0=yc, scalar1=pv[:, 4 + cc:5 + cc],
                                                  scalar2=pv[:, 8 + cc:9 + cc], op0=ALU.mult, op1=ALU.add),
                     reads=[Bycv[cc]], writes=[Bycv[cc]])
                S.op("dve", "tensor_tensor", dict(out=s2[:, cc, :], in0=t2_, in1=yc, op=ALU.mult),
                     reads=[Bt2_, Bycv[cc]], writes=[Bs2])
            for mo in range(4):
                ps, Bps = pool7.next()
                mm_group(ps, Bps, [(wpw[:, kc, mo * 128:(mo + 1) * 128], s2[:, kc, :]) for kc in range(4)],
                         [Bwpw, Bs2])
                S.op("act", "activation", dict(out=vpw, in_=ps[:], func=AF.Identity, scale=1.0,
                                               bias=pv[:, 12 + mo:13 + mo]), reads=[Bps], writes=[Bvpw])
                S.op("act", "activation", dict(out=sqc, in_=ps[:], func=AF.Square, scale=1.0,
                                               bias=pv[:, 12 + mo:13 + mo]), reads=[Bps], writes=[Bsqc])
                pss, Bpss = pool7.next()
                S.op("pe", "matmul", dict(out=pss[:], lhsT=Lg, rhs=sqc, start=True, stop=True), reads=[Bsqc],
                     writes=[Bpss])
                S.op("act", "activation", dict(out=stc, in_=pss[:], func=AF.Ln, bias=epsb[:, 0:1]), reads=[Bpss],
                     writes=[Bstc])
                S.op("act", "activation", dict(out=rsc, in_=stc, func=AF.Exp, scale=-0.5), reads=[Bstc], writes=[Brsc])
                S.op("dve", "scalar_tensor_tensor", dict(out=convT[:, mo, :], in0=vpw, scalar=pv[:, 20 + mo:21 + mo],
                                                         in1=rsc, op0=ALU.mult, op1=ALU.mult),
                     reads=[Bvpw, Brsc], writes=[BconvT])
            for t in range(4):
                tt = b * 4 + t
                x1_, Bx1_, ch = strot.next()
                for nh in range(2):
                    ps, Bps = pool7.next()
                    pairs = []
                    for kc in range(8):
                        l = attnT[:, kc, b * 512 + t * 128:b * 512 + (t + 1) * 128] if kc < 4 else \
                            convT[:, kc - 4, t * 128:(t + 1) * 128]
                        pairs.append((l, wout[:, kc, nh * 512:(nh + 1) * 512]))
                    mm_group(ps, Bps, pairs, [Bwout, BattnT, BconvT])
                    S.op("dve", "tensor_tensor", dict(out=x1_[:, nh * 512:(nh + 1) * 512], in0=ps[:],
                                                      in1=xt4[t][:, nh * 512:(nh + 1) * 512], op=ALU.add),
                         reads=[Bps, Bxt4[t]], writes=[Bx1_])
                S.dma("sp", x1s[s * SEQ + tt * 128:s * SEQ + (tt + 1) * 128, :], x1_, reads=[Bx1_], chan=ch)
        S.barrier()

    P = Carver(PBASE - SEQ * 8)
    wup = P.take(BF16, 8, 4096)
    wdn = P.take(BF16, 32, DM)
    gffn = P.take(F32, DM)
    gfin = P.take(F32, DM)
    xb = [P.take(F32, DM) for _ in range(4)]
    hb = P.take(BF16, DM)
    junk = P.take(BF16, DM)
    h2T = P.take(BF16, 8, 256)
    rl = [P.take(F32, 256) for _ in range(2)]
    hid = P.take(BF16, 32, 256)
    ob = [P.take(F32, DM) for _ in range(2)]
    Bwup, Bwdn, Bg = Buf("wup"), Buf("wdn"), Buf("g")
    Bxb = [Buf("xb%d" % i) for i in range(4)]
    Bhb, Bh2T, Bhid = Buf("hb"), Buf("h2T"), Buf("hid")
    Brl = [Buf("rl0"), Buf("rl1")]
    Bob = [Buf("ob0"), Buf("ob1")]
    S.dma("sp", gffn, gffn_d, writes=[Bg], chan="cst")
    S.dma("sp", gfin, gfin_d, writes=[Bg], chan="cst")
    for kc in range(8):
        for hf in range(2):
            S.dma("pool", wup[:, kc, hf * 2048:(hf + 1) * 2048], w_up[kc * 128:(kc + 1) * 128, hf * 2048:(hf + 1) * 2048],
                  writes=[Bwup], chan="wu")
    for kc in range(32):
        S.dma("pool", wdn[:, kc, :], w_dn[kc * 128:(kc + 1) * 128, :], writes=[Bwdn], chan="wd")
    rlrot = Rot(list(zip(rl, Brl)))
    obrot = Rot(list(zip(ob, Bob, ("so0", "so1"))))
    outf = out.rearrange("s t d -> (s t) d")
    for j in range(16):
        for t in range(2):
            tt = j * 2 + t
            k4 = tt % 4
            norm_transpose(x1s[tt * 128:(tt + 1) * 128, :], xb[k4], Bxb[k4], "xb%d" % k4, hb, Bhb, h2T, Bh2T, t * 128,
                           gffn, k4)
        for m in range(32):
            ps, Bps = pool7.next()
            mm_group(ps[:, 0:256], Bps, [(wup[:, kc, m * 128:(m + 1) * 128], h2T[:, kc, :]) for kc in range(8)],
                     [Bwup, Bh2T, Bg])
            rl_, Brl_ = rlrot.next()
            S.op("act", "activation", dict(out=rl_, in_=ps[:, 0:256], func=AF.Relu), reads=[Bps], writes=[Brl_])
            S.op("dve", "tensor_tensor", dict(out=hid[:, m, :], in0=rl_, in1=rl_, op=ALU.mult), reads=[Brl_],
                 writes=[Bhid])
        for t in range(2):
            tt = j * 2 + t
            k4 = tt % 4
            for nh in range(2):
                ps, Bps = pool7.next()
                mm_group(ps, Bps, [(hid[:, kc, t * 128:(t + 1) * 128], wdn[:, kc, nh * 512:(nh + 1) * 512])
                                   for kc in range(32)], [Bwdn, Bhid])
                S.op("dve", "tensor_tensor", dict(out=xb[k4][:, nh * 512:(nh + 1) * 512], in0=ps[:],
                                                  in1=xb[k4][:, nh * 512:(nh + 1) * 512], op=ALU.add),
                     reads=[Bps, Bxb[k4]], writes=[Bxb[k4]])
            si = 4 + k4
            S.op("act", "activation", dict(out=junk, in_=xb[k4], func=AF.Square, scale=1.0 / 32.0,
                                           accum_out=ssq[:, si:si + 1]), reads=[Bxb[k4]], writes=[Bssq[si]])
            S.op("pool", "tensor_scalar", dict(out=rs[:, si:si + 1], in0=ssq[:, si:si + 1], scalar1=EPS, scalar2=None,
                                               op0=ALU.add), reads=[Bssq[si]], writes=[Brs[si]])
            S.op("pool", "tensor_tensor", dict(out=rs[:, si:si + 1], in0=rs[:, si:si + 1], in1=mh[:, 0:1], op=ALU.pow),
                 reads=[Brs[si]], writes=[Brs[si]])
            ob_, Bob_, ch = obrot.next()
            S.op("dve", "scalar_tensor_tensor", dict(out=ob_, in0=xb[k4], scalar=rs[:, si:si + 1], in1=gfin,
                                                     op0=ALU.mult, op1=ALU.mult),
                 reads=[Bxb[k4], Brs[si], Bg], writes=[Bob_])
            S.dma("sp", outf[tt * 128:(tt + 1) * 128, :], ob_, reads=[Bob_], chan=ch)
    S.finish()
    counts = S.emit(nc)
    return nc, counts


_CACHE = {}


def kernel(x, norm_mix_g, w_in, b_forget, conv_dw_w, conv_dw_b, conv_ln_g, conv_ln_b, w_conv_pw, b_conv_pw,
           attn_out_g, conv_out_g, w_out, norm_ffn_g, w_ffn_up, w_ffn_down, norm_final_g):
    f32 = np.float32
    x = np.asarray(x, f32)

    def bc(v):
        return np.ascontiguousarray(np.broadcast_to(np.asarray(v, f32).reshape(1, DM), (128, DM)))

    def pvec(v):
        return np.asarray(v, f32).reshape(4, 128).T

    pv = np.ascontiguousarray(np.concatenate(
        [pvec(conv_dw_b), pvec(conv_ln_g), pvec(conv_ln_b), pvec(b_conv_pw), pvec(attn_out_g), pvec(conv_out_g)],
        axis=1))
    cw = np.ascontiguousarray(np.asarray(conv_dw_w, f32).reshape(31, 4, 128).transpose(2, 1, 0).reshape(128, 124))
    shared = {
        "w_in": np.ascontiguousarray(np.asarray(w_in, f32).reshape(DM, 2568)),
        "w_pw": np.ascontiguousarray(np.asarray(w_conv_pw, f32).reshape(512, 512)),
        "w_out": np.ascontiguousarray(np.asarray(w_out, f32).reshape(DM, DM)),
        "w_up": np.ascontiguousarray(np.asarray(w_ffn_up, f32).reshape(DM, 4096)),
        "w_dn": np.ascontiguousarray(np.asarray(w_ffn_down, f32).reshape(4096, DM)),
        "gmix": bc(norm_mix_g), "gffn": bc(norm_ffn_g), "gfin": bc(norm_final_g),
        "pv": pv, "cw": cw,
        "bfg": np.ascontiguousarray(np.asarray(b_forget, f32).reshape(8, 1)),
    }
    if "nc" not in _CACHE:
        _CACHE["nc"] = build_program()[0]
    nc = _CACHE["nc"]
    in_maps = []
    for i in range(NCORES):
        m = dict(shared)
        m["x"] = np.ascontiguousarray(x[2 * i:2 * i + 2])
        in_maps.append(m)
    res = run_bass_kernel_spmd(nc, in_maps, core_ids=list(range(NCORES)))
    return np.concatenate([np.asarray(r["out"], f32) for r in res.results], axis=0)
```

```python
from contextlib import ExitStack
import numpy as np
import concourse.bass as bass
import concourse.mybir as mybir
from concourse.bass_utils import run_bass_kernel_spmd

F32 = mybir.dt.float32
BF16 = mybir.dt.bfloat16
U8 = mybir.dt.uint8
AF = mybir.ActivationFunctionType
ALU = mybir.AluOpType

EPS = 1e-6
NEG = -30000.0
NCORES = 8
SEQ = 2048
DM = 1024
ARENA = 206848
ENGS = ("pe", "act", "dve", "pool", "sp")


class Buf:
    __slots__ = ("name", "w", "r")

    def __init__(self, name):
        self.name = name
        self.w = {}
        self.r = {}


class Op:
    __slots__ = ("eng", "meth", "kw", "deps", "sig", "idx", "chan", "src", "order")


class Sched:
    def __init__(self):
        self.ops = []
        self.last = {}
        self.pending = {e: [] for e in ENGS}
        self.chan_count = {}

    def _add(self, eng, meth, kw, reads, writes, chan=None):
        op = Op()
        op.eng, op.meth, op.kw, op.chan = eng, meth, kw, chan
        op.src = ("c:" + chan) if chan else eng
        op.order = len(self.ops)
        op.sig = False
        op.idx = 0
        raw, oth = {}, {}

        def add(d, o):
            k = o.src
            if k not in d or d[k].order < o.order:
                d[k] = o

        for b in reads:
            for o in b.w.values():
                add(raw, o)
        for b in writes:
            for o in b.w.values():
                add(oth, o)
            for o in b.r.values():
                add(oth, o)
        for o in self.pending[eng]:
            add(oth, o)
        self.pending[eng] = []
        deps = {}
        for k, o in raw.items():
            if k == eng and eng == "pe":
                continue
            deps[k] = o
        for k, o in oth.items():
            if k == eng:
                continue
            if k not in deps or deps[k].order < o.order:
                deps[k] = o
        op.deps = list(deps.values())
        for o in op.deps:
            o.sig = True
        for b in reads:
            b.r[op.src] = op
        for b in writes:
            b.w[op.src] = op
        self.last[op.src] = op
        if chan:
            self.chan_count[chan] = self.chan_count.get(chan, 0) + 1
            op.idx = 16 * self.chan_count[chan]
        self.ops.append(op)
        return op

    def op(self, eng, meth, kw, reads=(), writes=()):
        return self._add(eng, meth, kw, reads, writes)

    def dma(self, queue, out, in_, reads=(), writes=(), chan=None):
        return self._add(queue, "dma_start", dict(out=out, in_=in_), reads, writes, chan=chan)

    def barrier(self):
        srcs = list(self.last.values())
        for e in ENGS:
            self.pending[e] = list(srcs)

    def finish(self):
        self.barrier()
        self._add("sp", None, None, (), ())

    def emit(self, nc):
        cnt = {e: 0 for e in ENGS}
        for o in self.ops:
            if o.chan is None and o.sig:
                cnt[o.eng] += 1
                o.idx = cnt[o.eng]
        streams = {e: [o for o in self.ops if o.eng == e] for e in ENGS}
        names = list(ENGS) + ["c:" + c for c in self.chan_count]
        with ExitStack() as st:
            sems = {n: st.enter_context(nc.semaphore("s_" + n.replace(":", "_"))) for n in names}
            block = st.enter_context(nc.Block())

            def run(e, h):
                waited = {}
                for o in streams[e]:
                    for d in o.deps:
                        if waited.get(d.src, 0) < d.idx:
                            h.wait_ge(sems[d.src], d.idx)
                            waited[d.src] = d.idx
                    if o.meth is None:
                        continue
                    ins = getattr(h, o.meth)(**o.kw)
                    if o.chan is not None:
                        ins.then_inc(sems[o.src], 16)
                    elif o.sig:
                        ins.then_inc(sems[o.src], 1)

            @block.sync
            def _(h):
                run("sp", h)

            @block.gpsimd
            def _(h):
                run("pool", h)

            @block.scalar
            def _(h):
                run("act", h)

            @block.vector
            def _(h):
                run("dve", h)

            @block.tensor
            def _(h):
                run("pe", h)
        return {e: len(streams[e]) for e in ENGS}


class Rot:
    def __init__(self, items):
        self.items = list(items)
        self.i = 0

    def next(self):
        it = self.items[self.i % len(self.items)]
        self.i += 1
        return it


def build_program():
    nc = bass.Bass("TRN2", target_bir_lowering=False)
    S = Sched()

    def din(name, shape):
        return nc.dram_tensor(name, list(shape), F32, kind="ExternalInput").ap()

    x = din("x", [2, SEQ, DM])
    w_in = din("w_in", [DM, 2568])
    w_pw = din("w_pw", [512, 512])
    w_out = din("w_out", [DM, DM])
    w_up = din("w_up", [DM, 4096])
    w_dn = din("w_dn", [4096, DM])
    gmix_d = din("gmix", [128, DM])
    gffn_d = din("gffn", [128, DM])
    gfin_d = din("gfin", [128, DM])
    pv_d = din("pv", [128, 24])
    cw_d = din("cw", [128, 124])
    bfg_d = din("bfg", [8, 1])
    out = nc.dram_tensor("out", [2, SEQ, DM], F32, kind="ExternalOutput").ap()
    x1s = nc.dram_tensor("x1s", [2 * SEQ, DM], F32).ap()

    arena = nc.alloc_sbuf_tensor("arena", [128, ARENA], U8)
    pb = [nc.alloc_psum_tensor("pb%d" % i, [128, 512], F32)[:] for i in range(8)]
    Bpb = [Buf("pb%d" % i) for i in range(8)]
    pbT = pb[7].bitcast(BF16)

    class Carver:
        def __init__(self, base):
            self.off = base

        def take(self, dtype, *shape):
            n = 1
            for s_ in shape:
                n *= s_
            nb = n * (4 if dtype == F32 else 2)
            ap = arena[:, self.off:self.off + nb].bitcast(dtype)
            self.off += (nb + 31) // 32 * 32
            assert self.off <= ARENA, self.off
            if len(shape) == 2:
                ap = ap.rearrange("p (a b) -> p a b", a=shape[0])
            elif len(shape) == 3:
                ap = ap.rearrange("p (a b c) -> p a b c", a=shape[0], b=shape[1])
            return ap

    C = Carver(0)
    ident = C.take(BF16, 128)
    maskT = C.take(BF16, 128)
    zt = C.take(BF16, 128)
    L1 = C.take(BF16, 128)
    L2 = C.take(BF16, 128)
    Lg = C.take(BF16, 128)
    o512 = C.take(BF16, 128)
    ones3 = C.take(BF16, 512)
    mh = C.take(F32, 512)
    gmix = C.take(F32, DM)
    pv = C.take(F32, 24)
    hgb = C.take(F32, 8)
    bfg = C.take(F32, 1)
    negb = C.take(F32, 1)
    ssq = C.take(F32, 8)
    epsb = C.take(F32, 1)
    rs = C.take(F32, 8)
    attnTs = [C.take(BF16, 4, SEQ) for _ in range(2)]
    PBASE = C.off
    BattnTs = [Buf("attnT0"), Buf("attnT1")]

    ws_inb = nc.dram_tensor("ws_inb", [DM, 1024], BF16).ap()
    ws_pw = nc.dram_tensor("ws_pw", [512, 512], BF16).ap()
    ws_out = nc.dram_tensor("ws_out", [DM, DM], BF16).ap()
    ws_up = nc.dram_tensor("ws_up", [DM, 4096], BF16).ap()
    ws_dn = nc.dram_tensor("ws_dn", [4096, DM], BF16).ap()
    Bpre2, Bpre3 = Buf("pre2"), Buf("pre3")

    S.dma("sp", gmix, gmix_d, chan="cst")
    S.dma("sp", pv, pv_d, chan="cst")
    S.dma("sp", bfg[0:8, :], bfg_d, chan="cst")
    S.op("pool", "memset", dict(ap=ones3, constant=1.0))
    S.op("pool", "memset", dict(ap=zt, constant=0.0))
    S.op("pool", "memset", dict(ap=mh, constant=-0.5))
    S.op("pool", "memset", dict(ap=epsb, constant=EPS))
    S.op("pool", "affine_select", dict(out=ident, in_=ones3[:, 0:128], pattern=[[-1, 128]],
                                       compare_op=ALU.is_equal, fill=0.0, base=0, channel_multiplier=1))
    S.op("pool", "affine_select", dict(out=maskT, in_=zt, pattern=[[1, 128]],
                                       compare_op=ALU.is_ge, fill=NEG, base=0, channel_multiplier=-1))
    for t_, blocks in ((L1, ((0, 64, 0, 64, 1.0 / 64), (64, 128, 0, 64, EPS / 64), (0, 128, 64, 128, 0.0))),
                       (L2, ((0, 64, 64, 128, EPS / 64), (64, 128, 64, 128, 1.0 / 64), (0, 128, 0, 64, 0.0))),
                       (Lg, ((0, 64, 0, 64, 1.0 / 64), (64, 128, 64, 128, 1.0 / 64), (0, 64, 64, 128, 0.0),
                             (64, 128, 0, 64, 0.0))),
                       (o512, ((0, 128, 0, 128, 1.0 / 512),))):
        for (p0, p1, f0, f1, v) in blocks:
            S.op("dve", "memset", dict(ap=t_[p0:p1, f0:f1], constant=v))
    S.barrier()
    S.op("dve", "tensor_scalar", dict(out=hgb, in0=pv[:, 4:12], scalar1=-1.0, scalar2=None, op0=ALU.mult))
    S.op("dve", "tensor_scalar", dict(out=negb[0:8, :], in0=bfg[0:8, :], scalar1=-1.0, scalar2=None, op0=ALU.mult))
    S.barrier()

    pool7 = Rot([(pb[i], Bpb[i]) for i in range(7)])
    Bssq = [Buf("ssq%d" % i) for i in range(8)]
    Brs = [Buf("rs%d" % i) for i in range(8)]

    def norm_transpose(src_rows, xt_ap, Bxt, chan, hb, Bhb, hT, BhT, tcol, gvec, si, junk):
        S.dma("sp", xt_ap, src_rows, writes=[Bxt], chan=chan)
        Bs = Bssq[si]
        S.op("act", "activation", dict(out=junk, in_=xt_ap, func=AF.Square, scale=1.0 / 32.0,
                                       accum_out=ssq[:, si:si + 1]), reads=[Bxt], writes=[Bs])
        S.op("pool", "tensor_scalar", dict(out=rs[:, si:si + 1], in0=ssq[:, si:si + 1], scalar1=EPS, scalar2=None,
                                           op0=ALU.add), reads=[Bs], writes=[Brs[si]])
        S.op("pool", "tensor_tensor", dict(out=rs[:, si:si + 1], in0=rs[:, si:si + 1], in1=mh[:, 0:1], op=ALU.pow),
             reads=[Brs[si]], writes=[Brs[si]])
        S.op("dve", "scalar_tensor_tensor", dict(out=hb, in0=xt_ap, scalar=rs[:, si:si + 1], in1=gvec,
                                                 op0=ALU.mult, op1=ALU.mult), reads=[Bxt, Brs[si]], writes=[Bhb])
        for kc in range(8):
            S.op("pe", "transpose", dict(out=pbT[:, kc * 128:(kc + 1) * 128], in_=hb[:, kc * 128:(kc + 1) * 128],
                                         identity=ident), reads=[Bhb], writes=[Bpb[7]])
        S.op("act", "copy", dict(out=hT[:, :, tcol:tcol + 128], in_=pbT.rearrange("p (k t) -> p k t", k=8)),
             reads=[Bpb[7]], writes=[BhT])

    def mm_group(ps, Bps, pairs, reads):
        n = len(pairs)
        for i, (l, r) in enumerate(pairs):
            S.op("pe", "matmul", dict(out=ps, lhsT=l, rhs=r, start=(i == 0), stop=(i == n - 1)),
                 reads=reads, writes=[Bps])

    P = Carver(PBASE)
    win_a = P.take(BF16, 8, 1544)
    KA = P.take(BF16, 2, 4, SEQ)
    Vt = P.take(BF16, 16, 768)
    QA = [P.take(BF16, 2, 4, 512) for _ in range(2)]
    hT = P.take(BF16, 8, 512)
    xt = [P.take(F32, DM) for _ in range(2)]
    hb = P.take(BF16, DM)
    junk = P.take(BF16, DM)
    ef = P.take(F32, 512)
    spf = P.take(F32, 512)
    cn = [P.take(F32, 512) for _ in range(2)]
    r1 = P.take(F32, 512)
    r2 = P.take(F32, 512)
    kp = P.take(BF16, 3, 512)
    qp = P.take(BF16, 3, 512)
    pts = [P.take(BF16, 512) for _ in range(4)]
    sqA = P.take(BF16, 512)
    sqB = P.take(BF16, 512)
    stsb = P.take(F32, 512)
    rsat = P.take(F32, 512)
    stg = [P.take(F32, 1536) for _ in range(2)]
    stgf = P.take(F32, 8, 8)
    Bwa, BhT, Bhb = Buf("wa"), Buf("hT"), Buf("hb")
    BKAs = [Buf("KA%d" % i) for i in range(4)]
    BKaugs = [Buf("Kaug%d" % i) for i in range(4)]
    BVts = [Buf("Vt%d" % i) for i in range(4)]
    BQA = [Buf("QA0"), Buf("QA1")]
    BQaug = [Buf("Qaug0"), Buf("Qaug1")]
    Bxt = [Buf("xt0"), Buf("xt1")]
    Bef, Bspf, Br1, Br2, Bkp, Bqp = Buf("ef"), Buf("spf"), Buf("r1"), Buf("r2"), Buf("kp"), Buf("qp")
    Bcn = [Buf("cn0"), Buf("cn1")]
    Bpts = [Buf("pt%d" % i) for i in range(4)]
    BsqA, BsqB, Bstsb, Brsat = Buf("sqA"), Buf("sqB"), Buf("stsb"), Buf("rsat")
    Bstg = [Buf("stg0"), Buf("stg1")]
    Bstgf = Buf("stgf")
    Binit = Buf("init")

    w_in_k = w_in.rearrange("(kc p) n -> p kc n", p=128)
    S.dma("sp", stgf, w_in_k[:, :, 2560:2568], writes=[Bstgf], chan="sgf")
    for kc in range(8):
        k = kc % 2
        S.dma("sp", stg[k], w_in[kc * 128:(kc + 1) * 128, 0:1536], writes=[Bstg[k]], chan="sg%d" % k)
        S.op("dve" if k == 0 else "pool", "tensor_copy", dict(out=win_a[:, kc, 0:1536], in_=stg[k]),
             reads=[Bstg[k]], writes=[Bwa])
    S.op("dve", "tensor_copy", dict(out=win_a[:, :, 1536:1544], in_=stgf), reads=[Bstgf], writes=[Bwa])
    precast = []
    for kc in range(8):
        precast.append((ws_inb[kc * 128:(kc + 1) * 128, :], w_in[kc * 128:(kc + 1) * 128, 1536:2560], Bpre2, "pre2"))
    for kc in range(4):
        precast.append((ws_pw[kc * 128:(kc + 1) * 128, :], w_pw[kc * 128:(kc + 1) * 128, :], Bpre2, "pre2"))
    for kc in range(8):
        precast.append((ws_out[kc * 128:(kc + 1) * 128, :], w_out[kc * 128:(kc + 1) * 128, :], Bpre2, "pre2"))
    for kc in range(8):
        for hf in range(2):
            precast.append((ws_up[kc * 128:(kc + 1) * 128, hf * 2048:(hf + 1) * 2048],
                            w_up[kc * 128:(kc + 1) * 128, hf * 2048:(hf + 1) * 2048], Bpre3, "pre3"))
    for kc in range(32):
        precast.append((ws_dn[kc * 128:(kc + 1) * 128, :], w_dn[kc * 128:(kc + 1) * 128, :], Bpre3, "pre3"))

    def emit_precast(n):
        for _ in range(n):
            if precast:
                o_, i_, B_, ch_ = precast.pop(0)
                S.dma("pool", o_, i_, writes=[B_], chan=ch_)

    S.op("pool", "memset", dict(ap=KA[64:128, 0], constant=0.0), writes=[Binit])
    S.op("pool", "memset", dict(ap=KA[0:64, 1], constant=0.0), writes=[Binit])
    for q_ in QA:
        S.op("dve", "memset", dict(ap=q_[64:128, 0], constant=0.0), writes=[Binit])
        S.op("dve", "memset", dict(ap=q_[0:64, 1], constant=0.0), writes=[Binit])
    for c in range(4):
        S.op("dve", "memset", dict(ap=Vt[:, :, c * 192 + 64:c * 192 + 128], constant=1.0), writes=[Binit])
    for par, a0 in ((0, 64), (1, 0)):
        for c in range(4):
            for blk in range(4):
                S.dma("sp", KA[a0:a0 + 3, par, c, blk * 512:(blk + 1) * 512], ones3[0:3, :],
                      reads=[Binit], writes=[Binit], chan="ini")
            for q_ in QA:
                S.dma("sp", q_[a0 + 3:a0 + 6, par, c, :], ones3[0:3, :], reads=[Binit], writes=[Binit], chan="ini")
    for b_ in BKAs + BKaugs + BVts + BQA + BQaug:
        b_.w.update(Binit.w)

    pool3 = Rot([(pb[i], Bpb[i]) for i in range(3)])
    accsets = Rot([((pb[3], Bpb[3]), (pb[4], Bpb[4])), ((pb[5], Bpb[5]), (pb[6], Bpb[6]))])
    ptrot = Rot(list(zip(pts, Bpts)))

    def p1_inproj(s, b, qs):
        cols = slice(b * 512, (b + 1) * 512)
        BKA, BKaug, BVt = BKAs[b], BKaugs[b], BVts[b]
        for t in range(4):
            tt = b * 4 + t
            sl = tt % 2
            norm_transpose(x[s, tt * 128:(tt + 1) * 128, :], xt[sl], Bxt[sl], "xt%d" % sl, hb, Bhb, hT, BhT,
                           t * 128, gmix, sl, junk)
            if not (s == 0 and b == 0):
                emit_precast(4)
        for c in range(4):
            ps, Bps = pool7.next()
            mm_group(ps, Bps, [(win_a[:, kc, c * 128:(c + 1) * 128], hT[:, kc, :]) for kc in range(8)], [Bwa, BhT])
            S.op("act", "activation", dict(out=QA[qs][0:64, 0, c, :], in_=ps[0:64, :], func=AF.Identity,
                                           scale=0.125), reads=[Bps], writes=[BQA[qs]])
            S.op("act", "activation", dict(out=QA[qs][64:128, 1, c, :], in_=ps[64:128, :], func=AF.Identity,
                                           scale=0.125), reads=[Bps], writes=[BQA[qs]])
        for c in range(4):
            ps, Bps = pool7.next()
            mm_group(ps, Bps, [(win_a[:, kc, 512 + c * 128:512 + (c + 1) * 128], hT[:, kc, :]) for kc in range(8)],
                     [Bwa, BhT])
            S.op("dve", "tensor_copy", dict(out=KA[0:64, 0, c, cols], in_=ps[0:64, :]), reads=[Bps], writes=[BKA])
            S.op("dve", "tensor_copy", dict(out=KA[64:128, 1, c, cols], in_=ps[64:128, :]), reads=[Bps],
                 writes=[BKA])
        for t in range(4):
            tt = b * 4 + t
            ps, Bps = pool7.next()
            mm_group(ps, Bps, [(hT[:, kc, t * 128:(t + 1) * 128], win_a[:, kc, 1024:1536]) for kc in range(8)],
                     [Bwa, BhT])
            vdst = Vt[:, tt, :].rearrange("p (c x) -> p c x", x=192)
            vsrc = ps.rearrange("p (c e d) -> p c e d", e=2, d=64)
            S.op("dve", "tensor_copy", dict(out=vdst[:, :, 0:64], in_=vsrc[:, :, 0, :]), reads=[Bps], writes=[BVt])
            S.op("act", "copy", dict(out=vdst[:, :, 128:192], in_=vsrc[:, :, 1, :]), reads=[Bps], writes=[BVt])
        ps, Bps = pool7.next()
        mm_group(ps[0:8, :], Bps, [(win_a[:, kc, 1536:1544], hT[:, kc, :]) for kc in range(8)], [Bwa, BhT])
        S.op("act", "activation", dict(out=ef[0:8, :], in_=ps[0:8, :], func=AF.Exp, scale=-1.0, bias=negb[0:8, :]),
             reads=[Bps], writes=[Bef])
        S.op("act", "activation", dict(out=spf[0:8, :], in_=ef[0:8, :], func=AF.Ln, bias=1.0), reads=[Bef],
             writes=[Bspf])
        cur, prv = b % 2, (b + 1) % 2
        init = 0.0 if b == 0 else cn[prv][0:8, 511:512]
        S.op("dve", "tensor_tensor_scan", dict(out=cn[cur][0:8, :], data0=spf[0:8, :], data1=spf[0:8, :],
                                               initial=init, op0=ALU.add, op1=ALU.bypass),
             reads=[Bspf, Bcn[prv]], writes=[Bcn[cur]])
        S.op("dve", "tensor_copy", dict(out=kp[0:8, 0, :], in_=cn[cur][0:8, :]), reads=[Bcn[cur]], writes=[Bkp])
        S.op("dve", "tensor_tensor", dict(out=r1[0:8, :], in0=cn[cur][0:8, :], in1=kp[0:8, 0, :], op=ALU.subtract),
             reads=[Bcn[cur], Bkp], writes=[Br1])
        S.op("dve", "tensor_copy", dict(out=kp[0:8, 1, :], in_=r1[0:8, :]), reads=[Br1], writes=[Bkp])
        S.op("dve", "tensor_tensor", dict(out=r2[0:8, :], in0=r1[0:8, :], in1=kp[0:8, 1, :], op=ALU.subtract),
             reads=[Br1, Bkp], writes=[Br2])
        S.op("dve", "tensor_copy", dict(out=kp[0:8, 2, :], in_=r2[0:8, :]), reads=[Br2], writes=[Bkp])
        S.op("dve", "tensor_scalar", dict(out=qp[0:8], in0=kp[0:8], scalar1=-1.0, scalar2=None, op0=ALU.mult),
             reads=[Bkp], writes=[Bqp])
        for h in range(8):
            par, c = h % 2, h // 2
            a0 = 64 if par == 0 else 0
            S.dma("sp", KA[a0 + 3:a0 + 6, par, c, cols], kp[h:h + 1], reads=[Bkp], writes=[BKaug], chan="cq")
            S.dma("sp", QA[qs][a0:a0 + 3, par, c, :], qp[h:h + 1], reads=[Bqp], writes=[BQaug[qs]], chan="cq")

    def p1_attention(s, b, qs):
        cols = slice(b * 512, (b + 1) * 512)
        attnT, BattnT = attnTs[s], BattnTs[s]
        nch = 4 * b + 4
        for c in range(4):
            (accA, BaccA), (accB, BaccB) = accsets.next()
            units = [(par, j) for j in range(nch) for par in (0, 1)]
            info = {}

            def emit_qk(u):
                par, j = u
                dg = j >= 4 * b
                c0 = (j - 4 * b) * 128 if dg else 0
                ps, Bps = pool3.next()
                S.op("pe", "matmul", dict(out=ps[:, c0:512], lhsT=KA[:, par, c, j * 128:(j + 1) * 128],
                                          rhs=QA[qs][:, par, c, c0:512], start=True, stop=not dg),
                     reads=[BKAs[j // 4], BKaugs[j // 4], BQA[qs], BQaug[qs]], writes=[Bps])
                if dg:
                    S.op("pe", "matmul", dict(out=ps[:, c0:c0 + 128], lhsT=ident, rhs=maskT, start=False,
                                              stop=True), writes=[Bps])
                pt, Bpt = ptrot.next()
                S.op("act", "activation", dict(out=pt[:, c0:512], in_=ps[:, c0:512], func=AF.Exp), reads=[Bps],
                     writes=[Bpt])
                info[u] = (pt, Bpt, c0)

            def emit_pv(u):
                par, j = u
                pt, Bpt, c0 = info[u]
                acc, Bacc = (accA, BaccA) if par == 0 else (accB, BaccB)
                v0 = c * 192 + (0 if par == 0 else 64)
                S.op("pe", "matmul", dict(out=acc[:, c0:512], lhsT=Vt[:, j, v0:v0 + 128], rhs=pt[:, c0:512],
                                          start=(j == 0), stop=(j == nch - 1)), reads=[BVts[j // 4], Bpt], writes=[Bacc])

            D = 2
            for i in range(len(units) + D):
                if i < len(units):
                    emit_qk(units[i])
                if i >= D:
                    emit_pv(units[i - D])
            S.op("act", "activation", dict(out=sqA, in_=accA, func=AF.Square), reads=[BaccA], writes=[BsqA])
            S.op("act", "activation", dict(out=sqB, in_=accB, func=AF.Square), reads=[BaccB], writes=[BsqB])
            ps, Bps = pool3.next()
            S.op("pe", "matmul", dict(out=ps, lhsT=L1, rhs=sqA, start=True, stop=False), reads=[BsqA], writes=[Bps])
            S.op("pe", "matmul", dict(out=ps, lhsT=L2, rhs=sqB, start=False, stop=True), reads=[BsqB], writes=[Bps])
            S.op("act", "activation", dict(out=stsb, in_=ps, func=AF.Ln), reads=[Bps], writes=[Bstsb])
            S.op("act", "activation", dict(out=rsat, in_=stsb, func=AF.Exp, scale=-0.5), reads=[Bstsb],
                 writes=[Brsat])
            S.op("dve", "scalar_tensor_tensor", dict(out=attnT[0:64, c, cols], in0=accA[0:64, :],
                                                     scalar=pv[0:64, 16 + c:17 + c], in1=rsat[0:64, :],
                                                     op0=ALU.mult, op1=ALU.mult),
                 reads=[BaccA, Brsat], writes=[BattnT])
            S.op("dve", "scalar_tensor_tensor", dict(out=attnT[64:128, c, cols], in0=accB[64:128, :],
                                                     scalar=pv[64:128, 16 + c:17 + c], in1=rsat[64:128, :],
                                                     op0=ALU.mult, op1=ALU.mult),
                 reads=[BaccB, Brsat], writes=[BattnT])

    step = 0
    for s in range(2):
        p1_inproj(s, 0, step % 2)
        for b in range(4):
            if b < 3:
                p1_inproj(s, b + 1, (step + 1) % 2)
            p1_attention(s, b, step % 2)
            step += 1
    emit_precast(1000)
    S.barrier()

    P = Carver(PBASE)
    win_b = P.take(BF16, 8, 1024)
    wpw = P.take(BF16, 4, 512)
    diag = P.take(BF16, 4, 31, 128)
    wout = P.take(BF16, 8, DM)
    uT = P.take(BF16, 4, SEQ + 32)
    hT = P.take(BF16, 8, 512)
    xt4 = [P.take(F32, DM) for _ in range(4)]
    hb = P.take(BF16, DM)
    junk = P.take(BF16, DM)
    tg = [P.take(F32, 512) for _ in range(2)]
    ycv = P.take(F32, 4, 512)
    ybf = P.take(BF16, 4, 512)
    ysq = P.take(BF16, 4, 512)
    musb = P.take(F32, 512)
    nm2 = P.take(F32, 512)
    vare = P.take(F32, 512)
    rsln = P.take(F32, 512)
    t2 = [P.take(F32, 512) for _ in range(2)]
    s2 = P.take(BF16, 4, 512)
    vpw = P.take(F32, 512)
    sqc = P.take(BF16, 512)
    stc = P.take(F32, 512)
    rsc = P.take(F32, 512)
    convT = P.take(BF16, 4, 512)
    cw = P.take(F32, 124)
    Bwb, Bwpw, Bdiag, Bwout, BuT, BhT, Bhb = (Buf("wb"), Buf("wpw"), Buf("diag"), Buf("wout"), Buf("uT"),
                                              Buf("hT"), Buf("hb"))
    Bxt4 = [Buf("xt4_%d" % i) for i in range(4)]
    Btg = [Buf("tg0"), Buf("tg1")]
    Bycv = [Buf("ycv%d" % i) for i in range(4)]
    Bybf, Bysq, Bmusb, Bnm2, Bvare, Brsln = Buf("ybf"), Buf("ysq"), Buf("musb"), Buf("nm2"), Buf("vare"), Buf("rsln")
    Bt2 = [Buf("t2_0"), Buf("t2_1")]
    Bs2, Bvpw, Bsqc, Bstc, Brsc, BconvT, Bcw = (Buf("s2"), Buf("vpw"), Buf("sqc"), Buf("stc"), Buf("rsc"),
                                                Buf("convT"), Buf("cw"))

    S.dma("sp", cw, cw_d, writes=[Bcw], chan="cst")
    S.dma("sp", win_b, ws_inb.rearrange("(kc p) n -> p kc n", p=128), reads=[Bpre2], writes=[Bwb], chan="wb")
    S.dma("sp", wpw, ws_pw.rearrange("(kc p) n -> p kc n", p=128), reads=[Bpre2], writes=[Bwpw], chan="wb")
    S.dma("sp", wout, ws_out.rearrange("(kc p) n -> p kc n", p=128), reads=[Bpre2], writes=[Bwout], chan="wb")
    for cc in range(4):
        for k in range(31):
            S.op("dve", "tensor_scalar", dict(out=diag[:, cc, k, :], in0=ident,
                                              scalar1=cw[:, cc * 31 + k:cc * 31 + k + 1], scalar2=None,
                                              op0=ALU.mult), reads=[Bcw], writes=[Bdiag])
    tgrot = Rot(list(zip(tg, Btg)))
    t2rot = Rot(list(zip(t2, Bt2)))
    U0 = 2

    for s in range(2):
        attnT, BattnT = attnTs[s], BattnTs[s]
        for cc in range(4):
            S.op("dve", "memset", dict(ap=uT[:, cc, 0:32], constant=0.0), writes=[BuT])
        for b in range(4):
            for t in range(4):
                tt = b * 4 + t
                norm_transpose(x[s, tt * 128:(tt + 1) * 128, :], xt4[t], Bxt4[t], "xq%d" % t, hb, Bhb, hT, BhT,
                               t * 128, gmix, 2 + t, junk)
            for cc in range(4):
                psa, Bpsa = pool7.next()
                mm_group(psa, Bpsa, [(win_b[:, kc, cc * 128:(cc + 1) * 128], hT[:, kc, :]) for kc in range(8)],
                         [Bwb, BhT])
                psb, Bpsb = pool7.next()
                mm_group(psb, Bpsb, [(win_b[:, kc, 512 + cc * 128:512 + (cc + 1) * 128], hT[:, kc, :])
                                     for kc in range(8)], [Bwb, BhT])
                tg_, Btg_ = tgrot.next()
                S.op("act", "activation", dict(out=tg_, in_=psb, func=AF.Exp, scale=-1.0), reads=[Bpsb],
                     writes=[Btg_])
                S.op("act", "activation", dict(out=tg_, in_=tg_, func=AF.Ln, bias=1.0), reads=[Btg_], writes=[Btg_])
                S.op("act", "activation", dict(out=tg_, in_=tg_, func=AF.Exp, scale=-1.0), reads=[Btg_], writes=[Btg_])
                S.op("dve", "tensor_tensor", dict(out=uT[:, cc, 32 + b * 512:32 + (b + 1) * 512], in0=psa,
                                                  in1=tg_, op=ALU.mult), reads=[Btg_, Bpsa], writes=[BuT])
            for cc in range(4):
                ps, Bps = pool7.next()
                mm_group(ps, Bps, [(diag[:, cc, k, :], uT[:, cc, b * 512 + k + U0:b * 512 + k + U0 + 512])
                                   for k in range(31)], [Bdiag, BuT])
                S.op("act", "activation", dict(out=ycv[:, cc, :], in_=ps, func=AF.Identity, scale=1.0,
                                               bias=pv[:, 0 + cc:1 + cc]), reads=[Bps], writes=[Bycv[cc]])
                S.op("dve", "tensor_copy", dict(out=ybf[:, cc, :], in_=ycv[:, cc, :]), reads=[Bycv[cc]], writes=[Bybf])
                S.op("act", "activation", dict(out=ysq[:, cc, :], in_=ycv[:, cc, :], func=AF.Square),
                     reads=[Bycv[cc]], writes=[Bysq])
            psm, Bpsm = pool7.next()
            mm_group(psm, Bpsm, [(o512, ybf[:, cc, :]) for cc in range(4)], [Bybf])
            pse, Bpse = pool7.next()
            mm_group(pse, Bpse, [(o512, ysq[:, cc, :]) for cc in range(4)], [Bysq])
            S.op("dve", "tensor_copy", dict(out=musb, in_=psm), reads=[Bpsm], writes=[Bmusb])
            S.op("dve", "scalar_tensor_tensor", dict(out=nm2, in0=musb, scalar=-1.0, in1=musb, op0=ALU.mult,
                                                     op1=ALU.mult), reads=[Bmusb], writes=[Bnm2])
            S.op("dve", "scalar_tensor_tensor", dict(out=vare, in0=pse, scalar=EPS, in1=nm2, op0=ALU.add,
                                                     op1=ALU.add), reads=[Bpse, Bnm2], writes=[Bvare])
            S.op("act", "activation", dict(out=vare, in_=vare, func=AF.Ln), reads=[Bvare], writes=[Bvare])
            S.op("act", "activation", dict(out=rsln, in_=vare, func=AF.Exp, scale=-0.5), reads=[Bvare], writes=[Brsln])
            for cc in range(4):
                yc = ycv[:, cc, :]
                S.op("dve", "tensor_tensor", dict(out=yc, in0=yc, in1=musb, op=ALU.subtract),
                     reads=[Bycv[cc], Bmusb], writes=[Bycv[cc]])
                S.op("dve", "tensor_tensor", dict(out=yc, in0=yc, in1=rsln, op=ALU.mult),
                     reads=[Bycv[cc], Brsln], writes=[Bycv[cc]])
                t2_, Bt2_ = t2rot.next()
                S.op("act", "activation", dict(out=t2_, in_=yc, func=AF.Exp, scale=hgb[:, cc:cc + 1],
                                               bias=hgb[:, 4 + cc:5 + cc]), reads=[Bycv[cc]], writes=[Bt2_])
                S.op("act", "activation", dict(out=t2_, in_=t2_, func=AF.Ln, bias=1.0), reads=[Bt2_], writes=[Bt2_])
                S.op("act", "activation", dict(out=t2_, in_=t2_, func=AF.Exp, scale=-1.0), reads=[Bt2_], writes=[Bt2_])
                S.op("dve", "tensor_scalar", dict(out=yc, in0=yc, scalar1=pv[:, 4 + cc:5 + cc],
                                                  scalar2=pv[:, 8 + cc:9 + cc], op0=ALU.mult, op1=ALU.add),
                     reads=[Bycv[cc]], writes=[Bycv[cc]])
                S.op("dve", "tensor_tensor", dict(out=s2[:, cc, :], in0=t2_, in1=yc, op=ALU.mult),
                     reads=[Bt2_, Bycv[cc]], writes=[Bs2])
            for mo in range(4):
                ps, Bps = pool7.next()
                mm_group(ps, Bps, [(wpw[:, kc, mo * 128:(mo + 1) * 128], s2[:, kc, :]) for kc in range(4)],
                         [Bwpw, Bs2])
                S.op("act", "activation", dict(out=vpw, in_=ps, func=AF.Identity, scale=1.0,
                                               bias=pv[:, 12 + mo:13 + mo]), reads=[Bps], writes=[Bvpw])
                S.op("act", "activation", dict(out=sqc, in_=ps, func=AF.Square, scale=1.0,
                                               bias=pv[:, 12 + mo:13 + mo]), reads=[Bps], writes=[Bsqc])
                pss, Bpss = pool7.next()
                S.op("pe", "matmul", dict(out=pss, lhsT=Lg, rhs=sqc, start=True, stop=True), reads=[Bsqc],
                     writes=[Bpss])
                S.op("act", "activation", dict(out=stc, in_=pss, func=AF.Ln, bias=epsb[:, 0:1]), reads=[Bpss],
                     writes=[Bstc])
                S.op("act", "activation", dict(out=rsc, in_=stc, func=AF.Exp, scale=-0.5), reads=[Bstc], writes=[Brsc])
                S.op("dve", "scalar_tensor_tensor", dict(out=convT[:, mo, :], in0=vpw, scalar=pv[:, 20 + mo:21 + mo],
                                                         in1=rsc, op0=ALU.mult, op1=ALU.mult),
                     reads=[Bvpw, Brsc], writes=[BconvT])
            for t in range(4):
                tt = b * 4 + t
                for nh in range(2):
                    ps, Bps = pool7.next()
                    pairs = []
                    for kc in range(8):
                        l = attnT[:, kc, b * 512 + t * 128:b * 512 + (t + 1) * 128] if kc < 4 else \
                            convT[:, kc - 4, t * 128:(t + 1) * 128]
                        pairs.append((l, wout[:, kc, nh * 512:(nh + 1) * 512]))
                    mm_group(ps, Bps, pairs, [Bwout, BattnT, BconvT])
                    S.op("dve", "tensor_tensor", dict(out=xt4[t][:, nh * 512:(nh + 1) * 512], in0=ps,
                                                      in1=xt4[t][:, nh * 512:(nh + 1) * 512], op=ALU.add),
                         reads=[Bps, Bxt4[t]], writes=[Bxt4[t]])
                S.dma("sp", x1s[s * SEQ + tt * 128:s * SEQ + (tt + 1) * 128, :], xt4[t], reads=[Bxt4[t]],
                      chan="xs%d" % t)
    S.barrier()

    P = Carver(PBASE - 2 * SEQ * 8)
    wup = P.take(BF16, 8, 4096)
    wdn = P.take(BF16, 32, DM)
    gffn = P.take(F32, DM)
    gfin = P.take(F32, DM)
    xb = [P.take(F32, DM) for _ in range(4)]
    hbs = [P.take(BF16, DM) for _ in range(2)]
    junk = P.take(BF16, DM)
    h2Ts = [P.take(BF16, 8, 256) for _ in range(2)]
    rl = [P.take(F32, 256) for _ in range(2)]
    hid = P.take(BF16, 32, 256)
    Bwup = [Buf("wup%d" % i) for i in range(4)]
    Bwdn, Bg = Buf("wdn"), Buf("g")
    Bxb = [Buf("xb%d" % i) for i in range(4)]
    Bhbs = [Buf("hbB0"), Buf("hbB1")]
    Bh2Ts = [Buf("h2T0"), Buf("h2T1")]
    Bhid = Buf("hid")
    Brl = [Buf("rl0"), Buf("rl1")]
    S.dma("sp", gffn, gffn_d, writes=[Bg], chan="cst")
    S.dma("sp", gfin, gfin_d, writes=[Bg], chan="cst")
    ws_up_k = ws_up.rearrange("(kc p) n -> p kc n", p=128)
    ws_dn_k = ws_dn.rearrange("(kc p) n -> p kc n", p=128)
    for blk in range(4):
        S.dma("sp", wup[:, :, blk * 1024:(blk + 1) * 1024], ws_up_k[:, :, blk * 1024:(blk + 1) * 1024],
              reads=[Bpre3], writes=[Bwup[blk]], chan="wu%d" % blk)
    for q4 in range(4):
        S.dma("sp", wdn[:, q4 * 8:(q4 + 1) * 8, :], ws_dn_k[:, q4 * 8:(q4 + 1) * 8, :], reads=[Bpre3], writes=[Bwdn],
              chan="wd")
    rlrot = Rot(list(zip(rl, Brl)))
    outf = out.rearrange("s t d -> (s t) d")

    def b_norm(j):
        for t in range(2):
            tt = j * 2 + t
            k4 = tt % 4
            norm_transpose(x1s[tt * 128:(tt + 1) * 128, :], xb[k4], Bxb[k4], "xb%d" % k4, hbs[t], Bhbs[t],
                           h2Ts[j % 2], Bh2Ts[j % 2], t * 128, gffn, k4, junk)

    b_norm(0)
    for j in range(16):
        h2T, Bh2T = h2Ts[j % 2], Bh2Ts[j % 2]
        for m in range(32):
            ps, Bps = pool7.next()
            mm_group(ps[:, 0:256], Bps, [(wup[:, kc, m * 128:(m + 1) * 128], h2T[:, kc, :]) for kc in range(8)],
                     [Bwup[m // 8], Bh2T, Bg])
            rl_, Brl_ = rlrot.next()
            S.op("act", "activation", dict(out=rl_, in_=ps[:, 0:256], func=AF.Relu), reads=[Bps], writes=[Brl_])
            S.op("dve", "tensor_tensor", dict(out=hid[:, m, :], in0=rl_, in1=rl_, op=ALU.mult), reads=[Brl_],
                 writes=[Bhid])
        if j + 1 < 16:
            b_norm(j + 1)
        for t in range(2):
            tt = j * 2 + t
            k4 = tt % 4
            for nh in range(2):
                ps, Bps = pool7.next()
                mm_group(ps, Bps, [(hid[:, kc, t * 128:(t + 1) * 128], wdn[:, kc, nh * 512:(nh + 1) * 512])
                                   for kc in range(32)], [Bwdn, Bhid])
                S.op("dve", "tensor_tensor", dict(out=xb[k4][:, nh * 512:(nh + 1) * 512], in0=ps,
                                                  in1=xb[k4][:, nh * 512:(nh + 1) * 512], op=ALU.add),
                     reads=[Bps, Bxb[k4]], writes=[Bxb[k4]])
            si = 4 + k4
            S.op("act", "activation", dict(out=junk, in_=xb[k4], func=AF.Square, scale=1.0 / 32.0,
                                           accum_out=ssq[:, si:si + 1]), reads=[Bxb[k4]], writes=[Bssq[si]])
            S.op("pool", "tensor_scalar", dict(out=rs[:, si:si + 1], in0=ssq[:, si:si + 1], scalar1=EPS, scalar2=None,
                                               op0=ALU.add), reads=[Bssq[si]], writes=[Brs[si]])
            S.op("pool", "tensor_tensor", dict(out=rs[:, si:si + 1], in0=rs[:, si:si + 1], in1=mh[:, 0:1], op=ALU.pow),
                 reads=[Brs[si]], writes=[Brs[si]])
            S.op("dve", "scalar_tensor_tensor", dict(out=xb[k4], in0=xb[k4], scalar=rs[:, si:si + 1], in1=gfin,
                                                     op0=ALU.mult, op1=ALU.mult),
                 reads=[Bxb[k4], Brs[si], Bg], writes=[Bxb[k4]])
            S.dma("sp", outf[tt * 128:(tt + 1) * 128, :], xb[k4], reads=[Bxb[k4]], chan="so%d" % k4)
    S.finish()
    counts = S.emit(nc)
    return nc, counts


_CACHE = {}


def kernel(x, norm_mix_g, w_in, b_forget, conv_dw_w, conv_dw_b, conv_ln_g, conv_ln_b, w_conv_pw, b_conv_pw,
           attn_out_g, conv_out_g, w_out, norm_ffn_g, w_ffn_up, w_ffn_down, norm_final_g):
    f32 = np.float32
    x = np.asarray(x, f32)

    def bc(v):
        return np.ascontiguousarray(np.broadcast_to(np.asarray(v, f32).reshape(1, DM), (128, DM)))

    def pvec(v):
        return np.asarray(v, f32).reshape(4, 128).T

    pv = np.ascontiguousarray(np.concatenate(
        [pvec(conv_dw_b), pvec(conv_ln_g), pvec(conv_ln_b), pvec(b_conv_pw), pvec(attn_out_g), pvec(conv_out_g)],
        axis=1))
    cw = np.ascontiguousarray(np.asarray(conv_dw_w, f32).reshape(31, 4, 128).transpose(2, 1, 0).reshape(128, 124))
    shared = {
        "w_in": np.ascontiguousarray(np.asarray(w_in, f32).reshape(DM, 2568)),
        "w_pw": np.ascontiguousarray(np.asarray(w_conv_pw, f32).reshape(512, 512)),
        "w_out": np.ascontiguousarray(np.asarray(w_out, f32).reshape(DM, DM)),
        "w_up": np.ascontiguousarray(np.asarray(w_ffn_up, f32).reshape(DM, 4096)),
        "w_dn": np.ascontiguousarray(np.asarray(w_ffn_down, f32).reshape(4096, DM)),
        "gmix": bc(norm_mix_g), "gffn": bc(norm_ffn_g), "gfin": bc(norm_final_g),
        "pv": pv, "cw": cw,
        "bfg": np.ascontiguousarray(np.asarray(b_forget, f32).reshape(8, 1)),
    }
    if "nc" not in _CACHE:
        _CACHE["nc"] = build_program()[0]
    nc = _CACHE["nc"]
    in_maps = []
    for i in range(NCORES):
        m = dict(shared)
        m["x"] = np.ascontiguousarray(x[2 * i:2 * i + 2])
        in_maps.append(m)
    res = run_bass_kernel_spmd(nc, in_maps, core_ids=list(range(NCORES)))
    return np.concatenate([np.asarray(r["out"], f32) for r in res.results], axis=0)
```

```python
from contextlib import ExitStack
import numpy as np
import concourse.bass as bass
import concourse.mybir as mybir
from concourse.bass_utils import run_bass_kernel_spmd

F32 = mybir.dt.float32
BF16 = mybir.dt.bfloat16
U8 = mybir.dt.uint8
AF = mybir.ActivationFunctionType
ALU = mybir.AluOpType

EPS = 1e-6
NEG = -30000.0
NCORES = 8
SEQ = 2048
DM = 1024
ARENA = 211968
ENGS = ("pe", "act", "dve", "pool", "sp")


class Buf:
    __slots__ = ("name", "w", "r")

    def __init__(self, name):
        self.name = name
        self.w = {}
        self.r = {}


class Op:
    __slots__ = ("eng", "meth", "kw", "deps", "sig", "idx", "chan", "src", "order")


class Sched:
    def __init__(self):
        self.ops = []
        self.last = {}
        self.pending = {e: [] for e in ENGS}
        self.chan_count = {}

    def _add(self, eng, meth, kw, reads, writes, chan=None):
        op = Op()
        op.eng, op.meth, op.kw, op.chan = eng, meth, kw, chan
        op.src = ("c:" + chan) if chan else eng
        op.order = len(self.ops)
        op.sig = False
        op.idx = 0
        raw, oth = {}, {}

        def add(d, o):
            k = o.src
            if k not in d or d[k].order < o.order:
                d[k] = o

        for b in reads:
            for o in b.w.values():
                add(raw, o)
        for b in writes:
            for o in b.w.values():
                add(oth, o)
            for o in b.r.values():
                add(oth, o)
        for o in self.pending[eng]:
            add(oth, o)
        self.pending[eng] = []
        deps = {}
        for k, o in raw.items():
            if k == eng and eng == "pe":
                continue
            deps[k] = o
        for k, o in oth.items():
            if k == eng:
                continue
            if k not in deps or deps[k].order < o.order:
                deps[k] = o
        op.deps = list(deps.values())
        for o in op.deps:
            o.sig = True
        for b in reads:
            b.r[op.src] = op
        for b in writes:
            b.w[op.src] = op
        self.last[op.src] = op
        if chan:
            self.chan_count[chan] = self.chan_count.get(chan, 0) + 1
            op.idx = 16 * self.chan_count[chan]
        self.ops.append(op)
        return op

    def op(self, eng, meth, kw, reads=(), writes=()):
        return self._add(eng, meth, kw, reads, writes)

    def dma(self, queue, out, in_, reads=(), writes=(), chan=None):
        return self._add(queue, "dma_start", dict(out=out, in_=in_), reads, writes, chan=chan)

    def barrier(self):
        srcs = list(self.last.values())
        for e in ENGS:
            self.pending[e] = list(srcs)

    def finish(self):
        self.barrier()
        self._add("sp", None, None, (), ())

    def emit(self, nc):
        cnt = {e: 0 for e in ENGS}
        for o in self.ops:
            if o.chan is None and o.sig:
                cnt[o.eng] += 1
                o.idx = cnt[o.eng]
        streams = {e: [o for o in self.ops if o.eng == e] for e in ENGS}
        names = list(ENGS) + ["c:" + c for c in self.chan_count]
        with ExitStack() as st:
            sems = {n: st.enter_context(nc.semaphore("s_" + n.replace(":", "_"))) for n in names}
            block = st.enter_context(nc.Block())

            def run(e, h):
                waited = {}
                for o in streams[e]:
                    for d in o.deps:
                        if waited.get(d.src, 0) < d.idx:
                            h.wait_ge(sems[d.src], d.idx)
                            waited[d.src] = d.idx
                    if o.meth is None:
                        continue
                    ins = getattr(h, o.meth)(**o.kw)
                    if o.chan is not None:
                        ins.then_inc(sems[o.src], 16)
                    elif o.sig:
                        ins.then_inc(sems[o.src], 1)

            @block.sync
            def _(h):
                run("sp", h)

            @block.gpsimd
            def _(h):
                run("pool", h)

            @block.scalar
            def _(h):
                run("act", h)

            @block.vector
            def _(h):
                run("dve", h)

            @block.tensor
            def _(h):
                run("pe", h)
        return {e: len(streams[e]) for e in ENGS}


class Rot:
    def __init__(self, items):
        self.items = list(items)
        self.i = 0

    def next(self):
        it = self.items[self.i % len(self.items)]
        self.i += 1
        return it


def build_program():
    nc = bass.Bass("TRN2", target_bir_lowering=False)
    S = Sched()

    def din(name, shape):
        return nc.dram_tensor(name, list(shape), F32, kind="ExternalInput").ap()

    x = din("x", [2, SEQ, DM])
    w_in = din("w_in", [DM, 2568])
    w_pw = din("w_pw", [512, 512])
    w_out = din("w_out", [DM, DM])
    w_up = din("w_up", [DM, 4096])
    w_dn = din("w_dn", [4096, DM])
    gmix_d = din("gmix", [128, DM])
    gffn_d = din("gffn", [128, DM])
    gfin_d = din("gfin", [128, DM])
    pv_d = din("pv", [128, 24])
    cw_d = din("cw", [128, 124])
    bfg_d = din("bfg", [8, 1])
    out = nc.dram_tensor("out", [2, SEQ, DM], F32, kind="ExternalOutput").ap()
    x1s = nc.dram_tensor("x1s", [2 * SEQ, DM], F32).ap()

    arena = nc.alloc_sbuf_tensor("arena", [128, ARENA], U8)
    pb = [nc.alloc_psum_tensor("pb%d" % i, [128, 512], F32)[:] for i in range(8)]
    Bpb = [Buf("pb%d" % i) for i in range(8)]
    pbT = pb[7].bitcast(BF16)

    class Carver:
        def __init__(self, base):
            self.off = base

        def take(self, dtype, *shape):
            n = 1
            for s_ in shape:
                n *= s_
            nb = n * (4 if dtype == F32 else 2)
            ap = arena[:, self.off:self.off + nb].bitcast(dtype)
            self.off += (nb + 31) // 32 * 32
            assert self.off <= ARENA, self.off
            if len(shape) == 2:
                ap = ap.rearrange("p (a b) -> p a b", a=shape[0])
            elif len(shape) == 3:
                ap = ap.rearrange("p (a b c) -> p a b c", a=shape[0], b=shape[1])
            return ap

    C = Carver(0)
    ident = C.take(BF16, 128)
    maskT = C.take(BF16, 128)
    zt = C.take(BF16, 128)
    L1 = C.take(BF16, 128)
    L2 = C.take(BF16, 128)
    Lg = C.take(BF16, 128)
    o512 = C.take(BF16, 128)
    ones3 = C.take(BF16, 512)
    mh = C.take(F32, 512)
    gmix = C.take(F32, DM)
    pv = C.take(F32, 24)
    hgb = C.take(F32, 8)
    bfg = C.take(F32, 1)
    negb = C.take(F32, 1)
    ssq = C.take(F32, 8)
    epsb = C.take(F32, 1)
    rs = C.take(F32, 8)
    attnTs = [C.take(BF16, 4, SEQ) for _ in range(2)]
    PBASE = C.off
    BattnTs = [Buf("attnT0"), Buf("attnT1")]

    ws_inb = nc.dram_tensor("ws_inb", [DM, 1024], BF16).ap()
    ws_pw = nc.dram_tensor("ws_pw", [512, 512], BF16).ap()
    ws_out = nc.dram_tensor("ws_out", [DM, DM], BF16).ap()
    ws_up = nc.dram_tensor("ws_up", [DM, 4096], BF16).ap()
    ws_dn = nc.dram_tensor("ws_dn", [4096, DM], BF16).ap()
    Bpre2, Bpre3 = Buf("pre2"), Buf("pre3")

    S.dma("sp", gmix, gmix_d, chan="cst")
    S.dma("sp", pv, pv_d, chan="cst")
    S.dma("sp", bfg[0:8, :], bfg_d, chan="cst")
    S.op("pool", "memset", dict(ap=ones3, constant=1.0))
    S.op("pool", "memset", dict(ap=zt, constant=0.0))
    S.op("pool", "memset", dict(ap=mh, constant=-0.5))
    S.op("pool", "memset", dict(ap=epsb, constant=EPS))
    S.op("pool", "affine_select", dict(out=ident, in_=ones3[:, 0:128], pattern=[[-1, 128]],
                                       compare_op=ALU.is_equal, fill=0.0, base=0, channel_multiplier=1))
    S.op("pool", "affine_select", dict(out=maskT, in_=zt, pattern=[[1, 128]],
                                       compare_op=ALU.is_ge, fill=NEG, base=0, channel_multiplier=-1))
    for t_, blocks in ((L1, ((0, 64, 0, 64, 1.0 / 64), (64, 128, 0, 64, EPS / 64), (0, 128, 64, 128, 0.0))),
                       (L2, ((0, 64, 64, 128, EPS / 64), (64, 128, 64, 128, 1.0 / 64), (0, 128, 0, 64, 0.0))),
                       (Lg, ((0, 64, 0, 64, 1.0 / 64), (64, 128, 64, 128, 1.0 / 64), (0, 64, 64, 128, 0.0),
                             (64, 128, 0, 64, 0.0))),
                       (o512, ((0, 128, 0, 128, 1.0 / 512),))):
        for (p0, p1, f0, f1, v) in blocks:
            S.op("dve", "memset", dict(ap=t_[p0:p1, f0:f1], constant=v))
    S.barrier()
    S.op("dve", "tensor_scalar", dict(out=hgb, in0=pv[:, 4:12], scalar1=-1.0, scalar2=None, op0=ALU.mult))
    S.op("dve", "tensor_scalar", dict(out=negb[0:8, :], in0=bfg[0:8, :], scalar1=-1.0, scalar2=None, op0=ALU.mult))
    S.barrier()

    pool7 = Rot([(pb[i], Bpb[i]) for i in range(7)])
    Bssq = [Buf("ssq%d" % i) for i in range(8)]
    Brs = [Buf("rs%d" % i) for i in range(8)]

    def norm_transpose(src_rows, xt_ap, Bxt, chan, hb, Bhb, hT, BhT, tcol, gvec, si, junk):
        S.dma("sp", xt_ap, src_rows, writes=[Bxt], chan=chan)
        norm_front(xt_ap, Bxt, hb, Bhb, gvec, si, junk)
        norm_back(hb, Bhb, hT, BhT, tcol)

    def norm_front(xt_ap, Bxt, hb, Bhb, gvec, si, junk):
        Bs = Bssq[si]
        S.op("act", "activation", dict(out=junk, in_=xt_ap, func=AF.Square, scale=1.0 / 32.0,
                                       accum_out=ssq[:, si:si + 1]), reads=[Bxt], writes=[Bs])
        S.op("pool", "tensor_scalar", dict(out=rs[:, si:si + 1], in0=ssq[:, si:si + 1], scalar1=EPS, scalar2=None,
                                           op0=ALU.add), reads=[Bs], writes=[Brs[si]])
        S.op("pool", "tensor_tensor", dict(out=rs[:, si:si + 1], in0=rs[:, si:si + 1], in1=mh[:, 0:1], op=ALU.pow),
             reads=[Brs[si]], writes=[Brs[si]])
        S.op("dve", "scalar_tensor_tensor", dict(out=hb, in0=xt_ap, scalar=rs[:, si:si + 1], in1=gvec,
                                                 op0=ALU.mult, op1=ALU.mult), reads=[Bxt, Brs[si]], writes=[Bhb])

    def norm_back(hb, Bhb, hT, BhT, tcol):
        for kc in range(8):
            S.op("pe", "transpose", dict(out=pbT[:, kc * 128:(kc + 1) * 128], in_=hb[:, kc * 128:(kc + 1) * 128],
                                         identity=ident), reads=[Bhb], writes=[Bpb[7]])
        S.op("act", "copy", dict(out=hT[:, :, tcol:tcol + 128], in_=pbT.rearrange("p (k t) -> p k t", k=8)),
             reads=[Bpb[7]], writes=[BhT])

    def mm_group(ps, Bps, pairs, reads):
        n = len(pairs)
        for i, (l, r) in enumerate(pairs):
            S.op("pe", "matmul", dict(out=ps, lhsT=l, rhs=r, start=(i == 0), stop=(i == n - 1)),
                 reads=reads, writes=[Bps])

    P = Carver(PBASE)
    win_a = P.take(BF16, 8, 1544)
    KA = P.take(BF16, 2, 4, SEQ)
    Vt = P.take(BF16, 16, 768)
    QA = [P.take(BF16, 2, 4, 512) for _ in range(2)]
    hT = P.take(BF16, 8, 512)
    xt = [P.take(F32, DM) for _ in range(2)]
    hbp = [P.take(BF16, DM) for _ in range(2)]
    junk = P.take(BF16, DM)
    ef = P.take(F32, 512)
    spf = P.take(F32, 512)
    cn = [P.take(F32, 512) for _ in range(2)]
    r1 = P.take(F32, 512)
    r2 = P.take(F32, 512)
    kp = P.take(BF16, 3, 512)
    qp = P.take(BF16, 3, 512)
    pts = [P.take(BF16, 512) for _ in range(4)]
    sqA = P.take(BF16, 512)
    sqB = P.take(BF16, 512)
    stsb = P.take(F32, 512)
    rsat = P.take(F32, 512)
    stg = [P.take(F32, 1536) for _ in range(2)]
    stgf = P.take(F32, 8, 8)
    Bwa, BhT = Buf("wa"), Buf("hT")
    Bhbp = [Buf("hbp0"), Buf("hbp1")]
    BKAs = [Buf("KA%d" % i) for i in range(4)]
    BKaugs = [Buf("Kaug%d" % i) for i in range(4)]
    BVts = [Buf("Vt%d" % i) for i in range(4)]
    BQA = [Buf("QA0"), Buf("QA1")]
    BQaug = [Buf("Qaug0"), Buf("Qaug1")]
    Bxt = [Buf("xt0"), Buf("xt1")]
    Bef, Bspf, Br1, Br2, Bkp, Bqp = Buf("ef"), Buf("spf"), Buf("r1"), Buf("r2"), Buf("kp"), Buf("qp")
    Bcn = [Buf("cn0"), Buf("cn1")]
    Bpts = [Buf("pt%d" % i) for i in range(4)]
    BsqA, BsqB, Bstsb, Brsat = Buf("sqA"), Buf("sqB"), Buf("stsb"), Buf("rsat")
    Bstg = [Buf("stg0"), Buf("stg1")]
    Bstgf = Buf("stgf")
    xt = xt + [stg[0][:, 0:DM], stg[1][:, 0:DM]]
    Bxt = Bxt + Bstg
    xtchan = ["xt0", "xt1", "sg0", "sg1"]
    hts = nc.dram_tensor("hts", [8, 128, 8 * 512], BF16).ap()
    Binit = Buf("init")

    w_in_k = w_in.rearrange("(kc p) n -> p kc n", p=128)
    S.dma("act", stgf, w_in_k[:, :, 2560:2568], writes=[Bstgf], chan="sgf")
    for kc in range(8):
        k = kc % 2
        S.dma("act", stg[k], w_in[kc * 128:(kc + 1) * 128, 0:1536], writes=[Bstg[k]], chan="sg%d" % k)
        S.op("dve" if k == 0 else "pool", "tensor_copy", dict(out=win_a[:, kc, 0:1536], in_=stg[k]),
             reads=[Bstg[k]], writes=[Bwa])
    S.op("dve", "tensor_copy", dict(out=win_a[:, :, 1536:1544], in_=stgf), reads=[Bstgf], writes=[Bwa])
    precast = []
    for kc in range(8):
        precast.append((ws_inb[kc * 128:(kc + 1) * 128, :], w_in[kc * 128:(kc + 1) * 128, 1536:2560], Bpre2, "pre2"))
    for kc in range(4):
        precast.append((ws_pw[kc * 128:(kc + 1) * 128, :], w_pw[kc * 128:(kc + 1) * 128, :], Bpre2, "pre2"))
    for kc in range(8):
        precast.append((ws_out[kc * 128:(kc + 1) * 128, :], w_out[kc * 128:(kc + 1) * 128, :], Bpre2, "pre2"))
    for kc in range(8):
        for hf in range(2):
            precast.append((ws_up[kc * 128:(kc + 1) * 128, hf * 2048:(hf + 1) * 2048],
                            w_up[kc * 128:(kc + 1) * 128, hf * 2048:(hf + 1) * 2048], Bpre3, "pre3"))
    for kc in range(32):
        precast.append((ws_dn[kc * 128:(kc + 1) * 128, :], w_dn[kc * 128:(kc + 1) * 128, :], Bpre3, "pre3"))

    def emit_precast(n):
        for _ in range(n):
            if precast:
                o_, i_, B_, ch_ = precast.pop(0)
                S.dma("pool", o_, i_, writes=[B_], chan=ch_)

    S.op("pool", "memset", dict(ap=KA[64:128, 0], constant=0.0), writes=[Binit])
    S.op("pool", "memset", dict(ap=KA[0:64, 1], constant=0.0), writes=[Binit])
    for q_ in QA:
        S.op("dve", "memset", dict(ap=q_[64:128, 0], constant=0.0), writes=[Binit])
        S.op("dve", "memset", dict(ap=q_[0:64, 1], constant=0.0), writes=[Binit])
    for c in range(4):
        S.op("dve", "memset", dict(ap=Vt[:, :, c * 192 + 64:c * 192 + 128], constant=1.0), writes=[Binit])
    for par, a0 in ((0, 64), (1, 0)):
        S.op("pool", "memset", dict(ap=KA[a0:a0 + 6, par], constant=1.0), reads=[Binit], writes=[Binit])
        for q_ in QA:
            S.op("dve", "memset", dict(ap=q_[a0:a0 + 6, par], constant=1.0), reads=[Binit], writes=[Binit])
    for b_ in BKAs + BKaugs + BVts + BQA + BQaug:
        b_.w.update(Binit.w)

    pool3 = Rot([(pb[i], Bpb[i]) for i in range(3)])
    accsets = Rot([((pb[3], Bpb[3]), (pb[4], Bpb[4])), ((pb[5], Bpb[5]), (pb[6], Bpb[6]))])
    ptrot = Rot(list(zip(pts, Bpts)))

    def p1_inproj(s, b, qs):
        cols = slice(b * 512, (b + 1) * 512)
        BKA, BKaug, BVt = BKAs[b], BKaugs[b], BVts[b]
        S.dma("sp", hts[s * 4 + b].rearrange("p (k t) -> p k t", k=8), hT, reads=[BhT], chan="hts")
        for c in range(4):
            ps, Bps = pool7.next()
            mm_group(ps, Bps, [(win_a[:, kc, c * 128:(c + 1) * 128], hT[:, kc, :]) for kc in range(8)], [Bwa, BhT])
            S.op("act", "activation", dict(out=QA[qs][0:64, 0, c, :], in_=ps[0:64, :], func=AF.Identity,
                                           scale=0.125), reads=[Bps], writes=[BQA[qs]])
            S.op("act", "activation", dict(out=QA[qs][64:128, 1, c, :], in_=ps[64:128, :], func=AF.Identity,
                                           scale=0.125), reads=[Bps], writes=[BQA[qs]])
        for c in range(4):
            ps, Bps = pool7.next()
            mm_group(ps, Bps, [(win_a[:, kc, 512 + c * 128:512 + (c + 1) * 128], hT[:, kc, :]) for kc in range(8)],
                     [Bwa, BhT])
            S.op("dve", "tensor_copy", dict(out=KA[0:64, 0, c, cols], in_=ps[0:64, :]), reads=[Bps], writes=[BKA])
            S.op("dve", "tensor_copy", dict(out=KA[64:128, 1, c, cols], in_=ps[64:128, :]), reads=[Bps],
                 writes=[BKA])
        for t in range(4):
            tt = b * 4 + t
            ps, Bps = pool7.next()
            mm_group(ps, Bps, [(hT[:, kc, t * 128:(t + 1) * 128], win_a[:, kc, 1024:1536]) for kc in range(8)],
                     [Bwa, BhT])
            vdst = Vt[:, tt, :].rearrange("p (c x) -> p c x", x=192)
            vsrc = ps.rearrange("p (c e d) -> p c e d", e=2, d=64)
            S.op("dve", "tensor_copy", dict(out=vdst[:, :, 0:64], in_=vsrc[:, :, 0, :]), reads=[Bps], writes=[BVt])
            S.op("act", "copy", dict(out=vdst[:, :, 128:192], in_=vsrc[:, :, 1, :]), reads=[Bps], writes=[BVt])
        ps, Bps = pool7.next()
        mm_group(ps[0:8, :], Bps, [(win_a[:, kc, 1536:1544], hT[:, kc, :]) for kc in range(8)], [Bwa, BhT])
        S.op("act", "activation", dict(out=ef[0:8, :], in_=ps[0:8, :], func=AF.Exp, scale=-1.0, bias=negb[0:8, :]),
             reads=[Bps], writes=[Bef])
        S.op("act", "activation", dict(out=spf[0:8, :], in_=ef[0:8, :], func=AF.Ln, bias=1.0), reads=[Bef],
             writes=[Bspf])
        cur, prv = b % 2, (b + 1) % 2
        init = 0.0 if b == 0 else cn[prv][0:8, 511:512]
        S.op("dve", "tensor_tensor_scan", dict(out=cn[cur][0:8, :], data0=spf[0:8, :], data1=spf[0:8, :],
                                               initial=init, op0=ALU.add, op1=ALU.bypass),
             reads=[Bspf, Bcn[prv]], writes=[Bcn[cur]])
        S.op("dve", "tensor_copy", dict(out=kp[0:8, 0, :], in_=cn[cur][0:8, :]), reads=[Bcn[cur]], writes=[Bkp])
        S.op("dve", "tensor_tensor", dict(out=r1[0:8, :], in0=cn[cur][0:8, :], in1=kp[0:8, 0, :], op=ALU.subtract),
             reads=[Bcn[cur], Bkp], writes=[Br1])
        S.op("dve", "tensor_copy", dict(out=kp[0:8, 1, :], in_=r1[0:8, :]), reads=[Br1], writes=[Bkp])
        S.op("dve", "tensor_tensor", dict(out=r2[0:8, :], in0=r1[0:8, :], in1=kp[0:8, 1, :], op=ALU.subtract),
             reads=[Br1, Bkp], writes=[Br2])
        S.op("dve", "tensor_copy", dict(out=kp[0:8, 2, :], in_=r2[0:8, :]), reads=[Br2], writes=[Bkp])
        S.op("dve", "tensor_scalar", dict(out=qp[0:8], in0=kp[0:8], scalar1=-1.0, scalar2=None, op0=ALU.mult),
             reads=[Bkp], writes=[Bqp])
        for h in range(8):
            par, c = h % 2, h // 2
            a0 = 64 if par == 0 else 0
            S.dma("sp", KA[a0 + 3:a0 + 6, par, c, cols], kp[h:h + 1], reads=[Bkp], writes=[BKaug], chan="cq")
            S.dma("sp", QA[qs][a0:a0 + 3, par, c, :], qp[h:h + 1], reads=[Bqp], writes=[BQaug[qs]], chan="cq")

    def p1_attention(s, b, qs, pairs_):
        cols = slice(b * 512, (b + 1) * 512)
        attnT, BattnT = attnTs[s], BattnTs[s]
        nch = 4 * b + 4
        for c in pairs_:
            (accA, BaccA), (accB, BaccB) = accsets.next()
            units = [(par, j) for j in range(nch) for par in (0, 1)]
            info = {}

            def emit_qk(u):
                par, j = u
                dg = j >= 4 * b
                c0 = (j - 4 * b) * 128 if dg else 0
                ps, Bps = pool3.next()
                S.op("pe", "matmul", dict(out=ps[:, c0:512], lhsT=KA[:, par, c, j * 128:(j + 1) * 128],
                                          rhs=QA[qs][:, par, c, c0:512], start=True, stop=not dg),
                     reads=[BKAs[j // 4], BKaugs[j // 4], BQA[qs], BQaug[qs]], writes=[Bps])
                if dg:
                    S.op("pe", "matmul", dict(out=ps[:, c0:c0 + 128], lhsT=ident, rhs=maskT, start=False,
                                              stop=True), writes=[Bps])
                pt, Bpt = ptrot.next()
                S.op("act", "activation", dict(out=pt[:, c0:512], in_=ps[:, c0:512], func=AF.Exp), reads=[Bps],
                     writes=[Bpt])
                info[u] = (pt, Bpt, c0)

            def emit_pv(u):
                par, j = u
                pt, Bpt, c0 = info[u]
                acc, Bacc = (accA, BaccA) if par == 0 else (accB, BaccB)
                v0 = c * 192 + (0 if par == 0 else 64)
                S.op("pe", "matmul", dict(out=acc[:, c0:512], lhsT=Vt[:, j, v0:v0 + 128], rhs=pt[:, c0:512],
                                          start=(j == 0), stop=(j == nch - 1)), reads=[BVts[j // 4], Bpt], writes=[Bacc])

            D = 2
            for i in range(len(units) + D):
                if i < len(units):
                    emit_qk(units[i])
                if i >= D:
                    emit_pv(units[i - D])
            S.op("act", "activation", dict(out=sqA, in_=accA, func=AF.Square), reads=[BaccA], writes=[BsqA])
            S.op("act", "activation", dict(out=sqB, in_=accB, func=AF.Square), reads=[BaccB], writes=[BsqB])
            ps, Bps = pool3.next()
            S.op("pe", "matmul", dict(out=ps, lhsT=L1, rhs=sqA, start=True, stop=False), reads=[BsqA], writes=[Bps])
            S.op("pe", "matmul", dict(out=ps, lhsT=L2, rhs=sqB, start=False, stop=True), reads=[BsqB], writes=[Bps])
            S.op("act", "activation", dict(out=stsb, in_=ps, func=AF.Ln), reads=[Bps], writes=[Bstsb])
            S.op("act", "activation", dict(out=rsat, in_=stsb, func=AF.Exp, scale=-0.5), reads=[Bstsb],
                 writes=[Brsat])
            S.op("dve", "scalar_tensor_tensor", dict(out=attnT[0:64, c, cols], in0=accA[0:64, :],
                                                     scalar=pv[0:64, 16 + c:17 + c], in1=rsat[0:64, :],
                                                     op0=ALU.mult, op1=ALU.mult),
                 reads=[BaccA, Brsat], writes=[BattnT])
            S.op("dve", "scalar_tensor_tensor", dict(out=attnT[64:128, c, cols], in0=accB[64:128, :],
                                                     scalar=pv[64:128, 16 + c:17 + c], in1=rsat[64:128, :],
                                                     op0=ALU.mult, op1=ALU.mult),
                 reads=[BaccB, Brsat], writes=[BattnT])

    def p1_loads(s, b):
        for t in range(4):
            tt = b * 4 + t
            S.dma("sp", xt[t], x[s, tt * 128:(tt + 1) * 128, :], writes=[Bxt[t]], chan=xtchan[t])

    def p1_front(t):
        norm_front(xt[t], Bxt[t], hbp[t % 2], Bhbp[t % 2], gmix, t, junk)
        emit_precast(2)

    def p1_back(t):
        norm_back(hbp[t % 2], Bhbp[t % 2], hT, BhT, t * 128)

    step = 0
    for s in range(2):
        p1_loads(s, 0)
        for t in range(4):
            p1_front(t)
            p1_back(t)
        p1_inproj(s, 0, step % 2)
        for b in range(4):
            nxt = b < 3
            if nxt:
                p1_loads(s, b + 1)
                p1_front(0)
                p1_front(1)
            p1_attention(s, b, step % 2, [0])
            if nxt:
                p1_back(0)
                p1_back(1)
                p1_front(2)
                p1_front(3)
            p1_attention(s, b, step % 2, [1])
            if nxt:
                p1_back(2)
                p1_back(3)
                p1_inproj(s, b + 1, (step + 1) % 2)
            p1_attention(s, b, step % 2, [2, 3])
            step += 1
    emit_precast(1000)
    S.barrier()

    P = Carver(PBASE)
    win_b = P.take(BF16, 8, 1024)
    wpw = P.take(BF16, 4, 512)
    diag = P.take(BF16, 4, 31, 128)
    wout = P.take(BF16, 8, DM)
    uT = P.take(BF16, 4, SEQ + 32)
    hT2 = [P.take(BF16, 8, 512) for _ in range(2)]
    xt4 = [P.take(F32, DM) for _ in range(4)]
    tg = [P.take(F32, 512) for _ in range(2)]
    ycv = P.take(F32, 4, 512)
    ybf = P.take(BF16, 4, 512)
    ysq = P.take(BF16, 4, 512)
    musb = P.take(F32, 512)
    nm2 = P.take(F32, 512)
    vare = P.take(F32, 512)
    rsln = P.take(F32, 512)
    t2 = [P.take(F32, 512) for _ in range(2)]
    s2 = P.take(BF16, 4, 512)
    vpw = P.take(F32, 512)
    sqc = P.take(BF16, 512)
    stc = P.take(F32, 512)
    rsc = P.take(F32, 512)
    convT = P.take(BF16, 4, 512)
    cw = P.take(F32, 124)
    Bwb, Bwpw, Bdiag, Bwout, BuT = Buf("wb"), Buf("wpw"), Buf("diag"), Buf("wout"), Buf("uT")
    BhT2 = [Buf("hT2_0"), Buf("hT2_1")]
    Bxt4 = [Buf("xt4_%d" % i) for i in range(4)]
    Btg = [Buf("tg0"), Buf("tg1")]
    Bycv = [Buf("ycv%d" % i) for i in range(4)]
    Bybf, Bysq, Bmusb, Bnm2, Bvare, Brsln = Buf("ybf"), Buf("ysq"), Buf("musb"), Buf("nm2"), Buf("vare"), Buf("rsln")
    Bt2 = [Buf("t2_0"), Buf("t2_1")]
    Bs2, Bvpw, Bsqc, Bstc, Brsc, BconvT, Bcw = (Buf("s2"), Buf("vpw"), Buf("sqc"), Buf("stc"), Buf("rsc"),
                                                Buf("convT"), Buf("cw"))

    S.dma("sp", cw, cw_d, writes=[Bcw], chan="cst")
    S.dma("sp", win_b, ws_inb.rearrange("(kc p) n -> p kc n", p=128), reads=[Bpre2], writes=[Bwb], chan="wb")
    S.dma("sp", wpw, ws_pw.rearrange("(kc p) n -> p kc n", p=128), reads=[Bpre2], writes=[Bwpw], chan="wb")
    S.dma("sp", wout, ws_out.rearrange("(kc p) n -> p kc n", p=128), reads=[Bpre2], writes=[Bwout], chan="wb")
    for cc in range(4):
        for k in range(31):
            S.op("dve", "tensor_scalar", dict(out=diag[:, cc, k, :], in0=ident,
                                              scalar1=cw[:, cc * 31 + k:cc * 31 + k + 1], scalar2=None,
                                              op0=ALU.mult), reads=[Bcw], writes=[Bdiag])
    tgrot = Rot(list(zip(tg, Btg)))
    t2rot = Rot(list(zip(t2, Bt2)))
    U0 = 2

    def p2_load_h(i):
        S.dma("sp", hT2[i % 2], hts[i].rearrange("p (k t) -> p k t", k=8), writes=[BhT2[i % 2]], chan="hl%d" % (i % 2))

    def emit_outproj_tile(s, b, t):
        attnT, BattnT = attnTs[s], BattnTs[s]
        tt = b * 4 + t
        for nh in range(2):
            ps, Bps = pool7.next()
            pairs = []
            for kc in range(8):
                l = attnT[:, kc, b * 512 + t * 128:b * 512 + (t + 1) * 128] if kc < 4 else \
                    convT[:, kc - 4, t * 128:(t + 1) * 128]
                pairs.append((l, wout[:, kc, nh * 512:(nh + 1) * 512]))
            mm_group(ps, Bps, pairs, [Bwout, BattnT, BconvT])
            S.op("dve", "tensor_tensor", dict(out=xt4[t][:, nh * 512:(nh + 1) * 512], in0=ps,
                                              in1=xt4[t][:, nh * 512:(nh + 1) * 512], op=ALU.add),
                 reads=[Bps, Bxt4[t]], writes=[Bxt4[t]])
        S.dma("sp", x1s[s * SEQ + tt * 128:s * SEQ + (tt + 1) * 128, :], xt4[t], reads=[Bxt4[t]], chan="xs%d" % t)

    pending_out = None
    p2_load_h(0)
    for s in range(2):
        attnT, BattnT = attnTs[s], BattnTs[s]
        for cc in range(4):
            S.op("dve", "memset", dict(ap=uT[:, cc, 0:32], constant=0.0), writes=[BuT])
        for b in range(4):
            i_ = s * 4 + b
            hT, BhT = hT2[i_ % 2], BhT2[i_ % 2]
            if i_ + 1 < 8:
                p2_load_h(i_ + 1)
            for cc in range(4):
                psa, Bpsa = pool7.next()
                mm_group(psa, Bpsa, [(win_b[:, kc, cc * 128:(cc + 1) * 128], hT[:, kc, :]) for kc in range(8)],
                         [Bwb, BhT])
                psb, Bpsb = pool7.next()
                mm_group(psb, Bpsb, [(win_b[:, kc, 512 + cc * 128:512 + (cc + 1) * 128], hT[:, kc, :])
                                     for kc in range(8)], [Bwb, BhT])
                tg_, Btg_ = tgrot.next()
                S.op("act", "activation", dict(out=tg_, in_=psb, func=AF.Exp, scale=-1.0), reads=[Bpsb],
                     writes=[Btg_])
                S.op("act", "activation", dict(out=tg_, in_=tg_, func=AF.Ln, bias=1.0), reads=[Btg_], writes=[Btg_])
                S.op("act", "activation", dict(out=tg_, in_=tg_, func=AF.Exp, scale=-1.0), reads=[Btg_], writes=[Btg_])
                S.op("dve", "tensor_tensor", dict(out=uT[:, cc, 32 + b * 512:32 + (b + 1) * 512], in0=psa,
                                                  in1=tg_, op=ALU.mult), reads=[Btg_, Bpsa], writes=[BuT])
            for cc in range(4):
                ps, Bps = pool7.next()
                mm_group(ps, Bps, [(diag[:, cc, k, :], uT[:, cc, b * 512 + k + U0:b * 512 + k + U0 + 512])
                                   for k in range(31)], [Bdiag, BuT])
                S.op("act", "activation", dict(out=ycv[:, cc, :], in_=ps, func=AF.Identity, scale=1.0,
                                               bias=pv[:, 0 + cc:1 + cc]), reads=[Bps], writes=[Bycv[cc]])
                S.op("dve", "tensor_copy", dict(out=ybf[:, cc, :], in_=ycv[:, cc, :]), reads=[Bycv[cc]], writes=[Bybf])
                S.op("act", "activation", dict(out=ysq[:, cc, :], in_=ycv[:, cc, :], func=AF.Square),
                     reads=[Bycv[cc]], writes=[Bysq])
            psm, Bpsm = pool7.next()
            mm_group(psm, Bpsm, [(o512, ybf[:, cc, :]) for cc in range(4)], [Bybf])
            pse, Bpse = pool7.next()
            mm_group(pse, Bpse, [(o512, ysq[:, cc, :]) for cc in range(4)], [Bysq])
            S.op("dve", "tensor_copy", dict(out=musb, in_=psm), reads=[Bpsm], writes=[Bmusb])
            S.op("dve", "scalar_tensor_tensor", dict(out=nm2, in0=musb, scalar=-1.0, in1=musb, op0=ALU.mult,
                                                     op1=ALU.mult), reads=[Bmusb], writes=[Bnm2])
            S.op("dve", "scalar_tensor_tensor", dict(out=vare, in0=pse, scalar=EPS, in1=nm2, op0=ALU.add,
                                                     op1=ALU.add), reads=[Bpse, Bnm2], writes=[Bvare])
            S.op("act", "activation", dict(out=vare, in_=vare, func=AF.Ln), reads=[Bvare], writes=[Bvare])
            S.op("act", "activation", dict(out=rsln, in_=vare, func=AF.Exp, scale=-0.5), reads=[Bvare], writes=[Brsln])
            def ln_apply(cc):
                yc = ycv[:, cc, :]
                S.op("dve", "tensor_tensor", dict(out=yc, in0=yc, in1=musb, op=ALU.subtract),
                     reads=[Bycv[cc], Bmusb], writes=[Bycv[cc]])
                S.op("dve", "tensor_tensor", dict(out=yc, in0=yc, in1=rsln, op=ALU.mult),
                     reads=[Bycv[cc], Brsln], writes=[Bycv[cc]])
                t2_, Bt2_ = t2rot.next()
                S.op("act", "activation", dict(out=t2_, in_=yc, func=AF.Exp, scale=hgb[:, cc:cc + 1],
                                               bias=hgb[:, 4 + cc:5 + cc]), reads=[Bycv[cc]], writes=[Bt2_])
                S.op("act", "activation", dict(out=t2_, in_=t2_, func=AF.Ln, bias=1.0), reads=[Bt2_], writes=[Bt2_])
                S.op("act", "activation", dict(out=t2_, in_=t2_, func=AF.Exp, scale=-1.0), reads=[Bt2_], writes=[Bt2_])
                S.op("dve", "tensor_scalar", dict(out=yc, in0=yc, scalar1=pv[:, 4 + cc:5 + cc],
                                                  scalar2=pv[:, 8 + cc:9 + cc], op0=ALU.mult, op1=ALU.add),
                     reads=[Bycv[cc]], writes=[Bycv[cc]])
                S.op("dve", "tensor_tensor", dict(out=s2[:, cc, :], in0=t2_, in1=yc, op=ALU.mult),
                     reads=[Bt2_, Bycv[cc]], writes=[Bs2])
            for t in range(4):
                if pending_out is not None:
                    emit_outproj_tile(pending_out[0], pending_out[1], t)
                ln_apply(t)
            for t in range(4):
                tt = b * 4 + t
                S.dma("sp", xt4[t], x[s, tt * 128:(tt + 1) * 128, :], writes=[Bxt4[t]], chan="xq%d" % t)
            for mo in range(4):
                ps, Bps = pool7.next()
                mm_group(ps, Bps, [(wpw[:, kc, mo * 128:(mo + 1) * 128], s2[:, kc, :]) for kc in range(4)],
                         [Bwpw, Bs2])
                S.op("act", "activation", dict(out=vpw, in_=ps, func=AF.Identity, scale=1.0,
                                               bias=pv[:, 12 + mo:13 + mo]), reads=[Bps], writes=[Bvpw])
                S.op("act", "activation", dict(out=sqc, in_=ps, func=AF.Square, scale=1.0,
                                               bias=pv[:, 12 + mo:13 + mo]), reads=[Bps], writes=[Bsqc])
                pss, Bpss = pool7.next()
                S.op("pe", "matmul", dict(out=pss, lhsT=Lg, rhs=sqc, start=True, stop=True), reads=[Bsqc],
                     writes=[Bpss])
                S.op("act", "activation", dict(out=stc, in_=pss, func=AF.Ln, bias=epsb[:, 0:1]), reads=[Bpss],
                     writes=[Bstc])
                S.op("act", "activation", dict(out=rsc, in_=stc, func=AF.Exp, scale=-0.5), reads=[Bstc], writes=[Brsc])
                S.op("dve", "scalar_tensor_tensor", dict(out=convT[:, mo, :], in0=vpw, scalar=pv[:, 20 + mo:21 + mo],
                                                         in1=rsc, op0=ALU.mult, op1=ALU.mult),
                     reads=[Bvpw, Brsc], writes=[BconvT])
            pending_out = (s, b)
    for t in range(4):
        emit_outproj_tile(pending_out[0], pending_out[1], t)
    S.barrier()

    P = Carver(PBASE - 2 * SEQ * 8)
    wup = P.take(BF16, 8, 4096)
    wdn = P.take(BF16, 32, DM)
    gffn = P.take(F32, DM)
    gfin = P.take(F32, DM)
    xb = [P.take(F32, DM) for _ in range(4)]
    hbs = [P.take(BF16, DM) for _ in range(2)]
    junk = P.take(BF16, DM)
    h2Ts = [P.take(BF16, 8, 256) for _ in range(2)]
    rl = [P.take(F32, 256) for _ in range(2)]
    hid = P.take(BF16, 32, 256)
    Bwup = [Buf("wup%d" % i) for i in range(4)]
    Bwdn, Bg = Buf("wdn"), Buf("g")
    Bxb = [Buf("xb%d" % i) for i in range(4)]
    Bhbs = [Buf("hbB0"), Buf("hbB1")]
    Bh2Ts = [Buf("h2T0"), Buf("h2T1")]
    Bhid = Buf("hid")
    Brl = [Buf("rl0"), Buf("rl1")]
    S.dma("sp", gffn, gffn_d, writes=[Bg], chan="cst")
    S.dma("sp", gfin, gfin_d, writes=[Bg], chan="cst")
    ws_up_k = ws_up.rearrange("(kc p) n -> p kc n", p=128)
    ws_dn_k = ws_dn.rearrange("(kc p) n -> p kc n", p=128)
    for blk in range(4):
        S.dma("sp", wup[:, :, blk * 1024:(blk + 1) * 1024], ws_up_k[:, :, blk * 1024:(blk + 1) * 1024],
              reads=[Bpre3], writes=[Bwup[blk]], chan="wu%d" % blk)
    for q4 in range(4):
        S.dma("sp", wdn[:, q4 * 8:(q4 + 1) * 8, :], ws_dn_k[:, q4 * 8:(q4 + 1) * 8, :], reads=[Bpre3], writes=[Bwdn],
              chan="wd")
    rlrot = Rot(list(zip(rl, Brl)))
    outf = out.rearrange("s t d -> (s t) d")

    def b_norm(j):
        for t in range(2):
            tt = j * 2 + t
            k4 = tt % 4
            norm_transpose(x1s[tt * 128:(tt + 1) * 128, :], xb[k4], Bxb[k4], "xb%d" % k4, hbs[t], Bhbs[t],
                           h2Ts[j % 2], Bh2Ts[j % 2], t * 128, gffn, k4, junk)

    b_norm(0)
    for j in range(16):
        h2T, Bh2T = h2Ts[j % 2], Bh2Ts[j % 2]
        for m in range(32):
            ps, Bps = pool7.next()
            mm_group(ps[:, 0:256], Bps, [(wup[:, kc, m * 128:(m + 1) * 128], h2T[:, kc, :]) for kc in range(8)],
                     [Bwup[m // 8], Bh2T, Bg])
            rl_, Brl_ = rlrot.next()
            S.op("act", "activation", dict(out=rl_, in_=ps[:, 0:256], func=AF.Relu), reads=[Bps], writes=[Brl_])
            S.op("dve", "tensor_tensor", dict(out=hid[:, m, :], in0=rl_, in1=rl_, op=ALU.mult), reads=[Brl_],
                 writes=[Bhid])
        if j + 1 < 16:
            b_norm(j + 1)
        for t in range(2):
            tt = j * 2 + t
            k4 = tt % 4
            for nh in range(2):
                ps, Bps = pool7.next()
                mm_group(ps, Bps, [(hid[:, kc, t * 128:(t + 1) * 128], wdn[:, kc, nh * 512:(nh + 1) * 512])
                                   for kc in range(32)], [Bwdn, Bhid])
                S.op("dve", "tensor_tensor", dict(out=xb[k4][:, nh * 512:(nh + 1) * 512], in0=ps,
                                                  in1=xb[k4][:, nh * 512:(nh + 1) * 512], op=ALU.add),
                     reads=[Bps, Bxb[k4]], writes=[Bxb[k4]])
            si = 4 + k4
            S.op("act", "activation", dict(out=junk, in_=xb[k4], func=AF.Square, scale=1.0 / 32.0,
                                           accum_out=ssq[:, si:si + 1]), reads=[Bxb[k4]], writes=[Bssq[si]])
            S.op("pool", "tensor_scalar", dict(out=rs[:, si:si + 1], in0=ssq[:, si:si + 1], scalar1=EPS, scalar2=None,
                                               op0=ALU.add), reads=[Bssq[si]], writes=[Brs[si]])
            S.op("pool", "tensor_tensor", dict(out=rs[:, si:si + 1], in0=rs[:, si:si + 1], in1=mh[:, 0:1], op=ALU.pow),
                 reads=[Brs[si]], writes=[Brs[si]])
            S.op("dve", "scalar_tensor_tensor", dict(out=xb[k4], in0=xb[k4], scalar=rs[:, si:si + 1], in1=gfin,
                                                     op0=ALU.mult, op1=ALU.mult),
                 reads=[Bxb[k4], Brs[si], Bg], writes=[Bxb[k4]])
            S.dma("sp", outf[tt * 128:(tt + 1) * 128, :], xb[k4], reads=[Bxb[k4]], chan="so%d" % k4)
    S.finish()
    counts = S.emit(nc)
    return nc, counts


_CACHE = {}


def kernel(x, norm_mix_g, w_in, b_forget, conv_dw_w, conv_dw_b, conv_ln_g, conv_ln_b, w_conv_pw, b_conv_pw,
           attn_out_g, conv_out_g, w_out, norm_ffn_g, w_ffn_up, w_ffn_down, norm_final_g):
    f32 = np.float32
    x = np.asarray(x, f32)

    def bc(v):
        return np.ascontiguousarray(np.broadcast_to(np.asarray(v, f32).reshape(1, DM), (128, DM)))

    def pvec(v):
        return np.asarray(v, f32).reshape(4, 128).T

    pv = np.ascontiguousarray(np.concatenate(
        [pvec(conv_dw_b), pvec(conv_ln_g), pvec(conv_ln_b), pvec(b_conv_pw), pvec(attn_out_g), pvec(conv_out_g)],
        axis=1))
    cw = np.ascontiguousarray(np.asarray(conv_dw_w, f32).reshape(31, 4, 128).transpose(2, 1, 0).reshape(128, 124))
    shared = {
        "w_in": np.ascontiguousarray(np.asarray(w_in, f32).reshape(DM, 2568)),
        "w_pw": np.ascontiguousarray(np.asarray(w_conv_pw, f32).reshape(512, 512)),
        "w_out": np.ascontiguousarray(np.asarray(w_out, f32).reshape(DM, DM)),
        "w_up": np.ascontiguousarray(np.asarray(w_ffn_up, f32).reshape(DM, 4096)),
        "w_dn": np.ascontiguousarray(np.asarray(w_ffn_down, f32).reshape(4096, DM)),
        "gmix": bc(norm_mix_g), "gffn": bc(norm_ffn_g), "gfin": bc(norm_final_g),
        "pv": pv, "cw": cw,
        "bfg": np.ascontiguousarray(np.asarray(b_forget, f32).reshape(8, 1)),
    }
    if "nc" not in _CACHE:
        _CACHE["nc"] = build_program()[0]
    nc = _CACHE["nc"]
    in_maps = []
    for i in range(NCORES):
        m = dict(shared)
        m["x"] = np.ascontiguousarray(x[2 * i:2 * i + 2])
        in_maps.append(m)
    res = run_bass_kernel_spmd(nc, in_maps, core_ids=list(range(NCORES)))
    return np.concatenate([np.asarray(r["out"], f32) for r in res.results], axis=0)
```

```python
from contextlib import ExitStack
import numpy as np
import concourse.bass as bass
import concourse.mybir as mybir
from concourse.bass_utils import run_bass_kernel_spmd

F32 = mybir.dt.float32
BF16 = mybir.dt.bfloat16
U8 = mybir.dt.uint8
AF = mybir.ActivationFunctionType
ALU = mybir.AluOpType

EPS = 1e-6
NEG = -30000.0
NCORES = 8
SEQ = 2048
DM = 1024
ARENA = 211968
ENGS = ("pe", "act", "dve", "pool", "sp")


class Buf:
    __slots__ = ("name", "w", "r")

    def __init__(self, name):
        self.name = name
        self.w = {}
        self.r = {}


class Op:
    __slots__ = ("eng", "meth", "kw", "deps", "sig", "idx", "chan", "src", "order")


class Sched:
    def __init__(self):
        self.ops = []
        self.last = {}
        self.pending = {e: [] for e in ENGS}
        self.chan_count = {}

    def _add(self, eng, meth, kw, reads, writes, chan=None):
        op = Op()
        op.eng, op.meth, op.kw, op.chan = eng, meth, kw, chan
        op.src = ("c:" + chan) if chan else eng
        op.order = len(self.ops)
        op.sig = False
        op.idx = 0
        raw, oth = {}, {}

        def add(d, o):
            k = o.src
            if k not in d or d[k].order < o.order:
                d[k] = o

        for b in reads:
            for o in b.w.values():
                add(raw, o)
        for b in writes:
            for o in b.w.values():
                add(oth, o)
            for o in b.r.values():
                add(oth, o)
        for o in self.pending[eng]:
            add(oth, o)
        self.pending[eng] = []
        deps = {}
        for k, o in raw.items():
            if k == eng and eng == "pe":
                continue
            deps[k] = o
        for k, o in oth.items():
            if k == eng:
                continue
            if k not in deps or deps[k].order < o.order:
                deps[k] = o
        op.deps = list(deps.values())
        for o in op.deps:
            o.sig = True
        for b in reads:
            b.r[op.src] = op
        for b in writes:
            b.w[op.src] = op
        self.last[op.src] = op
        if chan:
            self.chan_count[chan] = self.chan_count.get(chan, 0) + 1
            op.idx = 16 * self.chan_count[chan]
        self.ops.append(op)
        return op

    def op(self, eng, meth, kw, reads=(), writes=()):
        return self._add(eng, meth, kw, reads, writes)

    def dma(self, queue, out, in_, reads=(), writes=(), chan=None):
        return self._add(queue, "dma_start", dict(out=out, in_=in_), reads, writes, chan=chan)

    def barrier(self):
        srcs = list(self.last.values())
        for e in ENGS:
            self.pending[e] = list(srcs)

    def finish(self):
        self.barrier()
        self._add("sp", None, None, (), ())

    def emit(self, nc):
        cnt = {e: 0 for e in ENGS}
        for o in self.ops:
            if o.chan is None and o.sig:
                cnt[o.eng] += 1
                o.idx = cnt[o.eng]
        streams = {e: [o for o in self.ops if o.eng == e] for e in ENGS}
        names = list(ENGS) + ["c:" + c for c in self.chan_count]
        with ExitStack() as st:
            sems = {n: st.enter_context(nc.semaphore("s_" + n.replace(":", "_"))) for n in names}
            block = st.enter_context(nc.Block())

            def run(e, h):
                waited = {}
                for o in streams[e]:
                    for d in o.deps:
                        if waited.get(d.src, 0) < d.idx:
                            h.wait_ge(sems[d.src], d.idx)
                            waited[d.src] = d.idx
                    if o.meth is None:
                        continue
                    ins = getattr(h, o.meth)(**o.kw)
                    if o.chan is not None:
                        ins.then_inc(sems[o.src], 16)
                    elif o.sig:
                        ins.then_inc(sems[o.src], 1)

            @block.sync
            def _(h):
                run("sp", h)

            @block.gpsimd
            def _(h):
                run("pool", h)

            @block.scalar
            def _(h):
                run("act", h)

            @block.vector
            def _(h):
                run("dve", h)

            @block.tensor
            def _(h):
                run("pe", h)
        return {e: len(streams[e]) for e in ENGS}


class Rot:
    def __init__(self, items):
        self.items = list(items)
        self.i = 0

    def next(self):
        it = self.items[self.i % len(self.items)]
        self.i += 1
        return it


def build_program():
    nc = bass.Bass("TRN2", target_bir_lowering=False)
    S = Sched()

    def din(name, shape):
        return nc.dram_tensor(name, list(shape), F32, kind="ExternalInput").ap()

    x = din("x", [2, SEQ, DM])
    w_in = din("w_in", [DM, 2568])
    w_pw = din("w_pw", [512, 512])
    w_out = din("w_out", [DM, DM])
    w_up = din("w_up", [DM, 4096])
    w_dn = din("w_dn", [4096, DM])
    gmix_d = din("gmix", [128, DM])
    gffn_d = din("gffn", [128, DM])
    gfin_d = din("gfin", [128, DM])
    pv_d = din("pv", [128, 24])
    cw_d = din("cw", [128, 124])
    bfg_d = din("bfg", [8, 1])
    out = nc.dram_tensor("out", [2, SEQ, DM], F32, kind="ExternalOutput").ap()
    x1s = nc.dram_tensor("x1s", [2 * SEQ, DM], F32).ap()

    arena = nc.alloc_sbuf_tensor("arena", [128, ARENA], U8)
    pb = [nc.alloc_psum_tensor("pb%d" % i, [128, 512], F32)[:] for i in range(8)]
    Bpb = [Buf("pb%d" % i) for i in range(8)]
    pbT = pb[7].bitcast(BF16)

    class Carver:
        def __init__(self, base):
            self.off = base

        def take(self, dtype, *shape):
            n = 1
            for s_ in shape:
                n *= s_
            nb = n * (4 if dtype == F32 else 2)
            ap = arena[:, self.off:self.off + nb].bitcast(dtype)
            self.off += (nb + 31) // 32 * 32
            assert self.off <= ARENA, self.off
            if len(shape) == 2:
                ap = ap.rearrange("p (a b) -> p a b", a=shape[0])
            elif len(shape) == 3:
                ap = ap.rearrange("p (a b c) -> p a b c", a=shape[0], b=shape[1])
            return ap

    C = Carver(0)
    ident = C.take(BF16, 128)
    maskT = C.take(BF16, 128)
    zt = C.take(BF16, 128)
    L1 = C.take(BF16, 128)
    L2 = C.take(BF16, 128)
    Lg = C.take(BF16, 128)
    o512 = C.take(BF16, 128)
    ones3 = C.take(BF16, 512)
    mh = C.take(F32, 512)
    gmix = C.take(F32, DM)
    pv = C.take(F32, 24)
    hgb = C.take(F32, 8)
    bfg = C.take(F32, 1)
    negb = C.take(F32, 1)
    ssq = C.take(F32, 8)
    epsb = C.take(F32, 1)
    rs = C.take(F32, 8)
    attnTs = [C.take(BF16, 4, SEQ) for _ in range(2)]
    PBASE = C.off
    BattnTs = [Buf("attnT0"), Buf("attnT1")]

    ws_inb = nc.dram_tensor("ws_inb", [DM, 1024], BF16).ap()
    ws_pw = nc.dram_tensor("ws_pw", [512, 512], BF16).ap()
    ws_out = nc.dram_tensor("ws_out", [DM, DM], BF16).ap()
    ws_up = nc.dram_tensor("ws_up", [DM, 4096], BF16).ap()
    ws_dn = nc.dram_tensor("ws_dn", [4096, DM], BF16).ap()
    Bpre2, Bpre3 = Buf("pre2"), Buf("pre3")

    S.dma("sp", gmix, gmix_d, chan="cst")
    S.dma("sp", pv, pv_d, chan="cst")
    S.dma("sp", bfg[0:8, :], bfg_d, chan="cst")
    S.op("pool", "memset", dict(ap=ones3, constant=1.0))
    S.op("pool", "memset", dict(ap=zt, constant=0.0))
    S.op("pool", "memset", dict(ap=mh, constant=-0.5))
    S.op("pool", "memset", dict(ap=epsb, constant=EPS))
    S.op("pool", "affine_select", dict(out=ident, in_=ones3[:, 0:128], pattern=[[-1, 128]],
                                       compare_op=ALU.is_equal, fill=0.0, base=0, channel_multiplier=1))
    S.op("pool", "affine_select", dict(out=maskT, in_=zt, pattern=[[1, 128]],
                                       compare_op=ALU.is_ge, fill=NEG, base=0, channel_multiplier=-1))
    for t_, blocks in ((L1, ((0, 64, 0, 64, 1.0 / 64), (64, 128, 0, 64, EPS / 64), (0, 128, 64, 128, 0.0))),
                       (L2, ((0, 64, 64, 128, EPS / 64), (64, 128, 64, 128, 1.0 / 64), (0, 128, 0, 64, 0.0))),
                       (Lg, ((0, 64, 0, 64, 1.0 / 64), (64, 128, 64, 128, 1.0 / 64), (0, 64, 64, 128, 0.0),
                             (64, 128, 0, 64, 0.0))),
                       (o512, ((0, 128, 0, 128, 1.0 / 512),))):
        for (p0, p1, f0, f1, v) in blocks:
            S.op("dve", "memset", dict(ap=t_[p0:p1, f0:f1], constant=v))
    S.barrier()
    S.op("dve", "tensor_scalar", dict(out=hgb, in0=pv[:, 4:12], scalar1=-1.0, scalar2=None, op0=ALU.mult))
    S.op("dve", "tensor_scalar", dict(out=negb[0:8, :], in0=bfg[0:8, :], scalar1=-1.0, scalar2=None, op0=ALU.mult))
    S.barrier()

    pool7 = Rot([(pb[i], Bpb[i]) for i in range(7)])
    Bssq = [Buf("ssq%d" % i) for i in range(8)]
    Brs = [Buf("rs%d" % i) for i in range(8)]

    def norm_transpose(src_rows, xt_ap, Bxt, chan, hb, Bhb, hT, BhT, tcol, gvec, si, junk):
        S.dma("sp", xt_ap, src_rows, writes=[Bxt], chan=chan)
        norm_front(xt_ap, Bxt, hb, Bhb, gvec, si, junk)
        norm_back(hb, Bhb, hT, BhT, tcol)

    def norm_front(xt_ap, Bxt, hb, Bhb, gvec, si, junk):
        Bs = Bssq[si]
        S.op("act", "activation", dict(out=junk, in_=xt_ap, func=AF.Square, scale=1.0 / 32.0,
                                       accum_out=ssq[:, si:si + 1]), reads=[Bxt], writes=[Bs])
        S.op("pool", "tensor_scalar", dict(out=rs[:, si:si + 1], in0=ssq[:, si:si + 1], scalar1=EPS, scalar2=None,
                                           op0=ALU.add), reads=[Bs], writes=[Brs[si]])
        S.op("pool", "tensor_tensor", dict(out=rs[:, si:si + 1], in0=rs[:, si:si + 1], in1=mh[:, 0:1], op=ALU.pow),
             reads=[Brs[si]], writes=[Brs[si]])
        S.op("dve", "scalar_tensor_tensor", dict(out=hb, in0=xt_ap, scalar=rs[:, si:si + 1], in1=gvec,
                                                 op0=ALU.mult, op1=ALU.mult), reads=[Bxt, Brs[si]], writes=[Bhb])

    def norm_back(hb, Bhb, hT, BhT, tcol):
        for kc in range(8):
            S.op("pe", "transpose", dict(out=pbT[:, kc * 128:(kc + 1) * 128], in_=hb[:, kc * 128:(kc + 1) * 128],
                                         identity=ident), reads=[Bhb], writes=[Bpb[7]])
        S.op("act", "copy", dict(out=hT[:, :, tcol:tcol + 128], in_=pbT.rearrange("p (k t) -> p k t", k=8)),
             reads=[Bpb[7]], writes=[BhT])

    def mm_group(ps, Bps, pairs, reads):
        n = len(pairs)
        for i, (l, r) in enumerate(pairs):
            S.op("pe", "matmul", dict(out=ps, lhsT=l, rhs=r, start=(i == 0), stop=(i == n - 1)),
                 reads=reads, writes=[Bps])

    P = Carver(PBASE)
    win_a = P.take(BF16, 8, 1544)
    KA = P.take(BF16, 2, 4, SEQ)
    Vt = P.take(BF16, 16, 768)
    QA = [P.take(BF16, 2, 4, 512) for _ in range(2)]
    hT = P.take(BF16, 8, 512)
    xt = [P.take(F32, DM) for _ in range(2)]
    hbp = [P.take(BF16, DM) for _ in range(2)]
    junk = P.take(BF16, DM)
    ef = P.take(F32, 512)
    spf = P.take(F32, 512)
    cn = [P.take(F32, 512) for _ in range(2)]
    r1 = P.take(F32, 512)
    r2 = P.take(F32, 512)
    kp = P.take(BF16, 3, 512)
    qp = P.take(BF16, 3, 512)
    pts = [P.take(BF16, 512) for _ in range(4)]
    sqA = P.take(BF16, 512)
    sqB = P.take(BF16, 512)
    stsb = P.take(F32, 512)
    rsat = P.take(F32, 512)
    stg = [P.take(F32, 1536) for _ in range(2)]
    stgf = P.take(F32, 8, 8)
    Bwa, BhT = Buf("wa"), Buf("hT")
    Bhbp = [Buf("hbp0"), Buf("hbp1")]
    BKAs = [Buf("KA%d" % i) for i in range(4)]
    BKaugs = [Buf("Kaug%d" % i) for i in range(4)]
    BVts = [Buf("Vt%d" % i) for i in range(4)]
    BQA = [Buf("QA0"), Buf("QA1")]
    BQaug = [Buf("Qaug0"), Buf("Qaug1")]
    Bxt = [Buf("xt0"), Buf("xt1")]
    Bef, Bspf, Br1, Br2, Bkp, Bqp = Buf("ef"), Buf("spf"), Buf("r1"), Buf("r2"), Buf("kp"), Buf("qp")
    Bcn = [Buf("cn0"), Buf("cn1")]
    Bpts = [Buf("pt%d" % i) for i in range(4)]
    BsqA, BsqB, Bstsb, Brsat = Buf("sqA"), Buf("sqB"), Buf("stsb"), Buf("rsat")
    Bstg = [Buf("stg0"), Buf("stg1")]
    Bstgf = Buf("stgf")
    xt = xt + [stg[0][:, 0:DM], stg[1][:, 0:DM]]
    Bxt = Bxt + Bstg
    xtchan = ["xt0", "xt1", "sg0", "sg1"]
    hts = nc.dram_tensor("hts", [8, 128, 8 * 512], BF16).ap()
    Binit = Buf("init")

    w_in_k = w_in.rearrange("(kc p) n -> p kc n", p=128)
    S.dma("act", stgf, w_in_k[:, :, 2560:2568], writes=[Bstgf], chan="sgf")
    for kc in range(8):
        k = kc % 2
        S.dma("act", stg[k], w_in[kc * 128:(kc + 1) * 128, 0:1536], writes=[Bstg[k]], chan="sg%d" % k)
        S.op("dve" if k == 0 else "pool", "tensor_copy", dict(out=win_a[:, kc, 0:1536], in_=stg[k]),
             reads=[Bstg[k]], writes=[Bwa])
    S.op("dve", "tensor_copy", dict(out=win_a[:, :, 1536:1544], in_=stgf), reads=[Bstgf], writes=[Bwa])
    precast = []
    for kc in range(8):
        precast.append((ws_inb[kc * 128:(kc + 1) * 128, :], w_in[kc * 128:(kc + 1) * 128, 1536:2560], Bpre2, "pre2"))
    for kc in range(4):
        precast.append((ws_pw[kc * 128:(kc + 1) * 128, :], w_pw[kc * 128:(kc + 1) * 128, :], Bpre2, "pre2"))
    for kc in range(8):
        precast.append((ws_out[kc * 128:(kc + 1) * 128, :], w_out[kc * 128:(kc + 1) * 128, :], Bpre2, "pre2"))
    for kc in range(8):
        for hf in range(2):
            precast.append((ws_up[kc * 128:(kc + 1) * 128, hf * 2048:(hf + 1) * 2048],
                            w_up[kc * 128:(kc + 1) * 128, hf * 2048:(hf + 1) * 2048], Bpre3, "pre3"))
    for kc in range(32):
        precast.append((ws_dn[kc * 128:(kc + 1) * 128, :], w_dn[kc * 128:(kc + 1) * 128, :], Bpre3, "pre3"))

    def emit_precast(n):
        for _ in range(n):
            if precast:
                o_, i_, B_, ch_ = precast.pop(0)
                S.dma("pool", o_, i_, writes=[B_], chan=ch_)

    S.op("pool", "memset", dict(ap=KA[64:128, 0], constant=0.0), writes=[Binit])
    S.op("pool", "memset", dict(ap=KA[0:64, 1], constant=0.0), writes=[Binit])
    for q_ in QA:
        S.op("dve", "memset", dict(ap=q_[64:128, 0], constant=0.0), writes=[Binit])
        S.op("dve", "memset", dict(ap=q_[0:64, 1], constant=0.0), writes=[Binit])
    for c in range(4):
        S.op("dve", "memset", dict(ap=Vt[:, :, c * 192 + 64:c * 192 + 128], constant=1.0), writes=[Binit])
    for par, a0 in ((0, 64), (1, 0)):
        S.op("pool", "memset", dict(ap=KA[a0:a0 + 6, par], constant=1.0), reads=[Binit], writes=[Binit])
        for q_ in QA:
            S.op("dve", "memset", dict(ap=q_[a0:a0 + 6, par], constant=1.0), reads=[Binit], writes=[Binit])
    for b_ in BKAs + BKaugs + BVts + BQA + BQaug:
        b_.w.update(Binit.w)

    pool3 = Rot([(pb[i], Bpb[i]) for i in range(3)])
    accsets = Rot([((pb[3], Bpb[3]), (pb[4], Bpb[4])), ((pb[5], Bpb[5]), (pb[6], Bpb[6]))])
    ptrot = Rot(list(zip(pts, Bpts)))

    def p1_inproj(s, b, qs):
        cols = slice(b * 512, (b + 1) * 512)
        BKA, BKaug, BVt = BKAs[b], BKaugs[b], BVts[b]
        S.dma("sp", hts[s * 4 + b].rearrange("p (k t) -> p k t", k=8), hT, reads=[BhT], chan="hts")
        for c in range(4):
            ps, Bps = pool7.next()
            mm_group(ps, Bps, [(win_a[:, kc, c * 128:(c + 1) * 128], hT[:, kc, :]) for kc in range(8)], [Bwa, BhT])
            S.op("act", "activation", dict(out=QA[qs][0:64, 0, c, :], in_=ps[0:64, :], func=AF.Identity,
                                           scale=0.125), reads=[Bps], writes=[BQA[qs]])
            S.op("act", "activation", dict(out=QA[qs][64:128, 1, c, :], in_=ps[64:128, :], func=AF.Identity,
                                           scale=0.125), reads=[Bps], writes=[BQA[qs]])
        for c in range(4):
            ps, Bps = pool7.next()
            mm_group(ps, Bps, [(win_a[:, kc, 512 + c * 128:512 + (c + 1) * 128], hT[:, kc, :]) for kc in range(8)],
                     [Bwa, BhT])
            S.op("dve", "tensor_copy", dict(out=KA[0:64, 0, c, cols], in_=ps[0:64, :]), reads=[Bps], writes=[BKA])
            S.op("dve", "tensor_copy", dict(out=KA[64:128, 1, c, cols], in_=ps[64:128, :]), reads=[Bps],
                 writes=[BKA])
        for t in range(4):
            tt = b * 4 + t
            ps, Bps = pool7.next()
            mm_group(ps, Bps, [(hT[:, kc, t * 128:(t + 1) * 128], win_a[:, kc, 1024:1536]) for kc in range(8)],
                     [Bwa, BhT])
            vdst = Vt[:, tt, :].rearrange("p (c x) -> p c x", x=192)
            vsrc = ps.rearrange("p (c e d) -> p c e d", e=2, d=64)
            S.op("dve", "tensor_copy", dict(out=vdst[:, :, 0:64], in_=vsrc[:, :, 0, :]), reads=[Bps], writes=[BVt])
            S.op("act", "copy", dict(out=vdst[:, :, 128:192], in_=vsrc[:, :, 1, :]), reads=[Bps], writes=[BVt])
        ps, Bps = pool7.next()
        mm_group(ps[0:8, :], Bps, [(win_a[:, kc, 1536:1544], hT[:, kc, :]) for kc in range(8)], [Bwa, BhT])
        S.op("act", "activation", dict(out=ef[0:8, :], in_=ps[0:8, :], func=AF.Exp, scale=-1.0, bias=negb[0:8, :]),
             reads=[Bps], writes=[Bef])
        S.op("act", "activation", dict(out=spf[0:8, :], in_=ef[0:8, :], func=AF.Ln, bias=1.0), reads=[Bef],
             writes=[Bspf])
        cur, prv = b % 2, (b + 1) % 2
        init = 0.0 if b == 0 else cn[prv][0:8, 511:512]
        S.op("dve", "tensor_tensor_scan", dict(out=cn[cur][0:8, :], data0=spf[0:8, :], data1=spf[0:8, :],
                                               initial=init, op0=ALU.add, op1=ALU.bypass),
             reads=[Bspf, Bcn[prv]], writes=[Bcn[cur]])
        S.op("dve", "tensor_copy", dict(out=kp[0:8, 0, :], in_=cn[cur][0:8, :]), reads=[Bcn[cur]], writes=[Bkp])
        S.op("dve", "tensor_tensor", dict(out=r1[0:8, :], in0=cn[cur][0:8, :], in1=kp[0:8, 0, :], op=ALU.subtract),
             reads=[Bcn[cur], Bkp], writes=[Br1])
        S.op("dve", "tensor_copy", dict(out=kp[0:8, 1, :], in_=r1[0:8, :]), reads=[Br1], writes=[Bkp])
        S.op("dve", "tensor_tensor", dict(out=r2[0:8, :], in0=r1[0:8, :], in1=kp[0:8, 1, :], op=ALU.subtract),
             reads=[Br1, Bkp], writes=[Br2])
        S.op("dve", "tensor_copy", dict(out=kp[0:8, 2, :], in_=r2[0:8, :]), reads=[Br2], writes=[Bkp])
        S.op("dve", "tensor_scalar", dict(out=qp[0:8], in0=kp[0:8], scalar1=-1.0, scalar2=None, op0=ALU.mult),
             reads=[Bkp], writes=[Bqp])
        for h in range(8):
            par, c = h % 2, h // 2
            a0 = 64 if par == 0 else 0
            S.dma("sp", KA[a0 + 3:a0 + 6, par, c, cols], kp[h:h + 1], reads=[Bkp], writes=[BKaug], chan="cq")
            S.dma("sp", QA[qs][a0:a0 + 3, par, c, :], qp[h:h + 1], reads=[Bqp], writes=[BQaug[qs]], chan="cq")

    def p1_attention(s, b, qs, pairs_):
        cols = slice(b * 512, (b + 1) * 512)
        attnT, BattnT = attnTs[s], BattnTs[s]
        nch = 4 * b + 4
        for c in pairs_:
            (accA, BaccA), (accB, BaccB) = accsets.next()
            units = [(par, j) for j in range(nch) for par in (0, 1)]
            info = {}

            def emit_qk(u):
                par, j = u
                dg = j >= 4 * b
                c0 = (j - 4 * b) * 128 if dg else 0
                ps, Bps = pool3.next()
                S.op("pe", "matmul", dict(out=ps[:, c0:512], lhsT=KA[:, par, c, j * 128:(j + 1) * 128],
                                          rhs=QA[qs][:, par, c, c0:512], start=True, stop=not dg),
                     reads=[BKAs[j // 4], BKaugs[j // 4], BQA[qs], BQaug[qs]], writes=[Bps])
                if dg:
                    S.op("pe", "matmul", dict(out=ps[:, c0:c0 + 128], lhsT=ident, rhs=maskT, start=False,
                                              stop=True), writes=[Bps])
                pt, Bpt = ptrot.next()
                S.op("act", "activation", dict(out=pt[:, c0:512], in_=ps[:, c0:512], func=AF.Exp), reads=[Bps],
                     writes=[Bpt])
                info[u] = (pt, Bpt, c0)

            def emit_pv(u):
                par, j = u
                pt, Bpt, c0 = info[u]
                acc, Bacc = (accA, BaccA) if par == 0 else (accB, BaccB)
                v0 = c * 192 + (0 if par == 0 else 64)
                S.op("pe", "matmul", dict(out=acc[:, c0:512], lhsT=Vt[:, j, v0:v0 + 128], rhs=pt[:, c0:512],
                                          start=(j == 0), stop=(j == nch - 1)), reads=[BVts[j // 4], Bpt], writes=[Bacc])

            D = 2
            for i in range(len(units) + D):
                if i < len(units):
                    emit_qk(units[i])
                if i >= D:
                    emit_pv(units[i - D])
            S.op("act", "activation", dict(out=sqA, in_=accA, func=AF.Square), reads=[BaccA], writes=[BsqA])
            S.op("act", "activation", dict(out=sqB, in_=accB, func=AF.Square), reads=[BaccB], writes=[BsqB])
            ps, Bps = pool3.next()
            S.op("pe", "matmul", dict(out=ps, lhsT=L1, rhs=sqA, start=True, stop=False), reads=[BsqA], writes=[Bps])
            S.op("pe", "matmul", dict(out=ps, lhsT=L2, rhs=sqB, start=False, stop=True), reads=[BsqB], writes=[Bps])
            S.op("act", "activation", dict(out=stsb, in_=ps, func=AF.Ln), reads=[Bps], writes=[Bstsb])
            S.op("act", "activation", dict(out=rsat, in_=stsb, func=AF.Exp, scale=-0.5), reads=[Bstsb],
                 writes=[Brsat])
            S.op("dve", "scalar_tensor_tensor", dict(out=attnT[0:64, c, cols], in0=accA[0:64, :],
                                                     scalar=pv[0:64, 16 + c:17 + c], in1=rsat[0:64, :],
                                                     op0=ALU.mult, op1=ALU.mult),
                 reads=[BaccA, Brsat], writes=[BattnT])
            S.op("dve", "scalar_tensor_tensor", dict(out=attnT[64:128, c, cols], in0=accB[64:128, :],
                                                     scalar=pv[64:128, 16 + c:17 + c], in1=rsat[64:128, :],
                                                     op0=ALU.mult, op1=ALU.mult),
                 reads=[BaccB, Brsat], writes=[BattnT])

    def p1_loads(s, b):
        for t in range(4):
            tt = b * 4 + t
            S.dma("sp", xt[t], x[s, tt * 128:(tt + 1) * 128, :], writes=[Bxt[t]], chan=xtchan[t])

    def p1_front(t):
        norm_front(xt[t], Bxt[t], hbp[t % 2], Bhbp[t % 2], gmix, t, junk)
        emit_precast(2)

    def p1_back(t):
        norm_back(hbp[t % 2], Bhbp[t % 2], hT, BhT, t * 128)

    step = 0
    for s in range(2):
        p1_loads(s, 0)
        for t in range(4):
            p1_front(t)
            p1_back(t)
        p1_inproj(s, 0, step % 2)
        for b in range(4):
            nxt = b < 3
            if nxt:
                p1_loads(s, b + 1)
                p1_front(0)
                p1_front(1)
            p1_attention(s, b, step % 2, [0])
            if nxt:
                p1_back(0)
                p1_back(1)
                p1_front(2)
                p1_front(3)
            p1_attention(s, b, step % 2, [1])
            if nxt:
                p1_back(2)
                p1_back(3)
                p1_inproj(s, b + 1, (step + 1) % 2)
            p1_attention(s, b, step % 2, [2, 3])
            step += 1
    emit_precast(1000)
    S.barrier()

    P = Carver(PBASE)
    win_b = P.take(BF16, 8, 1024)
    wpw = P.take(BF16, 4, 512)
    diag = P.take(BF16, 4, 31, 128)
    wout = P.take(BF16, 8, DM)
    uT = P.take(BF16, 4, SEQ + 32)
    hT2 = [P.take(BF16, 8, 512) for _ in range(2)]
    xt4 = [P.take(F32, DM) for _ in range(4)]
    tg = [P.take(F32, 512) for _ in range(2)]
    ycv = P.take(F32, 4, 512)
    ybf = P.take(BF16, 4, 512)
    ysq = P.take(BF16, 4, 512)
    musb = P.take(F32, 512)
    nm2 = P.take(F32, 512)
    vare = P.take(F32, 512)
    rsln = P.take(F32, 512)
    t2 = [P.take(F32, 512) for _ in range(2)]
    s2 = P.take(BF16, 4, 512)
    vpw = P.take(F32, 512)
    sqc = P.take(BF16, 512)
    stc = P.take(F32, 512)
    rsc = P.take(F32, 512)
    convT = P.take(BF16, 4, 512)
    cw = P.take(F32, 124)
    Bwb, Bwpw, Bdiag, Bwout, BuT = Buf("wb"), Buf("wpw"), Buf("diag"), Buf("wout"), Buf("uT")
    BhT2 = [Buf("hT2_0"), Buf("hT2_1")]
    Bxt4 = [Buf("xt4_%d" % i) for i in range(4)]
    Btg = [Buf("tg0"), Buf("tg1")]
    Bycv = [Buf("ycv%d" % i) for i in range(4)]
    Bybf, Bysq, Bmusb, Bnm2, Bvare, Brsln = Buf("ybf"), Buf("ysq"), Buf("musb"), Buf("nm2"), Buf("vare"), Buf("rsln")
    Bt2 = [Buf("t2_0"), Buf("t2_1")]
    Bs2, Bvpw, Bsqc, Bstc, Brsc, BconvT, Bcw = (Buf("s2"), Buf("vpw"), Buf("sqc"), Buf("stc"), Buf("rsc"),
                                                Buf("convT"), Buf("cw"))

    S.dma("sp", cw, cw_d, writes=[Bcw], chan="cst")
    S.dma("sp", win_b, ws_inb.rearrange("(kc p) n -> p kc n", p=128), reads=[Bpre2], writes=[Bwb], chan="wb")
    S.dma("sp", wpw, ws_pw.rearrange("(kc p) n -> p kc n", p=128), reads=[Bpre2], writes=[Bwpw], chan="wb")
    S.dma("sp", wout, ws_out.rearrange("(kc p) n -> p kc n", p=128), reads=[Bpre2], writes=[Bwout], chan="wb")
    for cc in range(4):
        for k in range(31):
            S.op("dve", "tensor_scalar", dict(out=diag[:, cc, k, :], in0=ident,
                                              scalar1=cw[:, cc * 31 + k:cc * 31 + k + 1], scalar2=None,
                                              op0=ALU.mult), reads=[Bcw], writes=[Bdiag])
    tgrot = Rot(list(zip(tg, Btg)))
    t2rot = Rot(list(zip(t2, Bt2)))
    U0 = 2

    def p2_load_h(i):
        S.dma("sp", hT2[i % 2], hts[i].rearrange("p (k t) -> p k t", k=8), writes=[BhT2[i % 2]], chan="hl%d" % (i % 2))

    def emit_outproj_tile(s, b, t):
        attnT, BattnT = attnTs[s], BattnTs[s]
        tt = b * 4 + t
        for nh in range(2):
            ps, Bps = pool7.next()
            pairs = []
            for kc in range(8):
                l = attnT[:, kc, b * 512 + t * 128:b * 512 + (t + 1) * 128] if kc < 4 else \
                    convT[:, kc - 4, t * 128:(t + 1) * 128]
                pairs.append((l, wout[:, kc, nh * 512:(nh + 1) * 512]))
            mm_group(ps, Bps, pairs, [Bwout, BattnT, BconvT])
            S.op("dve", "tensor_tensor", dict(out=xt4[t][:, nh * 512:(nh + 1) * 512], in0=ps,
                                              in1=xt4[t][:, nh * 512:(nh + 1) * 512], op=ALU.add),
                 reads=[Bps, Bxt4[t]], writes=[Bxt4[t]])
        S.dma("sp", x1s[s * SEQ + tt * 128:s * SEQ + (tt + 1) * 128, :], xt4[t], reads=[Bxt4[t]], chan="xs%d" % t)

    pending_out = None
    p2_load_h(0)
    for s in range(2):
        attnT, BattnT = attnTs[s], BattnTs[s]
        for cc in range(4):
            S.op("dve", "memset", dict(ap=uT[:, cc, 0:32], constant=0.0), writes=[BuT])
        for b in range(4):
            i_ = s * 4 + b
            hT, BhT = hT2[i_ % 2], BhT2[i_ % 2]
            if i_ + 1 < 8:
                p2_load_h(i_ + 1)
            for cc in range(4):
                psa, Bpsa = pool7.next()
                mm_group(psa, Bpsa, [(win_b[:, kc, cc * 128:(cc + 1) * 128], hT[:, kc, :]) for kc in range(8)],
                         [Bwb, BhT])
                psb, Bpsb = pool7.next()
                mm_group(psb, Bpsb, [(win_b[:, kc, 512 + cc * 128:512 + (cc + 1) * 128], hT[:, kc, :])
                                     for kc in range(8)], [Bwb, BhT])
                tg_, Btg_ = tgrot.next()
                S.op("act", "activation", dict(out=tg_, in_=psb, func=AF.Exp, scale=-1.0), reads=[Bpsb],
                     writes=[Btg_])
                S.op("act", "activation", dict(out=tg_, in_=tg_, func=AF.Ln, bias=1.0), reads=[Btg_], writes=[Btg_])
                S.op("act", "activation", dict(out=tg_, in_=tg_, func=AF.Exp, scale=-1.0), reads=[Btg_], writes=[Btg_])
                S.op("dve", "tensor_tensor", dict(out=uT[:, cc, 32 + b * 512:32 + (b + 1) * 512], in0=psa,
                                                  in1=tg_, op=ALU.mult), reads=[Btg_, Bpsa], writes=[BuT])
            for cc in range(4):
                ps, Bps = pool7.next()
                mm_group(ps, Bps, [(diag[:, cc, k, :], uT[:, cc, b * 512 + k + U0:b * 512 + k + U0 + 512])
                                   for k in range(31)], [Bdiag, BuT])
                S.op("act", "activation", dict(out=ycv[:, cc, :], in_=ps, func=AF.Identity, scale=1.0,
                                               bias=pv[:, 0 + cc:1 + cc]), reads=[Bps], writes=[Bycv[cc]])
                S.op("dve", "tensor_copy", dict(out=ybf[:, cc, :], in_=ycv[:, cc, :]), reads=[Bycv[cc]], writes=[Bybf])
                S.op("act", "activation", dict(out=ysq[:, cc, :], in_=ycv[:, cc, :], func=AF.Square),
                     reads=[Bycv[cc]], writes=[Bysq])
            psm, Bpsm = pool7.next()
            mm_group(psm, Bpsm, [(o512, ybf[:, cc, :]) for cc in range(4)], [Bybf])
            pse, Bpse = pool7.next()
            mm_group(pse, Bpse, [(o512, ysq[:, cc, :]) for cc in range(4)], [Bysq])
            S.op("dve", "tensor_copy", dict(out=musb, in_=psm), reads=[Bpsm], writes=[Bmusb])
            S.op("dve", "scalar_tensor_tensor", dict(out=nm2, in0=musb, scalar=-1.0, in1=musb, op0=ALU.mult,
                                                     op1=ALU.mult), reads=[Bmusb], writes=[Bnm2])
            S.op("dve", "scalar_tensor_tensor", dict(out=vare, in0=pse, scalar=EPS, in1=nm2, op0=ALU.add,
                                                     op1=ALU.add), reads=[Bpse, Bnm2], writes=[Bvare])
            S.op("act", "activation", dict(out=vare, in_=vare, func=AF.Ln), reads=[Bvare], writes=[Bvare])
            S.op("act", "activation", dict(out=rsln, in_=vare, func=AF.Exp, scale=-0.5), reads=[Bvare], writes=[Brsln])
            def ln_apply(cc):
                yc = ycv[:, cc, :]
                S.op("dve", "tensor_tensor", dict(out=yc, in0=yc, in1=musb, op=ALU.subtract),
                     reads=[Bycv[cc], Bmusb], writes=[Bycv[cc]])
                S.op("dve", "tensor_tensor", dict(out=yc, in0=yc, in1=rsln, op=ALU.mult),
                     reads=[Bycv[cc], Brsln], writes=[Bycv[cc]])
                t2_, Bt2_ = t2rot.next()
                S.op("act", "activation", dict(out=t2_, in_=yc, func=AF.Exp, scale=hgb[:, cc:cc + 1],
                                               bias=hgb[:, 4 + cc:5 + cc]), reads=[Bycv[cc]], writes=[Bt2_])
                S.op("act", "activation", dict(out=t2_, in_=t2_, func=AF.Ln, bias=1.0), reads=[Bt2_], writes=[Bt2_])
                S.op("act", "activation", dict(out=t2_, in_=t2_, func=AF.Exp, scale=-1.0), reads=[Bt2_], writes=[Bt2_])
                S.op("dve", "tensor_scalar", dict(out=yc, in0=yc, scalar1=pv[:, 4 + cc:5 + cc],
                                                  scalar2=pv[:, 8 + cc:9 + cc], op0=ALU.mult, op1=ALU.add),
                     reads=[Bycv[cc]], writes=[Bycv[cc]])
                S.op("dve", "tensor_tensor", dict(out=s2[:, cc, :], in0=t2_, in1=yc, op=ALU.mult),
                     reads=[Bt2_, Bycv[cc]], writes=[Bs2])
            for t in range(4):
                if pending_out is not None:
                    emit_outproj_tile(pending_out[0], pending_out[1], t)
                ln_apply(t)
            for t in range(4):
                tt = b * 4 + t
                S.dma("sp", xt4[t], x[s, tt * 128:(tt + 1) * 128, :], writes=[Bxt4[t]], chan="xq%d" % t)
            for mo in range(4):
                ps, Bps = pool7.next()
                mm_group(ps, Bps, [(wpw[:, kc, mo * 128:(mo + 1) * 128], s2[:, kc, :]) for kc in range(4)],
                         [Bwpw, Bs2])
                S.op("act", "activation", dict(out=vpw, in_=ps, func=AF.Identity, scale=1.0,
                                               bias=pv[:, 12 + mo:13 + mo]), reads=[Bps], writes=[Bvpw])
                S.op("act", "activation", dict(out=sqc, in_=ps, func=AF.Square, scale=1.0,
                                               bias=pv[:, 12 + mo:13 + mo]), reads=[Bps], writes=[Bsqc])
                pss, Bpss = pool7.next()
                S.op("pe", "matmul", dict(out=pss, lhsT=Lg, rhs=sqc, start=True, stop=True), reads=[Bsqc],
                     writes=[Bpss])
                S.op("act", "activation", dict(out=stc, in_=pss, func=AF.Ln, bias=epsb[:, 0:1]), reads=[Bpss],
                     writes=[Bstc])
                S.op("act", "activation", dict(out=rsc, in_=stc, func=AF.Exp, scale=-0.5), reads=[Bstc], writes=[Brsc])
                S.op("dve", "scalar_tensor_tensor", dict(out=convT[:, mo, :], in0=vpw, scalar=pv[:, 20 + mo:21 + mo],
                                                         in1=rsc, op0=ALU.mult, op1=ALU.mult),
                     reads=[Bvpw, Brsc], writes=[BconvT])
            pending_out = (s, b)
    for t in range(4):
        emit_outproj_tile(pending_out[0], pending_out[1], t)
    S.barrier()

    P = Carver(PBASE - 2 * SEQ * 8)
    wup = P.take(BF16, 8, 4096)
    wdn = P.take(BF16, 32, DM)
    gffn = P.take(F32, DM)
    gfin = P.take(F32, DM)
    xb = [P.take(F32, DM) for _ in range(4)]
    hbs = [P.take(BF16, DM) for _ in range(2)]
    junk = P.take(BF16, DM)
    h2Ts = [P.take(BF16, 8, 256) for _ in range(2)]
    rl = [P.take(F32, 256) for _ in range(2)]
    hid = P.take(BF16, 32, 256)
    Bwup = [Buf("wup%d" % i) for i in range(4)]
    Bwdn, Bg = Buf("wdn"), Buf("g")
    Bxb = [Buf("xb%d" % i) for i in range(4)]
    Bhbs = [Buf("hbB0"), Buf("hbB1")]
    Bh2Ts = [Buf("h2T0"), Buf("h2T1")]
    Bhid = Buf("hid")
    Brl = [Buf("rl0"), Buf("rl1")]
    S.dma("sp", gffn, gffn_d, writes=[Bg], chan="cst")
    S.dma("sp", gfin, gfin_d, writes=[Bg], chan="cst")
    ws_up_k = ws_up.rearrange("(kc p) n -> p kc n", p=128)
    ws_dn_k = ws_dn.rearrange("(kc p) n -> p kc n", p=128)
    for blk in range(4):
        S.dma("sp", wup[:, :, blk * 1024:(blk + 1) * 1024], ws_up_k[:, :, blk * 1024:(blk + 1) * 1024],
              reads=[Bpre3], writes=[Bwup[blk]], chan="wu%d" % blk)
    for q4 in range(4):
        S.dma("sp", wdn[:, q4 * 8:(q4 + 1) * 8, :], ws_dn_k[:, q4 * 8:(q4 + 1) * 8, :], reads=[Bpre3], writes=[Bwdn],
              chan="wd")
    rlrot = Rot(list(zip(rl, Brl)))
    outf = out.rearrange("s t d -> (s t) d")

    def b_front(j):
        for t in range(2):
            tt = j * 2 + t
            k4 = tt % 4
            S.dma("sp", xb[k4], x1s[tt * 128:(tt + 1) * 128, :], writes=[Bxb[k4]], chan="xb%d" % k4)
            norm_front(xb[k4], Bxb[k4], hbs[t], Bhbs[t], gffn, k4, junk)

    def b_back(j):
        for t in range(2):
            norm_back(hbs[t], Bhbs[t], h2Ts[j % 2], Bh2Ts[j % 2], t * 128)

    b_front(0)
    b_back(0)
    for j in range(16):
        h2T, Bh2T = h2Ts[j % 2], Bh2Ts[j % 2]
        if j + 1 < 16:
            b_front(j + 1)
        for m in range(32):
            ps, Bps = pool7.next()
            mm_group(ps[:, 0:256], Bps, [(wup[:, kc, m * 128:(m + 1) * 128], h2T[:, kc, :]) for kc in range(8)],
                     [Bwup[m // 8], Bh2T, Bg])
            rl_, Brl_ = rlrot.next()
            S.op("act", "activation", dict(out=rl_, in_=ps[:, 0:256], func=AF.Relu), reads=[Bps], writes=[Brl_])
            S.op("dve", "tensor_tensor", dict(out=hid[:, m, :], in0=rl_, in1=rl_, op=ALU.mult), reads=[Brl_],
                 writes=[Bhid])
        if j + 1 < 16:
            b_back(j + 1)
        for t in range(2):
            tt = j * 2 + t
            k4 = tt % 4
            for nh in range(2):
                ps, Bps = pool7.next()
                mm_group(ps, Bps, [(hid[:, kc, t * 128:(t + 1) * 128], wdn[:, kc, nh * 512:(nh + 1) * 512])
                                   for kc in range(32)], [Bwdn, Bhid])
                S.op("dve", "tensor_tensor", dict(out=xb[k4][:, nh * 512:(nh + 1) * 512], in0=ps,
                                                  in1=xb[k4][:, nh * 512:(nh + 1) * 512], op=ALU.add),
                     reads=[Bps, Bxb[k4]], writes=[Bxb[k4]])
            si = 4 + k4
            S.op("act", "activation", dict(out=junk, in_=xb[k4], func=AF.Square, scale=1.0 / 32.0,
                                           accum_out=ssq[:, si:si + 1]), reads=[Bxb[k4]], writes=[Bssq[si]])
            S.op("pool", "tensor_scalar", dict(out=rs[:, si:si + 1], in0=ssq[:, si:si + 1], scalar1=EPS, scalar2=None,
                                               op0=ALU.add), reads=[Bssq[si]], writes=[Brs[si]])
            S.op("pool", "tensor_tensor", dict(out=rs[:, si:si + 1], in0=rs[:, si:si + 1], in1=mh[:, 0:1], op=ALU.pow),
                 reads=[Brs[si]], writes=[Brs[si]])
            S.op("dve", "scalar_tensor_tensor", dict(out=xb[k4], in0=xb[k4], scalar=rs[:, si:si + 1], in1=gfin,
                                                     op0=ALU.mult, op1=ALU.mult),
                 reads=[Bxb[k4], Brs[si], Bg], writes=[Bxb[k4]])
            S.dma("sp", outf[tt * 128:(tt + 1) * 128, :], xb[k4], reads=[Bxb[k4]], chan="so%d" % k4)
    S.finish()
    counts = S.emit(nc)
    return nc, counts


_CACHE = {}


def kernel(x, norm_mix_g, w_in, b_forget, conv_dw_w, conv_dw_b, conv_ln_g, conv_ln_b, w_conv_pw, b_conv_pw,
           attn_out_g, conv_out_g, w_out, norm_ffn_g, w_ffn_up, w_ffn_down, norm_final_g):
    f32 = np.float32
    x = np.asarray(x, f32)

    def bc(v):
        return np.ascontiguousarray(np.broadcast_to(np.asarray(v, f32).reshape(1, DM), (128, DM)))

    def pvec(v):
        return np.asarray(v, f32).reshape(4, 128).T

    pv = np.ascontiguousarray(np.concatenate(
        [pvec(conv_dw_b), pvec(conv_ln_g), pvec(conv_ln_b), pvec(b_conv_pw), pvec(attn_out_g), pvec(conv_out_g)],
        axis=1))
    cw = np.ascontiguousarray(np.asarray(conv_dw_w, f32).reshape(31, 4, 128).transpose(2, 1, 0).reshape(128, 124))
    shared = {
        "w_in": np.ascontiguousarray(np.asarray(w_in, f32).reshape(DM, 2568)),
        "w_pw": np.ascontiguousarray(np.asarray(w_conv_pw, f32).reshape(512, 512)),
        "w_out": np.ascontiguousarray(np.asarray(w_out, f32).reshape(DM, DM)),
        "w_up": np.ascontiguousarray(np.asarray(w_ffn_up, f32).reshape(DM, 4096)),
        "w_dn": np.ascontiguousarray(np.asarray(w_ffn_down, f32).reshape(4096, DM)),
        "gmix": bc(norm_mix_g), "gffn": bc(norm_ffn_g), "gfin": bc(norm_final_g),
        "pv": pv, "cw": cw,
        "bfg": np.ascontiguousarray(np.asarray(b_forget, f32).reshape(8, 1)),
    }
    if "nc" not in _CACHE:
        _CACHE["nc"] = build_program()[0]
    nc = _CACHE["nc"]
    in_maps = []
    for i in range(NCORES):
        m = dict(shared)
        m["x"] = np.ascontiguousarray(x[2 * i:2 * i + 2])
        in_maps.append(m)
    res = run_bass_kernel_spmd(nc, in_maps, core_ids=list(range(NCORES)))
    return np.concatenate([np.asarray(r["out"], f32) for r in res.results], axis=0)
```
